# Optimizing a Trainium2 kernel written in Bass

```python
import math
import jax
import jax.numpy as jnp
from jax import lax
import numpy as np

D_MODEL = 1024
BATCH = 4
SEQ = 4096
DEPTH = 1

N_META = 16
D_MIX = 2 * D_MODEL
EPS = 1e-6
HG_WIDTH = D_MIX // 2
HG_DK = 128
HG_HEADS = HG_WIDTH // 128
HG_DV = HG_WIDTH // HG_HEADS
HG_KEY = HG_HEADS * HG_DK
HG_CHUNK = 16
M2_WIDTH = D_MIX - HG_WIDTH
M2_HEADDIM = 64
M2_HEADS = M2_WIDTH // M2_HEADDIM
M2_STATE = 128
M2_GROUPS = 2
M2_HPG = M2_HEADS // M2_GROUPS
M2_GN = M2_GROUPS * M2_STATE
M2_XBC = M2_WIDTH + 2 * M2_GN
M2_CONV = 4
M2_CHUNK = 64
DT_MIN = 1e-3
DT_MAX = 1e-1
D_FF = 2816
FFN_CONV = 3
PROJ_SIZES = (HG_KEY, HG_KEY, HG_WIDTH, HG_WIDTH, M2_WIDTH, M2_XBC, M2_HEADS)
D_PROJ = sum(PROJ_SIZES)
PROJ_SPLITS = tuple(int(v) for v in np.cumsum(PROJ_SIZES)[:-1])

kernel_name = 'hybrid_hgrn2_mamba2_convffn'


def rmsnorm(x, w):
    xf = x.astype(jnp.float32)
    y = xf * lax.rsqrt(jnp.mean(xf * xf, axis=-1, keepdims=True) + EPS)
    return (y * w.astype(jnp.float32)).astype(x.dtype)


def causal_dwconv(x, w, b):
    k_taps, length = w.shape[0], x.shape[1]
    xp = jnp.pad(x, ((0, 0), (k_taps - 1, 0), (0, 0)))
    y = b + xp[:, 0:length] * w[0]
    for k in range(1, k_taps):
        y = y + xp[:, k:k + length] * w[k]
    return y


def front_pad(t, n):
    return jnp.pad(t, ((0, 0), (n, 0)) + ((0, 0),) * (t.ndim - 2))


def to_chunks(t, c):
    return t.reshape((t.shape[0], t.shape[1] // c, c) + t.shape[2:])


def hgrn2_mixer(q, f_pre, i_in, g, lb, norm_w):
    bsz, length, _ = q.shape
    f32 = jnp.float32
    f = lb + (1.0 - lb) * jax.nn.sigmoid(f_pre.astype(f32))
    k = 1.0 - f
    log_f = jnp.log(f)
    qa = jax.nn.silu(q.astype(f32))
    pad = (-N_META) % HG_CHUNK
    lp = length + pad

    def prep(t, d):
        return to_chunks(front_pad(t, pad).reshape(bsz, lp, HG_HEADS, d), HG_CHUNK)

    qc, kc, gc = prep(qa, HG_DK), prep(k, HG_DK), prep(log_f, HG_DK)
    vc = prep(i_in.astype(f32), HG_DV)
    bcum = jnp.cumsum(gc, axis=2)
    blast = bcum[:, :, -1]
    q_dec = qc * jnp.exp(bcum)
    k_inv = kc * jnp.exp(-bcum)
    k_end = kc * jnp.exp(blast[:, :, None] - bcum)
    causal = jnp.tril(jnp.ones((HG_CHUNK, HG_CHUNK), dtype=bool))
    scores = jnp.einsum('bnrhk,bnshk->bnhrs', q_dec, k_inv)
    scores = jnp.where(causal, scores, 0.0)
    o_intra = jnp.einsum('bnhrs,bnshv->bnrhv', scores, vc)

    def step(state, inp):
        q_n, k_n, v_n, d_n = inp
        o_n = jnp.einsum('brhk,bhkv->brhv', q_n, state)
        state = d_n[..., None] * state + jnp.einsum('brhk,brhv->bhkv', k_n, v_n)
        return state, o_n

    s0 = jnp.zeros((bsz, HG_HEADS, HG_DK, HG_DV), f32)
    xs = (jnp.moveaxis(q_dec, 1, 0), jnp.moveaxis(k_end, 1, 0),
          jnp.moveaxis(vc, 1, 0), jnp.moveaxis(jnp.exp(blast), 1, 0))
    _, o_inter = lax.scan(step, s0, xs)
    o = o_intra + jnp.moveaxis(o_inter, 0, 1)
    o = o.reshape(bsz, lp, HG_HEADS, HG_DV)[:, pad:]
    o = o * lax.rsqrt(jnp.mean(o * o, axis=-1, keepdims=True) + EPS)
    o = o.reshape(bsz, length, HG_WIDTH) * norm_w.astype(f32)
    return (o * jax.nn.silu(g.astype(f32))).astype(q.dtype)


def mamba2_mixer(z, xbc, dt_pre, conv_w, conv_b, dt_bias, a_log, d_skip, norm_w):
    bsz, length, _ = z.shape
    f32 = jnp.float32
    xbc = jax.nn.silu(causal_dwconv(xbc, conv_w, conv_b).astype(f32))
    x_in = xbc[..., :M2_WIDTH].reshape(bsz, length, M2_GROUPS, M2_HPG, M2_HEADDIM)
    b_in = xbc[..., M2_WIDTH:M2_WIDTH + M2_GN].reshape(bsz, length, M2_GROUPS, M2_STATE)
    c_in = xbc[..., M2_WIDTH + M2_GN:].reshape(bsz, length, M2_GROUPS, M2_STATE)
    dt = jax.nn.softplus(dt_pre.astype(f32) + dt_bias.astype(f32))
    dt = dt.reshape(bsz, length, M2_GROUPS, M2_HPG)
    a = -jnp.exp(a_log.astype(f32)).reshape(M2_GROUPS, M2_HPG)
    da = dt * a
    xdt = x_in * dt[..., None]
    pad = (-N_META) % M2_CHUNK
    lp = length + pad
    xc = to_chunks(front_pad(xdt, pad), M2_CHUNK)
    ac = to_chunks(front_pad(da, pad), M2_CHUNK)
    bc = to_chunks(front_pad(b_in, pad), M2_CHUNK)
    cc = to_chunks(front_pad(c_in, pad), M2_CHUNK)
    a_cum = jnp.cumsum(ac, axis=2)
    causal = jnp.tril(jnp.ones((M2_CHUNK, M2_CHUNK), dtype=bool))
    seg = a_cum[:, :, :, None] - a_cum[:, :, None, :]
    l_mat = jnp.exp(jnp.where(causal[:, :, None, None], seg, -jnp.inf))
    cb = jnp.einsum('bnigs,bnjgs->bnijg', cc, bc)
    y_diag = jnp.einsum('bnijgh,bnjghp->bnighp', cb[..., None] * l_mat, xc)
    decay_to_end = jnp.exp(a_cum[:, :, -1:] - a_cum)
    u = jnp.einsum('bnjgs,bnjghp->bnghps', bc, xc * decay_to_end[..., None])
    chunk_decay = jnp.exp(a_cum[:, :, -1])

    def step(state, inp):
        c_n, e_n, u_n, d_n = inp
        y_n = jnp.einsum('bigs,bghps->bighp', c_n, state) * e_n[..., None]
        state = d_n[..., None, None] * state + u_n
        return state, y_n

    s0 = jnp.zeros((bsz, M2_GROUPS, M2_HPG, M2_HEADDIM, M2_STATE), f32)
    xs = (jnp.moveaxis(cc, 1, 0), jnp.moveaxis(jnp.exp(a_cum), 1, 0),
          jnp.moveaxis(u, 1, 0), jnp.moveaxis(chunk_decay, 1, 0))
    _, y_off = lax.scan(step, s0, xs)
    y = y_diag + jnp.moveaxis(y_off, 0, 1)
    y = y.reshape(bsz, lp, M2_GROUPS, M2_HPG, M2_HEADDIM)[:, pad:]
    y = y + d_skip.astype(f32).reshape(M2_GROUPS, M2_HPG, 1) * x_in
    y = y.reshape(bsz, length, M2_WIDTH) * jax.nn.silu(z.astype(f32))
    yg = y.reshape(bsz, length, M2_GROUPS, M2_WIDTH // M2_GROUPS)
    yg = yg * lax.rsqrt(jnp.mean(yg * yg, axis=-1, keepdims=True) + EPS)
    y = yg.reshape(bsz, length, M2_WIDTH) * norm_w.astype(f32)
    return y.astype(z.dtype)


def conv_glu_ffn(h, w_up, conv_w, conv_b, w_down):
    u = causal_dwconv(h @ w_up, conv_w, conv_b)
    gate, val = jnp.split(u, 2, axis=-1)
    return (jax.nn.silu(gate) * val) @ w_down


def setup_inputs(seed: int = 0) -> dict:
    key = jax.random.key(seed)
    ks = jax.random.split(key, 20)
    f32 = jnp.float32

    def nrm(k, shape, scale):
        return scale * jax.random.normal(k, shape, f32)

    def gain(k, shape):
        return 1.0 + 0.05 * jax.random.normal(k, shape, f32)

    dt0 = jnp.exp(jax.random.uniform(ks[8], (DEPTH, M2_HEADS), f32,
                                     minval=math.log(DT_MIN), maxval=math.log(DT_MAX)))
    dt_bias = dt0 + jnp.log(-jnp.expm1(-dt0))
    a_log = jnp.log(jax.random.uniform(ks[9], (DEPTH, M2_HEADS), f32, minval=1.0, maxval=16.0))
    return {
        'x': nrm(ks[0], (BATCH, SEQ, D_MODEL), 1.0),
        'meta_tokens': nrm(ks[1], (N_META, D_MODEL), 1.0),
        'norm1_w': gain(ks[2], (DEPTH, D_MODEL)),
        'w_in': nrm(ks[3], (DEPTH, D_MODEL, D_PROJ), D_MODEL ** -0.5),
        'hg_lb_logits': nrm(ks[4], (DEPTH + 1, HG_KEY), 0.1),
        'hg_norm_w': gain(ks[5], (DEPTH, HG_WIDTH)),
        'm2_conv_w': nrm(ks[6], (DEPTH, M2_CONV, M2_XBC), M2_CONV ** -0.5),
        'm2_conv_b': nrm(ks[7], (DEPTH, M2_XBC), 0.02),
        'm2_dt_bias': dt_bias,
        'm2_a_log': a_log,
        'm2_d': gain(ks[10], (DEPTH, M2_HEADS)),
        'm2_norm_w': gain(ks[11], (DEPTH, M2_WIDTH)),
        'w_out': nrm(ks[12], (DEPTH, D_MIX, D_MODEL), D_MIX ** -0.5),
        'norm2_w': gain(ks[13], (DEPTH, D_MODEL)),
        'ffn_w_up': nrm(ks[14], (DEPTH, D_MODEL, 2 * D_FF), D_MODEL ** -0.5),
        'ffn_conv_w': nrm(ks[15], (DEPTH, FFN_CONV, 2 * D_FF), FFN_CONV ** -0.5),
        'ffn_conv_b': nrm(ks[16], (DEPTH, 2 * D_FF), 0.02),
        'ffn_w_down': nrm(ks[17], (DEPTH, D_FF, D_MODEL), D_FF ** -0.5),
        'final_norm_w': gain(ks[18], (D_MODEL,)),
    }


def reference(x, meta_tokens, norm1_w, w_in, hg_lb_logits, hg_norm_w, m2_conv_w, m2_conv_b,
              m2_dt_bias, m2_a_log, m2_d, m2_norm_w, w_out, norm2_w, ffn_w_up, ffn_conv_w,
              ffn_conv_b, ffn_w_down, final_norm_w):
    bsz = x.shape[0]
    meta = jnp.broadcast_to(meta_tokens.astype(x.dtype)[None], (bsz, N_META, D_MODEL))
    h = jnp.concatenate([meta, x], axis=1)
    lbs = jnp.cumsum(jax.nn.softmax(hg_lb_logits.astype(jnp.float32), axis=0), axis=0)
    for l in range(DEPTH):
        u = rmsnorm(h, norm1_w[l])
        proj = u @ w_in[l]
        hg_q, hg_f, hg_i, hg_g, m2_z, m2_xbc, m2_dt = jnp.split(proj, PROJ_SPLITS, axis=-1)
        out_a = hgrn2_mixer(hg_q, hg_f, hg_i, hg_g, lbs[l], hg_norm_w[l])
        out_b = mamba2_mixer(m2_z, m2_xbc, m2_dt, m2_conv_w[l], m2_conv_b[l], m2_dt_bias[l],
                             m2_a_log[l], m2_d[l], m2_norm_w[l])
        h = h + jnp.concatenate([out_a, out_b], axis=-1) @ w_out[l]
        h = h + conv_glu_ffn(rmsnorm(h, norm2_w[l]), ffn_w_up[l], ffn_conv_w[l],
                             ffn_conv_b[l], ffn_w_down[l])
    y = rmsnorm(h, final_norm_w)
    return y[:, N_META:, :]
```

```python
import numpy as np
import concourse.bass as bass
import concourse.mybir as mybir
from concourse.bass_utils import run_bass_kernel_spmd

F32 = mybir.dt.float32
BF16 = mybir.dt.bfloat16
ALU = mybir.AluOpType
AF = mybir.ActivationFunctionType

ENGS = ['pe', 'dve', 'act', 'pool', 'sp']
SAME_ENG_RAW = ('dve', 'act', 'pool')
SAME_ENG_ALL = True
EPS = 1e-6
NMETA = 16
D = 1024
DPROJ = 6672
DFF = 2816


class Prog:
    def __init__(self, nc):
        self.nc = nc
        self.streams = {e: [] for e in ENGS}
        self.sems = {}
        self.count = {}
        self.seen = {e: {} for e in ENGS}
        self.buf = {}

    def _sem(self, key):
        if key not in self.sems:
            name = "s_" + "_".join(str(k) for k in (key if isinstance(key, tuple) else (key,)))
            name = name.replace("(", "").replace(")", "").replace(",", "_").replace(" ", "").replace("'", "")
            self.sems[key] = self.nc.alloc_semaphore(name)
            self.count[key] = 0
        return self.sems[key]

    def _need(self, eng, reads, writes):
        need = {}

        def add(kv, raw):
            if kv is None:
                return
            k, v = kv
            if k == eng and not ((raw or SAME_ENG_ALL) and eng in SAME_ENG_RAW):
                return
            if v > need.get(k, 0):
                need[k] = v

        for b in reads:
            st = self.buf.get(b)
            if st:
                add(st[0], True)
        for b in writes:
            st = self.buf.get(b)
            if st:
                add(st[0], False)
                for k, v in st[1].items():
                    add((k, v), False)
        seen = self.seen[eng]
        for k, v in need.items():
            if v > seen.get(k, 0):
                self.streams[eng].append(('wait', k, v))
                seen[k] = v

    def _mark(self, key, val, reads, writes):
        for b in reads:
            st = self.buf.setdefault(b, [None, {}])
            st[1][key] = val
        for b in writes:
            self.buf[b] = [(key, val), {}]

    def op(self, eng, fn, reads=(), writes=()):
        self._need(eng, reads, writes)
        self._sem(eng)
        self.count[eng] += 1
        self.streams[eng].append(('op', fn, eng, 1))
        self._mark(eng, self.count[eng], reads, writes)

    def dma(self, q, fn, reads=(), writes=(), chan=None):
        self._need(q, reads, writes)
        key = ('d', chan)
        self._sem(key)
        self.count[key] += 16
        self.streams[q].append(('op', fn, key, 16))
        self._mark(key, self.count[key], reads, writes)

    def barrier(self):
        for e in ENGS:
            for k, v in self.count.items():
                if k == e:
                    continue
                if isinstance(k, tuple) and k[0] == 'd' and isinstance(k[1], tuple) and k[1][0] in ('w', 'x', 'setup'):
                    continue
                if v > self.seen[e].get(k, 0):
                    self.streams[e].append(('wait', k, v))
                    self.seen[e][k] = v

    def wait_all(self, eng, keys):
        for k in keys:
            v = self.count.get(k, 0)
            if v > self.seen[eng].get(k, 0):
                self.streams[eng].append(('wait', k, v))
                self.seen[eng][k] = v

    def emit(self):
        nc = self.nc
        engmap = {'pe': 'tensor', 'dve': 'vector', 'act': 'scalar', 'pool': 'gpsimd', 'sp': 'sync'}
        with nc.Block() as block:
            for e in ENGS:
                stream = self.streams[e]
                if not stream:
                    continue

                def body(engine, stream=stream):
                    for item in stream:
                        if item[0] == 'wait':
                            engine.wait_ge(self.sems[item[1]], item[2])
                        else:
                            ins = item[1](engine)
                            ins.then_inc(self.sems[item[2]], item[3])

                getattr(block, engmap[e])(body)


class StopBuild(Exception):
    pass


class Arena:
    def __init__(self, nc):
        self.nc = nc
        self.base = (int(nc.sbuf_base) + 63) // 64 * 64
        self.top = int(nc.sbuf_top)
        self.cur = self.base
        self.n = 0
        self.hi = self.base

    def alloc(self, name, shape, dtype):
        esz = 4 if dtype == F32 else 2
        nbytes = esz
        for s in shape[1:]:
            nbytes *= s
        nbytes = (nbytes + 63) // 64 * 64
        off = self.cur
        self.cur += nbytes
        self.hi = max(self.hi, self.cur)
        assert self.cur <= self.top, f"SBUF overflow at {name}: {self.cur} > {self.top}"
        self.n += 1
        return self.nc.alloc_sbuf_tensor_at(f"{name}_{self.n}", list(shape), dtype, offset=off)

    def mark(self):
        return self.cur

    def reset(self, m):
        self.cur = m


def sb_layout(first):
    if first:
        tiles = [(0, 16)] + [(16 + 128 * j, 128) for j in range(4)]
        chunks = [(0, 16)] + [(16 + 64 * j, 64) for j in range(8)]
        segs = [(0, 16), (16, 512)]
        T = 528
    else:
        tiles = [(128 * j, 128) for j in range(4)]
        chunks = [(64 * j, 64) for j in range(8)]
        segs = [(0, 512)]
        T = 512
    return T, tiles, chunks, segs


R_N1, R_L0, R_L1, R_HGN, R_N2, R_MCW, R_MCB, R_FCW, R_FCB, PROWS = 0, 8, 16, 24, 32, 40, 88, 100, 232, 276


def build_program(n_sb, taps=None, stop_after=None, n_pre=0):
    taps = taps or []
    TOKP = 512 * n_pre
    TOKM = 16 + 512 * n_sb
    TOK = TOKP + TOKM
    nc = bass.Bass("TRN2", target_bir_lowering=False)
    dr = lambda name, shape, kind="ExternalInput": nc.dram_tensor(name, shape, F32, kind=kind).ap()
    xin = dr("xin", [TOK, D])
    w_in = dr("w_in", [D, DPROJ])
    w_out = dr("w_out", [2 * D, D])
    w_up = dr("w_up", [D, 2 * DFF])
    w_down = dr("w_down", [DFF, D])
    pvec = dr("pvec", [PROWS, 128])
    m2nw_d = dr("m2nw", [D])
    fnw_d = dr("fnw", [D])
    dtb_d = dr("dtb", [16])
    alog_d = dr("alog", [16])
    dsk_d = dr("dsk", [16])
    flag_d = dr("flag", [128, 1])
    out = dr("out", [TOKM, D], kind="ExternalOutput")
    tap_out = {}

    P = Prog(nc)
    A = Arena(nc)
    TM = 528

    acc = [nc.alloc_psum_tensor(f"acc{i}", [128, 512], F32) for i in range(2)]
    pb2 = nc.alloc_psum_tensor("pb2", [128, 512], F32)
    pT = nc.alloc_psum_tensor("pT", [128, 8, 128], BF16)
    pb45 = nc.alloc_psum_tensor("pb45", [128, 1024], F32)
    pb6 = nc.alloc_psum_tensor("pb6", [128, 512], F32)
    pb7 = nc.alloc_psum_tensor("pb7", [128, 512], F32)
    PB4 = ['pb4']
    PB5 = ['pb5']
    PB6 = ['pb6']
    PB7 = ['pb7']
    pT7 = pb7[:, :].bitcast(BF16).rearrange("p (k c) -> p k c", k=8)
    pTs = [(pT, 'pT'), (pT7, 'pb7')]
    pT2v = pb2[:, :].bitcast(BF16).rearrange("p (k c) -> p k c", k=8)
    pTm = [(pT, 'pT'), (pT2v, 'pb2')]
    pTf = pT[:, :, :].rearrange("p k c -> p (k c)").bitcast(F32)
    yob = [(pb7[:, :], 'pb7'), (pTf, 'pT')]
    st = {'acc': 0, 'mini': 0, 'cv': 0}

    def next_acc():
        i = st['acc'] % 2
        st['acc'] += 1
        return acc[i], f'acc{i}'

    def next_mini():
        i = st['mini'] % 16
        st['mini'] += 1
        return pb2[:, i * 16:(i + 1) * 16], 'pb2'

    cvbanks = [(pb45[:, 0:512], PB4), (pb45[:, 512:1024], PB5), (pb6[:, :], PB6), (pb7[:, :], PB7)]

    def next_cv():
        i = st['cv'] % 4
        st['cv'] += 1
        return cvbanks[i]

    h = A.alloc("h", [128, 5, D], F32)
    S_f = A.alloc("S_f", [128, 8, 128], F32)
    S_b = A.alloc("S_b", [128, 8, 128], BF16)
    ST_f = A.alloc("ST_f", [128, 2, 512], F32)
    ST_b = A.alloc("ST_b", [128, 2, 512], BF16)
    hal = A.alloc("hal", [128, 12, 4], BF16)
    hal2 = A.alloc("hal2", [128, 44, 2], BF16)
    identf = A.alloc("identf", [128, 128], F32)
    ident = A.alloc("ident", [128, 128], BF16)
    onesf = A.alloc("onesf", [128, 128], F32)
    onesb = A.alloc("onesb", [128, 128], BF16)
    triU = A.alloc("triU", [128, 128], F32)
    mneg = A.alloc("mneg", [128, 4, 128], BF16)
    rmask0 = A.alloc("rmask0", [128, 528], F32)
    rmask1 = A.alloc("rmask1", [128, 512], F32)
    pcol = A.alloc("pcol", [128, PROWS], F32)
    oml = A.alloc("oml", [128, 8], F32)
    homl = A.alloc("homl", [128, 8], F32)
    nhoml = A.alloc("nhoml", [128, 8], F32)
    dl = A.alloc("dl", [128, 8], F32)
    m2nw_b = A.alloc("m2nw_b", [128, D], F32)
    fnw_b = A.alloc("fnw_b", [128, D], F32)
    dtb_b = A.alloc("dtb_b", [128, 16], F32)
    aneg_b = A.alloc("aneg_b", [128, 16], F32)
    dsk_b = A.alloc("dsk_b", [128, 16], F32)
    epsc = A.alloc("epsc", [128, 1], F32)
    onec = A.alloc("onec", [128, 1], F32)
    nhalf = A.alloc("nhalf", [128, 2], F32)
    flagc = A.alloc("flagc", [128, 1], F32)
    ptmp = A.alloc("ptmp", [128, 128], F32)
    NSLOT = 4
    ws = [A.alloc(f"ws{i}", [128, 8, 512], BF16) for i in range(NSLOT)]
    uT = A.alloc("uT", [128, 8, TM], BF16)
    mixT = A.alloc("mixT", [128, 16, TM], BF16)
    junk = A.alloc("junk", [128, D], BF16)
    xn = A.alloc("xn", [128, D], BF16)
    xn2 = [xn, A.alloc("xnb", [128, D], BF16)]
    ssn = A.alloc("ssn", [128, 8], F32)
    rsn = A.alloc("rsn", [128, 8], F32)
    ss = A.alloc("ss", [128, 4], F32)
    rs = A.alloc("rs", [128, 4], F32)
    PH = A.mark()

    P.op('pool', lambda e: e.memset(identf[:], 1.0), writes=['identf'])
    P.op('pool', lambda e: e.affine_select(out=identf[:], in_=identf[:], pattern=[[-1, 128]], compare_op=ALU.is_equal,
                                           fill=0.0, base=0, channel_multiplier=1), reads=['identf'], writes=['identf'])
    P.op('pool', lambda e: e.tensor_copy(out=ident[:], in_=identf[:]), reads=['identf'], writes=['ident'])
    P.op('pool', lambda e: e.memset(onesf[:], 1.0), writes=['onesf'])
    P.op('pool', lambda e: e.memset(onesb[:], 1.0), writes=['onesb'])
    P.op('pool', lambda e: e.memset(triU[:], 1.0), writes=['triU'])
    P.op('pool', lambda e: e.affine_select(out=triU[:], in_=triU[:], pattern=[[1, 128]], compare_op=ALU.is_ge,
                                           fill=0.0, base=0, channel_multiplier=-1), reads=['triU'], writes=['triU'])
    P.op('pool', lambda e: e.memset(mneg[:], 0.0), writes=['mneg'])
    P.op('pool', lambda e: e.affine_select(out=mneg[:], in_=mneg[:], pattern=[[0, 4], [1, 128]], compare_op=ALU.is_ge,
                                           fill=-30000.0, base=0, channel_multiplier=-1), reads=['mneg'], writes=['mneg'])
    P.op('pool', lambda e: e.memset(rmask0[:], 1.0), writes=['rmask0'])
    P.op('pool', lambda e: e.memset(rmask0[:, 0:1], 0.0), writes=['rmask0'])
    P.op('pool', lambda e: e.memset(rmask0[:, 16:528:64], 0.0), writes=['rmask0'])
    P.op('pool', lambda e: e.memset(rmask1[:], 1.0), writes=['rmask1'])
    P.op('pool', lambda e: e.memset(rmask1[:, 0:512:64], 0.0), writes=['rmask1'])
    P.op('pool', lambda e: e.memset(epsc[:], EPS), writes=['epsc'])
    P.op('pool', lambda e: e.memset(onec[:], 1.0), writes=['onec'])
    P.op('pool', lambda e: e.memset(nhalf[:], -0.5), writes=['nhalf'])
    P.op('pool', lambda e: e.memset(S_f[:], 0.0), writes=['S_f'])
    P.op('pool', lambda e: e.memset(S_b[:], 0.0), writes=['S_b'])
    P.op('pool', lambda e: e.memset(ST_f[:], 0.0), writes=['ST_f'])
    P.op('pool', lambda e: e.memset(ST_b[:], 0.0), writes=['ST_b'])
    P.op('pool', lambda e: e.memset(hal[:], 0.0), writes=[('hal', i) for i in range(12)])
    P.op('pool', lambda e: e.memset(hal2[:], 0.0), writes=[('hal2', i) for i in range(44)])
    for r0 in range(0, PROWS, 128):
        nr = min(128, PROWS - r0)
        P.dma('sp', lambda e, r0=r0, nr=nr: e.dma_start(out=ptmp[0:nr, :], in_=pvec[r0:r0 + nr, :]), writes=['ptmp'], chan=('setup', r0))
        P.op('pe', lambda e, nr=nr: e.transpose(out=acc[0][:, 0:nr], in_=ptmp[0:nr, :], identity=identf[0:nr, 0:nr]),
             reads=['ptmp', 'identf'], writes=['acc0'])
        P.op('dve', lambda e, r0=r0, nr=nr: e.tensor_copy(out=pcol[:, r0:r0 + nr], in_=acc[0][:, 0:nr]), reads=['acc0'], writes=['pcol'])
    P.op('dve', lambda e: e.tensor_tensor(out=dl[:], in0=pcol[:, R_L0:R_L0 + 8], in1=pcol[:, R_L1:R_L1 + 8], op=ALU.subtract),
         reads=['pcol'], writes=['dl'])
    P.op('act', lambda e: e.activation(out=oml[:], in_=dl[:], func=AF.Sigmoid, scale=-1.0), reads=['dl'], writes=['oml'])
    P.op('dve', lambda e: e.tensor_scalar(out=homl[:], in0=oml[:], scalar1=0.5, scalar2=None, op0=ALU.mult), reads=['oml'], writes=['oml'])
    P.op('dve', lambda e: e.tensor_scalar(out=nhoml[:], in0=oml[:], scalar1=-0.5, scalar2=None, op0=ALU.mult), reads=['oml'], writes=['oml'])
    for dst, src, nm in [(m2nw_b, m2nw_d, 'm2nw_b'), (fnw_b, fnw_d, 'fnw_b'), (dtb_b, dtb_d, 'dtb_b'), (aneg_b, alog_d, 'aneg_b'),
                         (dsk_b, dsk_d, 'dsk_b')]:
        P.dma('sp', lambda e, dst=dst, src=src: e.dma_start(out=dst[:], in_=src.partition_broadcast(128)), writes=[nm], chan=('setup', nm))
    P.dma('sp', lambda e: e.dma_start(out=flagc[:], in_=flag_d[:, :]), writes=['flagc'], chan=('setup', 'flag'))
    P.op('act', lambda e: e.activation(out=aneg_b[:], in_=aneg_b[:], func=AF.Exp), reads=['aneg_b'], writes=['aneg_b'])
    P.op('dve', lambda e: e.tensor_scalar(out=aneg_b[:], in0=aneg_b[:], scalar1=-1.0, scalar2=None, op0=ALU.mult),
         reads=['aneg_b'], writes=['aneg_b'])

    pieces = []

    def add_piece(wd, r0, nk, c0, ncols):
        pieces.append((wd, r0, nk, c0, ncols))
        return len(pieces) - 1

    wstate = {'issued': 0}
    live = []

    def issue_to(idx):
        while wstate['issued'] <= min(idx, len(pieces) - 1):
            i = wstate['issued']
            wd, r0, nk, c0, ncols = pieces[i]
            sl = i % NSLOT
            src = wd[r0:r0 + nk * 128, c0:c0 + ncols].rearrange("(k p) n -> p k n", p=128)
            P.dma('pool', lambda e, sl=sl, src=src, nk=nk, ncols=ncols: e.dma_start(out=ws[sl][:, 0:nk, 0:ncols], in_=src),
                  writes=[('ws', sl)], chan=('w', sl))
            wstate['issued'] += 1

    def use_piece(idx, hold=False):
        if not hold:
            for j in live:
                issue_to(j + NSLOT)
            del live[:]
        issue_to(idx)
        live.append(idx)
        sl = idx % NSLOT
        return ws[sl], ('ws', sl)

    sched = []
    sb_modes = ['prefix'] * n_pre + ['main'] * n_sb
    for sb in range(n_pre + n_sb):
        d_ = {}
        pref = sb_modes[sb] == 'prefix'
        for hg in range(2):
            d_[('f', hg)] = add_piece(w_in, 0, 8, 1024 + hg * 512, 512)
            if not pref:
                d_[('q', hg)] = add_piece(w_in, 0, 8, hg * 512, 512)
            d_[('i', hg)] = add_piece(w_in, 0, 8, 2048 + hg * 512, 512)
            if not pref:
                d_[('g', hg)] = add_piece(w_in, 0, 8, 3072 + hg * 512, 512)
        for j in range(3):
            d_[('xbc', j)] = add_piece(w_in, 0, 8, 5120 + j * 512, 512 if not (pref and j == 2) else 256)
        if not pref:
            for j in range(2):
                d_[('z', j)] = add_piece(w_in, 0, 8, 4096 + j * 512, 512)
        d_['dt'] = add_piece(w_in, 0, 8, 6656, 16)
        if pref:
            sched.append(d_)
            continue
        for c2 in range(2):
            for kh in range(2):
                d_[('wo', c2, kh)] = add_piece(w_out, kh * 1024, 8, c2 * 512, 512)
        for j in range(6):
            ncol = 512 if j < 5 else 256
            d_[('ug', j)] = add_piece(w_up, 0, 8, j * 512, ncol)
            d_[('uv', j)] = add_piece(w_up, 0, 8, DFF + j * 512, ncol)
        for c2 in range(2):
            for kp in range(3):
                nk = 8 if kp < 2 else 6
                d_[('wd', c2, kp)] = add_piece(w_down, kp * 1024, nk, c2 * 512, 512)
        sched.append(d_)

    def tap(name, ap, shape, reads):
        if name in taps:
            t = nc.dram_tensor("tap_" + name, list(shape), ap.dtype if hasattr(ap, 'dtype') else F32, kind="ExternalOutput").ap()
            tap_out[name] = t
            P.dma('sp', lambda e: e.dma_start(out=t, in_=ap), reads=reads, chan=('tap', name))

    def stop_pt(name):
        if stop_after == name:
            raise StopBuild()

    issue_to(NSLOT - 1)
    tok0 = 0
    try:
      def do_sb(sb, tok0, out0):
          pref = sb_modes[sb] == 'prefix'
          first = (sb == n_pre)
          T, tiles, chunks, segs = sb_layout(first)
          NCH = len(chunks)
          rmask = rmask0 if first else rmask1
          rmask_n = 'rmask0' if first else 'rmask1'
          pc = sched[sb]
          dbg = (sb == n_pre)

          def mm_fm(wt, wname, cb, rhs_t, rhs_names, evac):
              for (soff, sn) in segs:
                  if sn <= 16:
                      ps, pn = next_mini()
                      pn = [pn]
                  else:
                      ps, pn = next_acc()
                      ps = ps[:, 0:sn]
                      pn = [pn]
                  for k in range(8):
                      P.op('pe', lambda e, ps=ps, k=k, cb=cb, soff=soff, sn=sn: e.matmul(
                          ps, lhsT=wt[:, k, cb * 128:(cb + 1) * 128], rhs=rhs_t[:, k, soff:soff + sn], start=(k == 0), stop=(k == 7)),
                          reads=[wname] + rhs_names, writes=pn)
                  evac(ps, pn, soff, sn)

          def norm_to_fm(r_w, with_load):
              def part_a(ti, off, n):
                  q = ti % 2
                  if with_load:
                      P.dma('act', lambda e: e.dma_start(out=h[0:n, ti, :], in_=xin[tok0 + off:tok0 + off + n, :]),
                            writes=[('h', ti)], chan=('x', ti))
                  P.op('act', lambda e: e.activation(out=junk[0:n, :], in_=h[0:n, ti, :], func=AF.Square, accum_out=ssn[0:n, ti:ti + 1]),
                       reads=[('h', ti)], writes=['junk', ('ssn', ti)])
                  P.op('act', lambda e: e.activation(out=rsn[0:n, ti:ti + 1], in_=ssn[0:n, ti:ti + 1], func=AF.Sqrt, scale=1.0 / D, bias=epsc[0:n, :]),
                       reads=[('ssn', ti), 'epsc'], writes=[('rsn', ti)])
                  P.op('dve', lambda e: e.reciprocal(out=rsn[0:n, ti:ti + 1], in_=rsn[0:n, ti:ti + 1]), reads=[('rsn', ti)], writes=[('rsn', ti)])
                  P.op('dve', lambda e: e.tensor_scalar(out=xn2[q][0:n, :], in0=h[0:n, ti, :], scalar1=rsn[0:n, ti:ti + 1], scalar2=None,
                                                        op0=ALU.mult), reads=[('h', ti), ('rsn', ti)], writes=[('xn', q)])

              def part_b(ti, off, n):
                  q = ti % 2
                  tb, tbn = pTs[q]
                  for k in range(8):
                      P.op('pe', lambda e, k=k: e.transpose(out=tb[:, k, 0:n], in_=xn2[q][0:n, k * 128:(k + 1) * 128], identity=ident[0:n, 0:n]),
                           reads=[('xn', q), 'ident'], writes=[tbn])
                  P.op('dve', lambda e: e.tensor_tensor(out=uT[:, :, off:off + n], in0=tb[:, :, 0:n],
                                                        in1=pcol[:, r_w:r_w + 8].unsqueeze(2).broadcast_to([128, 8, n]), op=ALU.mult),
                       reads=[tbn, 'pcol'], writes=['uT'])

              nt = len(tiles)
              for i in range(nt + 1):
                  if i < nt:
                      part_a(i, *tiles[i])
                  if i >= 1:
                      part_b(i - 1, *tiles[i - 1])

          A.reset(PH)
          norm_to_fm(R_N1, True)
          if dbg:
              tap('uT', uT[:, :, 0:T], [128, 8, T], ['uT'])
          stop_pt('s0')

          A.reset(PH)
          kinv = A.alloc("kinv", [128, 4, TM], BF16)
          kend = A.alloc("kend", [128, 4, TM], BF16)
          qdec = A.alloc("qdec", [128, 4, TM], BF16)
          sgf = A.alloc("sgf", [128, 4, TM], BF16)
          epos = A.alloc("epos", [128, 4, TM], F32)
          fa = A.alloc("fa", [128, 4, TM], F32)
          fb = A.alloc("fb", [128, 4, TM], F32)
          fc = A.alloc("fc", [128, 4, TM], F32)
          t1 = A.alloc("t1", [128, TM], F32)
          t3 = A.alloc("t3", [128, TM], F32)
          dch = A.alloc("dch", [128, 4, 16], F32)
          v_tm = A.alloc("v_tm", [64, 9, 512], BF16)
          ke_tm = A.alloc("ke_tm", [64, 9, 512], BF16)
          scT = [A.alloc(f"scT{i}", [64, 4, 64], BF16) for i in range(2)]
          o_sb = A.alloc("o_sb", [128, 4, TM], F32)
          osq4 = A.alloc("osq4", [128, 4, TM], BF16)
          rst4 = A.alloc("rst4", [128, 4, TM], F32)
          P.barrier()

          for hg in range(2):
              wt, wn = use_piece(pc[('f', hg)])
              for hl in range(4):
                  mm_fm(wt, wn, hl, uT, ['uT'], lambda ps, pn, soff, sn, hl=hl: P.op(
                      'act', lambda e: e.activation(out=fa[:, hl, soff:soff + sn], in_=ps, func=AF.Tanh, scale=0.5), reads=pn, writes=[('fa', hl)]))
              for hl in range(4):
                  hd = hg * 4 + hl
                  P.op('dve', lambda e, hl=hl, hd=hd: e.tensor_scalar(out=fa[:, hl, 0:T], in0=fa[:, hl, 0:T], scalar1=nhoml[:, hd:hd + 1],
                                                                     scalar2=homl[:, hd:hd + 1], op0=ALU.mult, op1=ALU.add),
                       reads=[('fa', hl), 'oml'], writes=[('fa', hl)])
              for hl in range(4):
                  P.op('act', lambda e, hl=hl: e.activation(out=fb[:, hl, 0:T], in_=fa[:, hl, 0:T], func=AF.Ln, scale=-1.0, bias=onec[:, :]),
                       reads=[('fa', hl), 'onec'], writes=[('fb', hl)])
              for hl in range(4):
                  P.op('dve', lambda e, hl=hl: e.tensor_tensor_scan(out=fc[:, hl, 0:T], data0=rmask[:, 0:T], data1=fb[:, hl, 0:T], initial=0.0,
                                                                    op0=ALU.mult, op1=ALU.add), reads=[('fb', hl), rmask_n], writes=[('fc', hl)])
              for hl in range(4):
                  P.op('act', lambda e, hl=hl: e.activation(out=fb[:, hl, 0:T], in_=fc[:, hl, 0:T], func=AF.Exp, scale=-1.0),
                       reads=[('fc', hl)], writes=[('fb', hl)])
                  P.op('act', lambda e, hl=hl: e.activation(out=epos[:, hl, 0:T], in_=fc[:, hl, 0:T], func=AF.Exp), reads=[('fc', hl)], writes=[('epos', hl)])
              ce0 = chunks[0][0] + chunks[0][1] - 1
              for hl in range(4):
                  P.op('dve', lambda e, hl=hl: e.tensor_tensor(out=kinv[:, hl, 0:T], in0=fa[:, hl, 0:T], in1=fb[:, hl, 0:T], op=ALU.mult),
                       reads=[('fa', hl), ('fb', hl)], writes=[('kinv', hl)])
                  P.op('dve', lambda e, hl=hl: e.tensor_copy(out=dch[:, hl, 0:NCH], in_=epos[:, hl, ce0:T:64]),
                       reads=[('epos', hl)], writes=[('dch', hl)])
                  cs = 0
                  if first:
                      P.op('dve', lambda e, hl=hl: e.tensor_scalar(out=kend[:, hl, 0:16], in0=kinv[:, hl, 0:16], scalar1=dch[:, hl, 0:1], scalar2=None,
                                                                   op0=ALU.mult), reads=[('kinv', hl), ('dch', hl)], writes=[('kend', hl)])
                      cs = 1
                  t0_ = chunks[cs][0]
                  P.op('dve', lambda e, hl=hl, cs=cs, t0_=t0_: e.tensor_tensor(
                      out=kend[:, hl, t0_:T].rearrange("p (c j) -> p c j", j=64), in0=kinv[:, hl, t0_:T].rearrange("p (c j) -> p c j", j=64),
                      in1=dch[:, hl, cs:cs + 8].unsqueeze(2).broadcast_to([128, 8, 64]), op=ALU.mult),
                      reads=[('kinv', hl), ('dch', hl)], writes=[('kend', hl)])
              stop_pt('h_f')
              if not pref:
                wt, wn = use_piece(pc[('q', hg)])
              for hl in range(4 if not pref else 0):
                  mm_fm(wt, wn, hl, uT, ['uT'], lambda ps, pn, soff, sn, hl=hl: P.op(
                      'act', lambda e: e.activation(out=(t1, t3)[hl % 2][:, soff:soff + sn], in_=ps, func=AF.Silu), reads=pn, writes=[('tq', hl % 2)]))
                  P.op('dve', lambda e, hl=hl: e.tensor_tensor(out=qdec[:, hl, 0:T], in0=(t1, t3)[hl % 2][:, 0:T], in1=epos[:, hl, 0:T], op=ALU.mult),
                       reads=[('tq', hl % 2), ('epos', hl)], writes=[('qdec', hl)])
              stop_pt('h_q')
              wt, wn = use_piece(pc[('i', hg)])
              for c, (c0, cn) in enumerate(chunks):
                  ps, pn = next_acc()
                  for k in range(8):
                      P.op('pe', lambda e, ps=ps, k=k, c0=c0, cn=cn, wt=wt: e.matmul(ps[0:cn, :], lhsT=uT[:, k, c0:c0 + cn], rhs=wt[:, k, :],
                                                                                      start=(k == 0), stop=(k == 7)),
                           reads=[wn, 'uT'], writes=[pn])
                  P.op('act', lambda e, ps=ps, c=c, cn=cn: e.activation(out=v_tm[0:cn, c, :], in_=ps[0:cn, :], func=AF.Copy),
                       reads=[pn], writes=[('v_tm', c)])
              stop_pt('h_i')
              if not pref:
                wt, wn = use_piece(pc[('g', hg)])
              for hl in range(4 if not pref else 0):
                  mm_fm(wt, wn, hl, uT, ['uT'], lambda ps, pn, soff, sn, hl=hl: P.op(
                      'act', lambda e: e.activation(out=sgf[:, hl, soff:soff + sn], in_=ps, func=AF.Silu), reads=pn, writes=[('sgf', hl)]))
              stop_pt('h_g')
              for c, (c0, cn) in enumerate(chunks):
                  tb, tbn = pTs[c % 2]
                  for hl in range(4):
                      P.op('pe', lambda e, hl=hl, c0=c0, cn=cn, tb=tb: e.transpose(out=tb[0:cn, hl, :], in_=kend[:, hl, c0:c0 + cn], identity=ident[:, :]),
                           reads=[('kend', hl), 'ident'], writes=[tbn])
                  P.op('dve', lambda e, c=c, cn=cn, tb=tb: e.tensor_copy(out=ke_tm[0:cn, c, :].rearrange("p (h k) -> p h k", h=4), in_=tb[0:cn, 0:4, :]),
                       reads=[tbn], writes=[('ke_tm', c)])
              stop_pt('h_t')
              for c, (c0, cn) in enumerate(chunks):
                  par = c % 2
                  sbk, sbn = [(pb45[:, 0:512], 'pb4'), (acc[0][:, :], 'acc0')][par]
                  obk, obn = [(pb45[:, 512:1024], 'pb5'), (acc[1][:, :], 'acc1')][par]
                  psS = sbk[0:cn, 0:256].rearrange("p (h r) -> p h r", h=4)[:, :, 0:cn]
                  psO = obk[:, 0:256].rearrange("p (h r) -> p h r", h=4)[:, :, 0:cn]
                  for hl in range(4 if not pref else 0):
                      P.op('pe', lambda e, psS=psS, hl=hl, c0=c0, cn=cn: e.matmul(psS[:, hl, :], lhsT=kinv[:, hl, c0:c0 + cn], rhs=qdec[:, hl, c0:c0 + cn],
                                                                                  start=True, stop=True),
                           reads=[('kinv', hl), ('qdec', hl)], writes=[sbn])
                  if not pref:
                   P.op('dve', lambda e, psS=psS, par=par, cn=cn: e.tensor_tensor(
                      out=scT[par][0:cn, :, 0:cn], in0=psS, in1=triU[0:cn, 0:cn].unsqueeze(1).broadcast_to([cn, 4, cn]), op=ALU.mult),
                      reads=[sbn, 'triU'], writes=[('scT', par)])
                  ubk, ubn = [(pb6, 'pb6'), (pb7, 'pb7')][c % 2]
                  for hl in range(4):
                      P.op('pe', lambda e, hl=hl, c=c, cn=cn, ubk=ubk: e.matmul(ubk[:, hl * 128:(hl + 1) * 128], lhsT=ke_tm[0:cn, c, hl * 128:(hl + 1) * 128],
                                                                                rhs=v_tm[0:cn, c, hl * 128:(hl + 1) * 128], start=True, stop=True),
                           reads=[('ke_tm', c), ('v_tm', c)], writes=[ubn])
                  for hl in range(4):
                      hd = hg * 4 + hl
                      P.op('dve', lambda e, hl=hl, hd=hd, c=c, ubk=ubk: e.scalar_tensor_tensor(
                          out=S_f[:, hd, :], in0=S_f[:, hd, :], scalar=dch[:, hl, c:c + 1], in1=ubk[:, hl * 128:(hl + 1) * 128],
                          op0=ALU.mult, op1=ALU.add), reads=[('S_f', hd), ('dch', hl), ubn], writes=[('S_f', hd)])
                  for hl in range(4 if not pref else 0):
                      hd = hg * 4 + hl
                      P.op('pe', lambda e, psO=psO, hl=hl, c=c, cn=cn, par=par: e.matmul(
                          psO[:, hl, :], lhsT=v_tm[0:cn, c, hl * 128:(hl + 1) * 128], rhs=scT[par][0:cn, hl, 0:cn], start=True, stop=False),
                          reads=[('v_tm', c), ('scT', par)], writes=[obn])
                      P.op('pe', lambda e, psO=psO, hl=hl, hd=hd, c0=c0, cn=cn: e.matmul(
                          psO[:, hl, :], lhsT=S_b[:, hd, :], rhs=qdec[:, hl, c0:c0 + cn], start=False, stop=True),
                          reads=[('S_b', hd), ('qdec', hl)], writes=[obn])
                  if not pref:
                   P.op('act', lambda e, psO=psO, c0=c0, cn=cn: e.activation(out=o_sb[:, :, c0:c0 + cn], in_=psO, func=AF.Copy),
                       reads=[obn], writes=['o_sb'])
                  if (not pref) or c == NCH - 1:
                      P.op('act', lambda e, hg=hg: e.activation(out=S_b[:, hg * 4:(hg + 1) * 4, :], in_=S_f[:, hg * 4:(hg + 1) * 4, :], func=AF.Copy),
                           reads=[('S_f', hg * 4 + i) for i in range(4)], writes=[('S_b', hg * 4 + i) for i in range(4)])
              stop_pt('h_scan')
              NH = 4 if not pref else 0
              nbanks = [(pb45[:, 0:512], ['pb4']), (pb45[:, 512:1024], ['pb5']), (pb6[:, :], ['pb6']), (pb7[:, :], ['pb7'])]
              for hl in range(NH):
                  P.op('act', lambda e, hl=hl: e.activation(out=osq4[:, hl, 0:T], in_=o_sb[:, hl, 0:T], func=AF.Square), reads=['o_sb'], writes=[('osq', hl)])
              npss = {}
              for hl in range(NH):
                  for (soff, sn) in segs:
                      if sn <= 16:
                          ps, pn = next_mini()
                          pn = [pn]
                      else:
                          ps, pn = nbanks[hl][0][:, 0:sn], nbanks[hl][1]
                      npss[(hl, soff)] = (ps, pn)
                      P.op('pe', lambda e, ps=ps, soff=soff, sn=sn, hl=hl: e.matmul(ps, lhsT=onesb[:, :], rhs=osq4[:, hl, soff:soff + sn], start=True, stop=True),
                           reads=[('osq', hl), 'onesb'], writes=pn)
                      if sn <= 16:
                          P.op('act', lambda e, ps=ps, soff=soff, sn=sn, hl=hl: e.activation(out=rst4[:, hl, soff:soff + sn], in_=ps, func=AF.Sqrt, scale=1.0 / 128,
                                                                                             bias=epsc[:, :]), reads=pn + ['epsc'], writes=[('rst', hl)])
              for hl in range(NH):
                  for (soff, sn) in segs:
                      if sn > 16:
                          ps, pn = npss[(hl, soff)]
                          P.op('act', lambda e, ps=ps, soff=soff, sn=sn, hl=hl: e.activation(out=rst4[:, hl, soff:soff + sn], in_=ps, func=AF.Sqrt, scale=1.0 / 128,
                                                                                             bias=epsc[:, :]), reads=pn + ['epsc'], writes=[('rst', hl)])
              for hl in range(NH):
                  hd = hg * 4 + hl
                  P.op('dve', lambda e, hl=hl: e.reciprocal(out=rst4[:, hl, 0:T], in_=rst4[:, hl, 0:T]), reads=[('rst', hl)], writes=[('rst', hl)])
                  P.op('dve', lambda e, hl=hl: e.tensor_tensor(out=rst4[:, hl, 0:T], in0=o_sb[:, hl, 0:T], in1=rst4[:, hl, 0:T], op=ALU.mult),
                       reads=['o_sb', ('rst', hl)], writes=[('rst', hl)])
                  P.op('dve', lambda e, hl=hl, hd=hd: e.scalar_tensor_tensor(
                      out=mixT[:, hd, 0:T], in0=rst4[:, hl, 0:T], scalar=pcol[:, R_HGN + hd:R_HGN + hd + 1], in1=sgf[:, hl, 0:T], op0=ALU.mult, op1=ALU.mult),
                      reads=[('rst', hl), 'pcol', ('sgf', hl)], writes=[('mixT', hd)])
          if dbg:
              tap('mixA', mixT[:, 0:8, 0:T], [128, 8, T], [('mixT', i) for i in range(8)])
          stop_pt('hgrn')

          A.reset(PH)
          pre = [A.alloc(f"pre{i}", [128, 3 + TM], BF16) for i in range(4)]
          dg = [A.alloc(f"dg{i}", [128, 4, 128], BF16) for i in range(4)]
          xfs = [A.alloc(f"xf{i}", [128, 4, TM], BF16) for i in range(2)]
          B_fm = A.alloc("B_fm", [128, 2, TM], BF16)
          C_fm = A.alloc("C_fm", [128, 2, TM], BF16)
          x_tm = A.alloc("x_tm", [128, 5, 1024], BF16)
          B_tm = A.alloc("B_tm", [128, 5, 256], BF16)
          zs_tm = A.alloc("zs_tm", [128, 5, 1024], BF16)
          dt_tm = A.alloc("dt_tm", [128, 5, 16], F32)
          da_tm = A.alloc("da_tm", [128, 5, 16], F32)
          lndt = A.alloc("lndt", [128, 5, 16], F32)
          acum, nb, Eexp, cd, dte, w2, dtp, lnv, pol, msk = [A.alloc(nm, [128, 16], F32) for nm in
                                                           ['acum', 'nb', 'Eexp', 'cd', 'dte', 'w2', 'dtp', 'lnv', 'pol', 'msk']]
          rhsb = A.alloc("rhsb", [128, 16, 128], F32)
          LTs = [A.alloc(f"LT{i}", [128, 8, 128], BF16) for i in range(2)]
          MTs = [A.alloc(f"MT{i}", [128, 8, 128], BF16) for i in range(2)]
          xd = A.alloc("xd", [128, 1024], BF16)
          xw = A.alloc("xw", [128, 512], BF16)
          tmpf = A.alloc("tmpf", [128, 512], F32)
          yy = A.alloc("yy", [128, 1024], F32)
          y2 = A.alloc("y2", [128, 1024], F32)
          mo = A.alloc("mo", [128, 1024], BF16)
          P.barrier()

          def conv_fm(wt, wn, cbl, cb, ntap, prebuf, prename, dgbuf, dgname, halbuf, halname, rw, evac, rhs_t=uT, rhs_n='uT', phase=None):
              hw = ntap - 1
              hn = (halname, cb)
              if phase in (None, 'A'):
                  P.op('dve', lambda e: e.tensor_copy(out=prebuf[:, 0:hw], in_=halbuf[:, cb, 0:hw]), reads=[hn], writes=[prename])
                  for k in range(ntap):
                      P.op('dve', lambda e, k=k: e.tensor_scalar(out=dgbuf[:, k, :], in0=ident[:, :], scalar1=pcol[:, rw(k):rw(k) + 1], scalar2=None,
                                                                 op0=ALU.mult), reads=['ident', 'pcol'], writes=[dgname])
                  mm_fm(wt, wn, cbl, rhs_t, [rhs_n], lambda ps, pn, soff, sn: P.op(
                      'act', lambda e: e.activation(out=prebuf[:, hw + soff:hw + soff + sn], in_=ps, func=AF.Copy), reads=pn, writes=[prename]))
                  P.op('dve', lambda e: e.tensor_copy(out=halbuf[:, cb, 0:hw], in_=prebuf[:, T:T + hw]), reads=[prename], writes=[hn])
              if phase in (None, 'B'):
                  for (soff, sn) in segs:
                      if sn <= 16:
                          cps, cpn = next_mini()
                          cpn = [cpn]
                      else:
                          cps, cpn = next_cv()
                          cps = cps[:, 0:sn]
                      for k in range(ntap):
                          P.op('pe', lambda e, cps=cps, k=k, soff=soff, sn=sn: e.matmul(cps, lhsT=dgbuf[:, k, :], rhs=prebuf[:, soff + k:soff + k + sn],
                                                                                       start=(k == 0), stop=(k == ntap - 1)),
                               reads=[dgname, prename], writes=cpn)
                      evac(cps, cpn, soff, sn)

          def run_skewed(tasks, skew):
              nt = len(tasks)
              for i in range(nt + skew):
                  if i < nt:
                      tasks[i]('A')
                  if i - skew >= 0:
                      tasks[i - skew]('B')

          for j in range(3):
              wt, wn = use_piece(pc[('xbc', j)])
              tasks = []
              for cbl in range(4 if not (pref and j == 2) else 2):
                  cb = j * 4 + cbl
                  pi = cb % 4
                  if cb < 8:
                      dst, dname = xfs[j % 2][:, cbl, :], ('xf', j % 2)
                  elif cb < 10:
                      dst, dname = B_fm[:, cb - 8, :], 'B_fm'
                  else:
                      dst, dname = C_fm[:, cb - 10, :], 'C_fm'
                  tasks.append(lambda ph, wt=wt, wn=wn, cbl=cbl, cb=cb, pi=pi, dst=dst, dname=dname: conv_fm(
                      wt, wn, cbl, cb, 4, pre[pi], ('pre', pi), dg[pi], ('dg', pi), hal, 'hal', lambda k, cb=cb: R_MCW + k * 12 + cb,
                      lambda cps, cpn, soff, sn, dst=dst, dname=dname, cb=cb: P.op(
                          'act', lambda e: e.activation(out=dst[:, soff:soff + sn], in_=cps, func=AF.Silu, bias=pcol[:, R_MCB + cb:R_MCB + cb + 1]),
                          reads=cpn + ['pcol'], writes=[dname]), phase=ph))
              run_skewed(tasks, 1)
              if j < 2:
                  for ti, (off, n) in enumerate(tiles):
                      tb, tbn = pTm[ti % 2]
                      for cbl in range(4):
                          P.op('pe', lambda e, cbl=cbl, off=off, n=n, tb=tb, j=j: e.transpose(out=tb[0:n, cbl, :], in_=xfs[j % 2][:, cbl, off:off + n], identity=ident[:, :]),
                               reads=[('xf', j % 2), 'ident'], writes=[tbn])
                      P.op('dve', lambda e, ti=ti, n=n, j=j, tb=tb: e.tensor_copy(
                          out=x_tm[0:n, ti, j * 512:(j + 1) * 512].rearrange("p (c k) -> p c k", c=4), in_=tb[0:n, 0:4, :]),
                          reads=[tbn], writes=[('x_tm', ti)])
              else:
                  for ti, (off, n) in enumerate(tiles):
                      tb, tbn = pTm[ti % 2]
                      for g in range(2):
                          P.op('pe', lambda e, g=g, off=off, n=n, tb=tb: e.transpose(out=tb[0:n, g, :], in_=B_fm[:, g, off:off + n], identity=ident[:, :]),
                               reads=['B_fm', 'ident'], writes=[tbn])
                      P.op('dve', lambda e, ti=ti, n=n, tb=tb: e.tensor_copy(out=B_tm[0:n, ti, :].rearrange("p (c k) -> p c k", c=2), in_=tb[0:n, 0:2, :]),
                           reads=[tbn], writes=[('B_tm', ti)])
          for j in range(2 if not pref else 0):
              wt, wn = use_piece(pc[('z', j)])
              for ti, (off, n) in enumerate(tiles):
                  ps, pn = next_acc()
                  for k in range(8):
                      P.op('pe', lambda e, ps=ps, k=k, off=off, n=n, wt=wt: e.matmul(ps[0:n, :], lhsT=uT[:, k, off:off + n], rhs=wt[:, k, :],
                                                                                      start=(k == 0), stop=(k == 7)), reads=[wn, 'uT'], writes=[pn])
                  P.op('act', lambda e, ps=ps, ti=ti, n=n, j=j: e.activation(out=zs_tm[0:n, ti, j * 512:(j + 1) * 512], in_=ps[0:n, :], func=AF.Silu),
                       reads=[pn], writes=[('zs', ti)])
          wt, wn = use_piece(pc['dt'])
          for ti, (off, n) in enumerate(tiles):
              ps, pn = next_mini()
              for k in range(8):
                  P.op('pe', lambda e, ps=ps, k=k, off=off, n=n, wt=wt: e.matmul(ps[0:n, :], lhsT=uT[:, k, off:off + n], rhs=wt[:, k, 0:16],
                                                                                  start=(k == 0), stop=(k == 7)), reads=[wn, 'uT'], writes=[pn])
              P.op('dve', lambda e, ps=ps, n=n: e.tensor_tensor(out=dtp[0:n, :], in0=ps[0:n, :], in1=dtb_b[0:n, :], op=ALU.add),
                   reads=[pn, 'dtb_b'], writes=['dtp'])
              P.op('act', lambda e, n=n: e.activation(out=dtp[0:n, :], in_=dtp[0:n, :], func=AF.Exp), reads=['dtp'], writes=['dtp'])
              P.op('act', lambda e, n=n: e.activation(out=lnv[0:n, :], in_=dtp[0:n, :], func=AF.Ln, bias=onec[0:n, :]), reads=['dtp', 'onec'], writes=['lnv'])
              P.op('dve', lambda e, n=n: e.tensor_scalar(out=pol[0:n, :], in0=dtp[0:n, :], scalar1=1.0 / 7.0, scalar2=None, op0=ALU.mult), reads=['dtp'], writes=['pol'])
              for cc in (-1.0 / 6.0, 0.2, -0.25, 1.0 / 3.0, -0.5, 1.0):
                  P.op('dve', lambda e, n=n, cc=cc: e.scalar_tensor_tensor(out=pol[0:n, :], in0=pol[0:n, :], scalar=cc, in1=dtp[0:n, :], op0=ALU.add, op1=ALU.mult),
                       reads=['pol', 'dtp'], writes=['pol'])
              P.op('dve', lambda e, n=n: e.tensor_single_scalar(out=msk[0:n, :], in_=dtp[0:n, :], scalar=0.125, op=ALU.is_lt), reads=['dtp'], writes=['msk'])
              P.op('dve', lambda e, n=n: e.tensor_tensor(out=pol[0:n, :], in0=pol[0:n, :], in1=lnv[0:n, :], op=ALU.subtract), reads=['pol', 'lnv'], writes=['pol'])
              P.op('dve', lambda e, n=n: e.tensor_tensor(out=pol[0:n, :], in0=pol[0:n, :], in1=msk[0:n, :], op=ALU.mult), reads=['pol', 'msk'], writes=['pol'])
              P.op('dve', lambda e, n=n, ti=ti: e.tensor_tensor(out=dt_tm[0:n, ti, :], in0=pol[0:n, :], in1=lnv[0:n, :], op=ALU.add),
                   reads=['pol', 'lnv'], writes=[('dt', ti)])
              P.op('dve', lambda e, n=n, ti=ti: e.tensor_tensor(out=da_tm[0:n, ti, :], in0=dt_tm[0:n, ti, :], in1=aneg_b[0:n, :], op=ALU.mult),
                   reads=[('dt', ti), 'aneg_b'], writes=[('da', ti)])
              P.op('act', lambda e, n=n, ti=ti: e.activation(out=lndt[0:n, ti, :], in_=dt_tm[0:n, ti, :], func=AF.Ln), reads=[('dt', ti)], writes=[('lndt', ti)])
          stop_pt('m_proj')

          h8 = lambda ap: ap.rearrange("p (h q) -> p h q", q=64)
          for ti, (off, n) in enumerate(tiles):
              if not pref:
                P.op('dve', lambda e, ti=ti, n=n: e.tensor_tensor(out=h8(xd[0:n, :]), in0=h8(x_tm[0:n, ti, :]),
                                                                  in1=dsk_b[0:n, :].unsqueeze(2).broadcast_to([n, 16, 64]), op=ALU.mult),
                     reads=[('x_tm', ti), 'dsk_b'], writes=['xd'])
              psA, pnA = next_mini()
              P.op('pe', lambda e, psA=psA, ti=ti, n=n: e.matmul(psA[0:n, :], lhsT=triU[0:n, 0:n], rhs=da_tm[0:n, ti, :], start=True, stop=True),
                   reads=['triU', ('da', ti)], writes=[pnA])
              psL, pnL = next_mini()
              P.op('pe', lambda e, psL=psL, ti=ti, n=n: e.matmul(psL[:, :], lhsT=onesf[0:n, :], rhs=da_tm[0:n, ti, :], start=True, stop=True),
                   reads=['onesf', ('da', ti)], writes=[pnL])
              P.op('dve', lambda e, psA=psA, n=n: e.tensor_copy(out=acum[0:n, :], in_=psA[0:n, :]), reads=[pnA], writes=['acum'])
              if not pref:
                P.op('dve', lambda e, psA=psA, n=n, ti=ti: e.tensor_tensor(out=nb[0:n, :], in0=lndt[0:n, ti, :], in1=psA[0:n, :], op=ALU.subtract),
                   reads=[pnA, ('lndt', ti)], writes=['nb'])
                P.op('act', lambda e, psA=psA, n=n: e.activation(out=Eexp[0:n, :], in_=psA[0:n, :], func=AF.Exp), reads=[pnA], writes=['Eexp'])
              P.op('act', lambda e, psL=psL: e.activation(out=cd[:, :], in_=psL[:, :], func=AF.Exp), reads=[pnL], writes=['cd'])
              P.op('dve', lambda e, psL=psL, n=n: e.tensor_tensor(out=dte[0:n, :], in0=psL[0:n, :], in1=acum[0:n, :], op=ALU.subtract),
                   reads=[pnL, 'acum'], writes=['dte'])
              P.op('act', lambda e, n=n: e.activation(out=dte[0:n, :], in_=dte[0:n, :], func=AF.Exp), reads=['dte'], writes=['dte'])
              P.op('dve', lambda e, n=n, ti=ti: e.tensor_tensor(out=w2[0:n, :], in0=dte[0:n, :], in1=dt_tm[0:n, ti, :], op=ALU.mult),
                   reads=['dte', ('dt', ti)], writes=['w2'])
              if not pref:
               P.op('dve', lambda e, n=n, ti=ti: e.tensor_tensor(out=rhsb[0:n, :, 0:n], in0=triU[0:n, 0:n].unsqueeze(1).broadcast_to([n, 16, n]),
                                                                in1=da_tm[0:n, ti, :].unsqueeze(2).broadcast_to([n, 16, n]), op=ALU.mult),
                   reads=['triU', ('da', ti)], writes=['rhsb'])
              def gp1(g):
                  pcb = pb2[0:n, 256 + g * 128:256 + g * 128 + n]
                  if not pref:
                      gbk = [((pb45[:, 0:512], 'pb4'), (pb45[:, 512:1024], 'pb5')), ((acc[0][:, :], 'acc0'), (acc[1][:, :], 'acc1'))][g]
                      if n == 128:
                          for hh in range(2):
                              outp, bn = gbk[hh]
                              h0 = g * 8 + hh * 4
                              P.op('pe', lambda e, outp=outp, h0=h0: e.matmul(outp, lhsT=onesf[:, :], rhs=rhsb[:, h0:h0 + 4, :].rearrange("p h i -> p (h i)"),
                                                                              start=True, stop=False), reads=['onesf', 'rhsb'], writes=[bn])
                              P.op('pe', lambda e, outp=outp: e.matmul(outp, lhsT=ident[:, :], rhs=mneg[:, :, :].rearrange("p h i -> p (h i)"), start=False, stop=True),
                                   reads=['ident', 'mneg'], writes=[bn])
                      else:
                          for hl in range(8):
                              outp = gbk[hl // 4][0][0:n, (hl % 4) * 128:(hl % 4) * 128 + n]
                              bn = gbk[hl // 4][1]
                              P.op('pe', lambda e, outp=outp, n=n, hl=hl, g=g: e.matmul(outp, lhsT=onesf[0:n, 0:n], rhs=rhsb[0:n, g * 8 + hl, 0:n], start=True, stop=False),
                                   reads=['onesf', 'rhsb'], writes=[bn])
                              P.op('pe', lambda e, outp=outp, n=n: e.matmul(outp, lhsT=ident[0:n, 0:n], rhs=mneg[0:n, 0, 0:n], start=False, stop=True),
                                   reads=['ident', 'mneg'], writes=[bn])
                      P.op('pe', lambda e, pcb=pcb, g=g, off=off, n=n: e.matmul(pcb, lhsT=B_fm[:, g, off:off + n], rhs=C_fm[:, g, off:off + n], start=True, stop=True),
                           reads=['B_fm', 'C_fm'], writes=['pb2'])
                      P.op('pe', lambda e, g=g, off=off, n=n: e.matmul(yob[g][0][0:n, :], lhsT=C_fm[:, g, off:off + n], rhs=ST_b[:, g, :], start=True, stop=True),
                           reads=['C_fm', ('ST_b', g)], writes=[yob[g][1]])

              def gp2(g):
                  pcb = pb2[0:n, 256 + g * 128:256 + g * 128 + n]
                  P.op('dve', lambda e, g=g, n=n, ti=ti: e.tensor_tensor(out=h8(xw[0:n, :]), in0=h8(x_tm[0:n, ti, g * 512:(g + 1) * 512]),
                                                                         in1=w2[0:n, g * 8:(g + 1) * 8].unsqueeze(2).broadcast_to([n, 8, 64]), op=ALU.mult),
                       reads=[('x_tm', ti), 'w2'], writes=['xw'])
                  if not pref:
                      P.op('dve', lambda e, g=g, n=n: e.tensor_tensor(out=h8(tmpf[0:n, :]), in0=h8(yob[g][0][0:n, :]),
                                                                      in1=Eexp[0:n, g * 8:(g + 1) * 8].unsqueeze(2).broadcast_to([n, 8, 64]), op=ALU.mult),
                           reads=[yob[g][1], 'Eexp'], writes=['tmpf'])
                  psU, pnU = yob[g]
                  P.op('pe', lambda e, psU=psU, g=g, n=n, ti=ti: e.matmul(psU[:, :], lhsT=B_tm[0:n, ti, g * 128:(g + 1) * 128], rhs=xw[0:n, :], start=True, stop=True),
                       reads=[('B_tm', ti), 'xw'], writes=[pnU])
                  P.op('dve', lambda e, g=g: e.tensor_tensor(out=h8(ST_f[:, g, :]), in0=h8(ST_f[:, g, :]),
                                                             in1=cd[:, g * 8:(g + 1) * 8].unsqueeze(2).broadcast_to([128, 8, 64]), op=ALU.mult),
                       reads=[('ST_f', g), 'cd'], writes=[('ST_f', g)])
                  P.op('dve', lambda e, g=g, psU=psU: e.tensor_tensor(out=ST_f[:, g, :], in0=ST_f[:, g, :], in1=psU[:, :], op=ALU.add),
                       reads=[('ST_f', g), pnU], writes=[('ST_f', g)])
                  if not pref:
                      gbk = [((pb45[:, 0:512], 'pb4'), (pb45[:, 512:1024], 'pb5')), ((acc[0][:, :], 'acc0'), (acc[1][:, :], 'acc1'))][g]
                      for hl in range(8):
                          hh = g * 8 + hl
                          bk, bn = gbk[hl // 4]
                          P.op('act', lambda e, hl=hl, hh=hh, n=n, bk=bk: e.activation(out=LTs[g][0:n, hl, 0:n], in_=bk[0:n, (hl % 4) * 128:(hl % 4) * 128 + n], func=AF.Exp,
                                                                                      bias=nb[0:n, hh:hh + 1]), reads=[bn, 'nb'], writes=[('LT', g)])
                  P.op('act', lambda e, g=g: e.activation(out=ST_b[:, g, :], in_=ST_f[:, g, :], func=AF.Copy), reads=[('ST_f', g)], writes=[('ST_b', g)])

              def gp3(g):
                  pcb = pb2[0:n, 256 + g * 128:256 + g * 128 + n]
                  if not pref:
                      P.op('dve', lambda e, pcb=pcb, n=n: e.tensor_tensor(out=MTs[g][0:n, :, 0:n], in0=LTs[g][0:n, :, 0:n], in1=pcb.unsqueeze(1).broadcast_to([n, 8, n]),
                                                                          op=ALU.mult), reads=[('LT', g), 'pb2'], writes=[('MT', g)])
                      P.op('pe', lambda e, g=g, n=n: e.matmul(pb6[0:n, :], lhsT=ident[0:n, 0:n], rhs=xd[0:n, g * 512:(g + 1) * 512], start=True, stop=False),
                           reads=['ident', 'xd'], writes=['pb6'])
                      for hl in range(8):
                          hh = g * 8 + hl
                          P.op('pe', lambda e, hl=hl, hh=hh, n=n, ti=ti: e.matmul(pb6[0:n, hl * 64:(hl + 1) * 64], lhsT=MTs[g][0:n, hl, 0:n],
                                                                                  rhs=x_tm[0:n, ti, hh * 64:(hh + 1) * 64], start=False, stop=(hl == 7)),
                               reads=[('MT', g), ('x_tm', ti)], writes=['pb6'])
                      P.op('dve', lambda e, g=g, n=n: e.tensor_tensor(out=yy[0:n, g * 512:(g + 1) * 512], in0=pb6[0:n, :], in1=tmpf[0:n, :], op=ALU.add),
                           reads=['pb6', 'tmpf'], writes=['yy'])

              if pref:
                  gp2(0)
                  gp2(1)
              else:
                  gp1(0)
                  gp2(0)
                  gp1(1)
                  gp3(0)
                  gp2(1)
                  gp3(1)
              if pref:
                  continue
              if dbg and ti == 1:
                  tap('y_raw1', yy[:, :], [128, 1024], ['yy'])
              P.op('dve', lambda e, n=n, ti=ti: e.tensor_tensor(out=y2[0:n, :], in0=yy[0:n, :], in1=zs_tm[0:n, ti, :], op=ALU.mult),
                   reads=['yy', ('zs', ti)], writes=['y2'])
              for g in range(2):
                  P.op('act', lambda e, g=g, n=n: e.activation(out=junk[0:n, 0:512], in_=y2[0:n, g * 512:(g + 1) * 512], func=AF.Square,
                                                               accum_out=ss[0:n, g:g + 1]), reads=['y2'], writes=['junk', 'ss'])
              P.op('dve', lambda e, n=n: e.tensor_scalar(out=rs[0:n, 0:2], in0=ss[0:n, 0:2], scalar1=1.0 / 512, scalar2=EPS, op0=ALU.mult, op1=ALU.add),
                   reads=['ss'], writes=['rs'])
              P.op('pool', lambda e, n=n: e.tensor_tensor(out=rs[0:n, 0:2], in0=rs[0:n, 0:2], in1=nhalf[0:n, 0:2], op=ALU.pow), reads=['rs', 'nhalf'], writes=['rs'])
              for g in range(2):
                  P.op('dve', lambda e, g=g, n=n: e.scalar_tensor_tensor(out=mo[0:n, g * 512:(g + 1) * 512], in0=y2[0:n, g * 512:(g + 1) * 512],
                                                                         scalar=rs[0:n, g:g + 1], in1=m2nw_b[0:n, g * 512:(g + 1) * 512],
                                                                         op0=ALU.mult, op1=ALU.mult), reads=['y2', 'rs', 'm2nw_b'], writes=['mo'])
              for j in range(8):
                  P.op('pe', lambda e, j=j, n=n: e.transpose(out=pT[:, j, 0:n], in_=mo[0:n, j * 128:(j + 1) * 128], identity=ident[0:n, 0:n]),
                       reads=['mo', 'ident'], writes=['pT'])
              P.op('act', lambda e, off=off, n=n: e.activation(out=mixT[:, 8:16, off:off + n], in_=pT[:, :, 0:n], func=AF.Copy),
                   reads=['pT'], writes=[('mixT', 8 + j) for j in range(8)])
          if dbg:
              tap('mixB', mixT[:, 8:16, 0:T], [128, 8, T], [('mixT', 8 + j) for j in range(8)])
          stop_pt('mamba')
          if pref:
              return T

          A.reset(PH)
          for c2 in range(2):
              w0, n0 = use_piece(pc[('wo', c2, 0)])
              w1, n1 = use_piece(pc[('wo', c2, 1)], hold=True)
              for ti, (off, n) in enumerate(tiles):
                  ps, pn = next_acc()
                  for kk in range(16):
                      wt, wn = (w0, n0) if kk < 8 else (w1, n1)
                      P.op('pe', lambda e, ps=ps, kk=kk, off=off, n=n, wt=wt: e.matmul(ps[0:n, :], lhsT=mixT[:, kk, off:off + n], rhs=wt[:, kk % 8, :],
                                                                                        start=(kk == 0), stop=(kk == 15)),
                           reads=[('mixT', kk), wn], writes=[pn])
                  P.op('dve', lambda e, ps=ps, ti=ti, n=n, c2=c2: e.tensor_tensor(out=h[0:n, ti, c2 * 512:(c2 + 1) * 512], in0=h[0:n, ti, c2 * 512:(c2 + 1) * 512],
                                                                                  in1=ps[0:n, :], op=ALU.add), reads=[('h', ti), pn], writes=[('h', ti)])
          if dbg:
              tap('hmid1', h[:, 1, :], [128, 1024], [('h', 1)])
          norm_to_fm(R_N2, False)
          stop_pt('oproj')

          pre2 = [A.alloc(f"pre2{i}", [128, 2 + TM], BF16) for i in range(4)]
          dg2 = [A.alloc(f"dg2{i}", [128, 3, 128], BF16) for i in range(4)]
          gs = [A.alloc(f"gs{i}", [128, TM], BF16) for i in range(2)]
          aT = A.alloc("aT", [128, 22, TM], BF16)
          ob = [A.alloc(f"ob{i}", [128, D], F32) for i in range(2)]
          P.barrier()
          tasks = []
          for j in range(6):
              ncb = 4 if j < 5 else 2
              for cbl in range(ncb):
                  cbg = j * 4 + cbl
                  cbv = 22 + cbg
                  gi = cbg % 2
                  pg, pv = (2 * cbg) % 4, (2 * cbg + 1) % 4
                  tasks.append(lambda ph, j=j, cbl=cbl, cbg=cbg, gi=gi, pg=pg: conv_fm(
                      wsl[('ug', j)][0], wsl[('ug', j)][1], cbl, cbg, 3, pre2[pg], ('pre2', pg), dg2[pg], ('dg2', pg), hal2, 'hal2',
                      lambda k, cbg=cbg: R_FCW + k * 44 + cbg,
                      lambda cps, cpn, soff, sn, gi=gi, cbg=cbg: P.op(
                          'act', lambda e: e.activation(out=gs[gi][:, soff:soff + sn], in_=cps, func=AF.Silu, bias=pcol[:, R_FCB + cbg:R_FCB + cbg + 1]),
                          reads=cpn + ['pcol'], writes=[('gs', gi)]), phase=ph))
                  tasks.append(lambda ph, j=j, cbl=cbl, cbg=cbg, cbv=cbv, gi=gi, pv=pv: conv_fm(
                      wsl[('uv', j)][0], wsl[('uv', j)][1], cbl, cbv, 3, pre2[pv], ('pre2', pv), dg2[pv], ('dg2', pv), hal2, 'hal2',
                      lambda k, cbv=cbv: R_FCW + k * 44 + cbv,
                      lambda cps, cpn, soff, sn, gi=gi, cbg=cbg, cbv=cbv: P.op(
                          'dve', lambda e: e.scalar_tensor_tensor(out=aT[:, cbg, soff:soff + sn], in0=cps, scalar=pcol[:, R_FCB + cbv:R_FCB + cbv + 1],
                                                                  in1=gs[gi][:, soff:soff + sn], op0=ALU.add, op1=ALU.mult),
                          reads=cpn + ['pcol', ('gs', gi)], writes=[('aT', cbg)]), phase=ph))
          wsl = {}
          nt = len(tasks)
          for i in range(nt + 2):
              if i < nt and i % 8 == 0:
                  j = i // 8
                  wsl[('ug', j)] = use_piece(pc[('ug', j)])
                  wsl[('uv', j)] = use_piece(pc[('uv', j)], hold=True)
              if i < nt:
                  tasks[i]('A')
              if i - 2 >= 0:
                  tasks[i - 2]('B')
          stop_pt('ffn_up')

          dbanks = [(acc[0][:, :], 'acc0'), (acc[1][:, :], 'acc1'), (pb45[:, 0:512], 'pb4'), (pb45[:, 512:1024], 'pb5'), (pb6[:, :], 'pb6')]
          for c2 in range(2):
              for kp in range(3):
                  wt, wn = use_piece(pc[('wd', c2, kp)])
                  nk = 8 if kp < 2 else 6
                  for ti, (off, n) in enumerate(tiles):
                      ps, pn = dbanks[ti]
                      for k in range(nk):
                          kk = kp * 8 + k
                          P.op('pe', lambda e, ps=ps, kk=kk, k=k, off=off, n=n, wt=wt: e.matmul(ps[0:n, :], lhsT=aT[:, kk, off:off + n], rhs=wt[:, k, :],
                                                                                                 start=(kk == 0), stop=(kk == 21)),
                               reads=[('aT', kk), wn], writes=[pn])
              for ti, (off, n) in enumerate(tiles):
                  ps, pn = dbanks[ti]
                  P.op('dve', lambda e, ps=ps, ti=ti, n=n, c2=c2: e.tensor_tensor(out=h[0:n, ti, c2 * 512:(c2 + 1) * 512], in0=h[0:n, ti, c2 * 512:(c2 + 1) * 512],
                                                                                  in1=ps[0:n, :], op=ALU.add), reads=[('h', ti), pn], writes=[('h', ti)])
          for ti, (off, n) in enumerate(tiles):
              oi = ti % 2
              P.op('act', lambda e, ti=ti, n=n: e.activation(out=junk[0:n, :], in_=h[0:n, ti, :], func=AF.Square, accum_out=ssn[0:n, ti:ti + 1]),
                   reads=[('h', ti)], writes=['junk', ('ssn', ti)])
              P.op('act', lambda e, ti=ti, n=n: e.activation(out=rsn[0:n, ti:ti + 1], in_=ssn[0:n, ti:ti + 1], func=AF.Sqrt, scale=1.0 / D, bias=epsc[0:n, :]),
                   reads=[('ssn', ti), 'epsc'], writes=[('rsn', ti)])
              P.op('dve', lambda e, ti=ti, n=n: e.reciprocal(out=rsn[0:n, ti:ti + 1], in_=rsn[0:n, ti:ti + 1]), reads=[('rsn', ti)], writes=[('rsn', ti)])
              P.op('dve', lambda e, ti=ti, n=n, oi=oi: e.scalar_tensor_tensor(out=ob[oi][0:n, :], in0=h[0:n, ti, :], scalar=rsn[0:n, ti:ti + 1], in1=fnw_b[0:n, :],
                                                                              op0=ALU.mult, op1=ALU.mult), reads=[('h', ti), ('rsn', ti), 'fnw_b'], writes=[('ob', oi)])
              P.dma('sp', lambda e, oi=oi, off=off, n=n, out0=out0: e.dma_start(out=out[out0 + off:out0 + off + n, :], in_=ob[oi][0:n, :]),
                    reads=[('ob', oi)], chan=('o', oi))

          return T

      out0 = 0
      for sb_i in range((n_pre + n_sb) if stop_after != 'setup' else 0):
          T_ = do_sb(sb_i, tok0, out0)
          tok0 += T_
          if sb_modes[sb_i] == 'main':
              out0 += T_
          if sb_i == n_pre - 1:
              for tns, nm in [(S_f, 'S_f'), (S_b, 'S_b')]:
                  P.op('dve', lambda e, tns=tns: e.tensor_scalar(out=tns[:], in0=tns[:], scalar1=flagc[:, 0:1], scalar2=None, op0=ALU.mult),
                       reads=[(nm, i) for i in range(8)] + ['flagc'], writes=[(nm, i) for i in range(8)])
              for tns, nm in [(ST_f, 'ST_f'), (ST_b, 'ST_b')]:
                  P.op('dve', lambda e, tns=tns: e.tensor_scalar(out=tns[:], in0=tns[:], scalar1=flagc[:, 0:1], scalar2=None, op0=ALU.mult),
                       reads=[(nm, i) for i in range(2)] + ['flagc'], writes=[(nm, i) for i in range(2)])
              P.op('dve', lambda e: e.tensor_scalar(out=hal[:], in0=hal[:], scalar1=flagc[:, 0:1], scalar2=None, op0=ALU.mult),
                   reads=[('hal', i) for i in range(12)] + ['flagc'], writes=[('hal', i) for i in range(12)])
    except StopBuild:
        pass

    tap('pcol', pcol[:, :], [128, PROWS], ['pcol'])
    tap('oml', oml[:, :], [128, 8], ['oml'])
    P.wait_all('sp', [k for k in P.count if isinstance(k, tuple) and k[0] == 'd'])
    P.emit()
    return nc, tap_out


def pack_shared(inp):
    f = lambda a: np.ascontiguousarray(np.asarray(a, dtype=np.float32))
    rows = [f(inp['norm1_w'][0]).reshape(8, 128), f(inp['hg_lb_logits'][0]).reshape(8, 128), f(inp['hg_lb_logits'][1]).reshape(8, 128),
            f(inp['hg_norm_w'][0]).reshape(8, 128), f(inp['norm2_w'][0]).reshape(8, 128), f(inp['m2_conv_w'][0]).reshape(48, 128),
            f(inp['m2_conv_b'][0]).reshape(12, 128), f(inp['ffn_conv_w'][0]).reshape(132, 128), f(inp['ffn_conv_b'][0]).reshape(44, 128)]
    pvec = np.ascontiguousarray(np.concatenate(rows, 0))
    assert pvec.shape == (PROWS, 128)
    return {
        'w_in': f(inp['w_in'][0]), 'w_out': f(inp['w_out'][0]), 'w_up': f(inp['ffn_w_up'][0]), 'w_down': f(inp['ffn_w_down'][0]),
        'pvec': pvec, 'm2nw': f(inp['m2_norm_w'][0]), 'fnw': f(inp['final_norm_w']), 'dtb': f(inp['m2_dt_bias'][0]),
        'alog': f(inp['m2_a_log'][0]), 'dsk': f(inp['m2_d'][0]),
    }


def core_tokens(inp, b, TOK):
    seq = np.concatenate([np.asarray(inp['meta_tokens'], np.float32), np.asarray(inp['x'][b], np.float32)], 0)
    return np.ascontiguousarray(seq[:TOK])


_CACHE = {}
N_PRE, N_SB = 4, 4


def kernel(**inputs):
    TOKP, TOKM = 512 * N_PRE, 16 + 512 * N_SB
    if 'nc' not in _CACHE:
        _CACHE['nc'] = build_program(N_SB, n_pre=N_PRE)[0]
    nc = _CACHE['nc']
    shared = pack_shared(inputs)
    meta = np.asarray(inputs['meta_tokens'], np.float32)
    in_maps = []
    for c in range(8):
        b, half = c // 2, c % 2
        seq = np.concatenate([meta, np.asarray(inputs['x'][b], np.float32)], 0)
        m = dict(shared)
        if half == 0:
            m['xin'] = np.ascontiguousarray(np.concatenate([np.zeros((TOKP, D), np.float32), seq[0:TOKM]], 0))
            m['flag'] = np.zeros((128, 1), np.float32)
        else:
            m['xin'] = np.ascontiguousarray(seq[0:TOKP + TOKM])
            m['flag'] = np.ones((128, 1), np.float32)
        in_maps.append(m)
    res = run_bass_kernel_spmd(nc, in_maps, core_ids=list(range(8)))
    outs = []
    for b in range(4):
        outs.append(np.concatenate([res.results[2 * b]['out'][NMETA:], res.results[2 * b + 1]['out'][NMETA:]], 0))
    return np.stack(outs, 0).astype(np.float32)
```

```python
import numpy as np
import concourse.bass as bass
import concourse.mybir as mybir
from concourse.bass_utils import run_bass_kernel_spmd

F32 = mybir.dt.float32
BF16 = mybir.dt.bfloat16
ALU = mybir.AluOpType
AF = mybir.ActivationFunctionType

ENGS = ['pe', 'dve', 'act', 'pool', 'sp']
SAME_ENG_RAW = ('dve', 'act', 'pool')
SAME_ENG_ALL = True
EPS = 1e-6
NMETA = 16
D = 1024
DPROJ = 6672
DFF = 2816


class Prog:
    def __init__(self, nc):
        self.nc = nc
        self.streams = {e: [] for e in ENGS}
        self.sems = {}
        self.count = {}
        self.seen = {e: {} for e in ENGS}
        self.buf = {}

    def _sem(self, key):
        if key not in self.sems:
            name = "s_" + "_".join(str(k) for k in (key if isinstance(key, tuple) else (key,)))
            name = name.replace("(", "").replace(")", "").replace(",", "_").replace(" ", "").replace("'", "")
            self.sems[key] = self.nc.alloc_semaphore(name)
            self.count[key] = 0
        return self.sems[key]

    def _need(self, eng, reads, writes):
        need = {}

        def add(kv, raw):
            if kv is None:
                return
            k, v = kv
            if k == eng and not ((raw or SAME_ENG_ALL) and eng in SAME_ENG_RAW):
                return
            if v > need.get(k, 0):
                need[k] = v

        for b in reads:
            st = self.buf.get(b)
            if st:
                add(st[0], True)
        for b in writes:
            st = self.buf.get(b)
            if st:
                add(st[0], False)
                for k, v in st[1].items():
                    add((k, v), False)
        seen = self.seen[eng]
        for k, v in need.items():
            if v > seen.get(k, 0):
                self.streams[eng].append(('wait', k, v))
                seen[k] = v

    def _mark(self, key, val, reads, writes):
        for b in reads:
            st = self.buf.setdefault(b, [None, {}])
            st[1][key] = val
        for b in writes:
            self.buf[b] = [(key, val), {}]

    def op(self, eng, fn, reads=(), writes=(), inc=True):
        self._need(eng, reads, writes)
        self._sem(eng)
        if inc:
            self.count[eng] += 1
            self.streams[eng].append(('op', fn, eng, 1))
            self._mark(eng, self.count[eng], reads, writes)
        else:
            assert eng == 'pe'
            self.streams[eng].append(('op', fn, eng, 0))
            self._mark(eng, self.count[eng] + 1, reads, writes)

    def dma(self, q, fn, reads=(), writes=(), chan=None):
        self._need(q, reads, writes)
        key = ('d', chan)
        self._sem(key)
        self.count[key] += 16
        self.streams[q].append(('op', fn, key, 16))
        self._mark(key, self.count[key], reads, writes)

    def barrier(self):
        for e in ENGS:
            for k, v in self.count.items():
                if k == e:
                    continue
                if isinstance(k, tuple) and k[0] == 'd' and isinstance(k[1], tuple) and k[1][0] in ('w', 'x', 'setup'):
                    continue
                if v > self.seen[e].get(k, 0):
                    self.streams[e].append(('wait', k, v))
                    self.seen[e][k] = v

    def wait_all(self, eng, keys):
        for k in keys:
            v = self.count.get(k, 0)
            if v > self.seen[eng].get(k, 0):
                self.streams[eng].append(('wait', k, v))
                self.seen[eng][k] = v

    def emit(self):
        nc = self.nc
        engmap = {'pe': 'tensor', 'dve': 'vector', 'act': 'scalar', 'pool': 'gpsimd', 'sp': 'sync'}
        with nc.Block() as block:
            for e in ENGS:
                stream = self.streams[e]
                if not stream:
                    continue

                def body(engine, stream=stream):
                    for item in stream:
                        if item[0] == 'wait':
                            engine.wait_ge(self.sems[item[1]], item[2])
                        else:
                            ins = item[1](engine)
                            if item[3]:
                                ins.then_inc(self.sems[item[2]], item[3])

                getattr(block, engmap[e])(body)


class StopBuild(Exception):
    pass


class Arena:
    def __init__(self, nc):
        self.nc = nc
        self.base = (int(nc.sbuf_base) + 63) // 64 * 64
        self.top = int(nc.sbuf_top)
        self.cur = self.base
        self.n = 0
        self.hi = self.base

    def alloc(self, name, shape, dtype):
        esz = 4 if dtype == F32 else 2
        nbytes = esz
        for s in shape[1:]:
            nbytes *= s
        nbytes = (nbytes + 63) // 64 * 64
        off = self.cur
        self.cur += nbytes
        self.hi = max(self.hi, self.cur)
        assert self.cur <= self.top, f"SBUF overflow at {name}: {self.cur} > {self.top}"
        self.n += 1
        return self.nc.alloc_sbuf_tensor_at(f"{name}_{self.n}", list(shape), dtype, offset=off)

    def mark(self):
        return self.cur

    def reset(self, m):
        self.cur = m


def sb_layout(first):
    if first:
        tiles = [(0, 16)] + [(16 + 128 * j, 128) for j in range(4)]
        chunks = [(0, 16)] + [(16 + 64 * j, 64) for j in range(8)]
        segs = [(0, 16), (16, 512)]
        T = 528
    else:
        tiles = [(128 * j, 128) for j in range(4)]
        chunks = [(64 * j, 64) for j in range(8)]
        segs = [(0, 512)]
        T = 512
    return T, tiles, chunks, segs


R_N1, R_L0, R_L1, R_HGN, R_N2, R_MCW, R_MCB, R_FCW, R_FCB, PROWS = 0, 8, 16, 24, 32, 40, 88, 100, 232, 276


def build_program(n_sb, taps=None, stop_after=None, n_pre=0):
    taps = taps or []
    TOKP = 512 * n_pre
    TOKM = 16 + 512 * n_sb
    TOK = TOKP + TOKM
    nc = bass.Bass("TRN2", target_bir_lowering=False)
    dr = lambda name, shape, kind="ExternalInput": nc.dram_tensor(name, shape, F32, kind=kind).ap()
    xin = dr("xin", [TOK, D])
    w_in = dr("w_in", [D, DPROJ])
    w_out = dr("w_out", [2 * D, D])
    w_up = dr("w_up", [D, 2 * DFF])
    w_down = dr("w_down", [DFF, D])
    pvec = dr("pvec", [PROWS, 128])
    m2nw_d = dr("m2nw", [D])
    fnw_d = dr("fnw", [D])
    dtb_d = dr("dtb", [16])
    alog_d = dr("alog", [16])
    dsk_d = dr("dsk", [16])
    flag_d = dr("flag", [128, 1])
    out = dr("out", [TOKM, D], kind="ExternalOutput")
    tap_out = {}

    P = Prog(nc)
    A = Arena(nc)
    TM = 528

    acc = [nc.alloc_psum_tensor(f"acc{i}", [128, 512], F32) for i in range(2)]
    pb2 = nc.alloc_psum_tensor("pb2", [128, 512], F32)
    pT = nc.alloc_psum_tensor("pT", [128, 8, 128], BF16)
    pb45 = nc.alloc_psum_tensor("pb45", [128, 1024], F32)
    pb6 = nc.alloc_psum_tensor("pb6", [128, 512], F32)
    pb7 = nc.alloc_psum_tensor("pb7", [128, 512], F32)
    PB4 = ['pb4']
    PB5 = ['pb5']
    PB6 = ['pb6']
    PB7 = ['pb7']
    pT7 = pb7[:, :].bitcast(BF16).rearrange("p (k c) -> p k c", k=8)
    pTs = [(pT, 'pT'), (pT7, 'pb7')]
    pT2v = pb2[:, :].bitcast(BF16).rearrange("p (k c) -> p k c", k=8)
    pTm = [(pT, 'pT'), (pT2v, 'pb2')]
    pTf = pT[:, :, :].rearrange("p k c -> p (k c)").bitcast(F32)
    yob = [(pb7[:, :], 'pb7'), (pTf, 'pT')]
    st = {'acc': 0, 'mini': 0, 'cv': 0}

    def next_acc():
        i = st['acc'] % 2
        st['acc'] += 1
        return acc[i], f'acc{i}'

    def next_mini():
        i = st['mini'] % 16
        st['mini'] += 1
        return pb2[:, i * 16:(i + 1) * 16], 'pb2'

    cvbanks = [(pb45[:, 0:512], PB4), (pb45[:, 512:1024], PB5), (pb6[:, :], PB6), (pb7[:, :], PB7)]

    def next_cv():
        i = st['cv'] % 4
        st['cv'] += 1
        return cvbanks[i]

    h = A.alloc("h", [128, 5, D], F32)
    S_f = A.alloc("S_f", [128, 8, 128], F32)
    S_b = A.alloc("S_b", [128, 8, 128], BF16)
    ST_f = A.alloc("ST_f", [128, 2, 512], F32)
    ST_b = A.alloc("ST_b", [128, 2, 512], BF16)
    hal = A.alloc("hal", [128, 12, 4], BF16)
    hal2 = A.alloc("hal2", [128, 44, 2], BF16)
    identf = A.alloc("identf", [128, 128], F32)
    ident = A.alloc("ident", [128, 128], BF16)
    onesf = A.alloc("onesf", [128, 128], F32)
    onesb = A.alloc("onesb", [128, 128], BF16)
    triU = A.alloc("triU", [128, 128], F32)
    mneg = A.alloc("mneg", [128, 4, 128], BF16)
    rmask0 = A.alloc("rmask0", [128, 528], F32)
    rmask1 = A.alloc("rmask1", [128, 512], F32)
    pcol = A.alloc("pcol", [128, PROWS], F32)
    oml = A.alloc("oml", [128, 8], F32)
    homl = A.alloc("homl", [128, 8], F32)
    nhoml = A.alloc("nhoml", [128, 8], F32)
    dl = A.alloc("dl", [128, 8], F32)
    m2nw_b = A.alloc("m2nw_b", [128, D], F32)
    fnw_b = A.alloc("fnw_b", [128, D], F32)
    dtb_b = A.alloc("dtb_b", [128, 16], F32)
    aneg_b = A.alloc("aneg_b", [128, 16], F32)
    dsk_b = A.alloc("dsk_b", [128, 16], F32)
    epsc = A.alloc("epsc", [128, 1], F32)
    onec = A.alloc("onec", [128, 1], F32)
    nhalf = A.alloc("nhalf", [128, 2], F32)
    flagc = A.alloc("flagc", [128, 1], F32)
    ptmp = A.alloc("ptmp", [128, 128], F32)
    NSLOT = 4
    ws = [A.alloc(f"ws{i}", [128, 8, 512], BF16) for i in range(NSLOT)]
    uT = A.alloc("uT", [128, 8, TM], BF16)
    mixT = A.alloc("mixT", [128, 16, TM], BF16)
    junk = A.alloc("junk", [128, D], BF16)
    xn = A.alloc("xn", [128, D], BF16)
    xn2 = [xn, A.alloc("xnb", [128, D], BF16)]
    ssn = A.alloc("ssn", [128, 8], F32)
    rsn = A.alloc("rsn", [128, 8], F32)
    ss = A.alloc("ss", [128, 4], F32)
    rs = A.alloc("rs", [128, 4], F32)
    PH = A.mark()

    P.op('pool', lambda e: e.memset(identf[:], 1.0), writes=['identf'])
    P.op('pool', lambda e: e.affine_select(out=identf[:], in_=identf[:], pattern=[[-1, 128]], compare_op=ALU.is_equal,
                                           fill=0.0, base=0, channel_multiplier=1), reads=['identf'], writes=['identf'])
    P.op('pool', lambda e: e.tensor_copy(out=ident[:], in_=identf[:]), reads=['identf'], writes=['ident'])
    P.op('pool', lambda e: e.memset(onesf[:], 1.0), writes=['onesf'])
    P.op('pool', lambda e: e.memset(onesb[:], 1.0), writes=['onesb'])
    P.op('pool', lambda e: e.memset(triU[:], 1.0), writes=['triU'])
    P.op('pool', lambda e: e.affine_select(out=triU[:], in_=triU[:], pattern=[[1, 128]], compare_op=ALU.is_ge,
                                           fill=0.0, base=0, channel_multiplier=-1), reads=['triU'], writes=['triU'])
    P.op('pool', lambda e: e.memset(mneg[:], 0.0), writes=['mneg'])
    P.op('pool', lambda e: e.affine_select(out=mneg[:], in_=mneg[:], pattern=[[0, 4], [1, 128]], compare_op=ALU.is_ge,
                                           fill=-30000.0, base=0, channel_multiplier=-1), reads=['mneg'], writes=['mneg'])
    P.op('pool', lambda e: e.memset(rmask0[:], 1.0), writes=['rmask0'])
    P.op('pool', lambda e: e.memset(rmask0[:, 0:1], 0.0), writes=['rmask0'])
    P.op('pool', lambda e: e.memset(rmask0[:, 16:528:64], 0.0), writes=['rmask0'])
    P.op('pool', lambda e: e.memset(rmask1[:], 1.0), writes=['rmask1'])
    P.op('pool', lambda e: e.memset(rmask1[:, 0:512:64], 0.0), writes=['rmask1'])
    P.op('pool', lambda e: e.memset(epsc[:], EPS), writes=['epsc'])
    P.op('pool', lambda e: e.memset(onec[:], 1.0), writes=['onec'])
    P.op('pool', lambda e: e.memset(nhalf[:], -0.5), writes=['nhalf'])
    P.op('pool', lambda e: e.memset(S_f[:], 0.0), writes=['S_f'])
    P.op('pool', lambda e: e.memset(S_b[:], 0.0), writes=['S_b'])
    P.op('pool', lambda e: e.memset(ST_f[:], 0.0), writes=['ST_f'])
    P.op('pool', lambda e: e.memset(ST_b[:], 0.0), writes=['ST_b'])
    P.op('pool', lambda e: e.memset(hal[:], 0.0), writes=[('hal', i) for i in range(12)])
    P.op('pool', lambda e: e.memset(hal2[:], 0.0), writes=[('hal2', i) for i in range(44)])
    for r0 in range(0, PROWS, 128):
        nr = min(128, PROWS - r0)
        P.dma('sp', lambda e, r0=r0, nr=nr: e.dma_start(out=ptmp[0:nr, :], in_=pvec[r0:r0 + nr, :]), writes=['ptmp'], chan=('setup', r0))
        P.op('pe', lambda e, nr=nr: e.transpose(out=acc[0][:, 0:nr], in_=ptmp[0:nr, :], identity=identf[0:nr, 0:nr]),
             reads=['ptmp', 'identf'], writes=['acc0'])
        P.op('dve', lambda e, r0=r0, nr=nr: e.tensor_copy(out=pcol[:, r0:r0 + nr], in_=acc[0][:, 0:nr]), reads=['acc0'], writes=['pcol'])
    P.op('dve', lambda e: e.tensor_tensor(out=dl[:], in0=pcol[:, R_L0:R_L0 + 8], in1=pcol[:, R_L1:R_L1 + 8], op=ALU.subtract),
         reads=['pcol'], writes=['dl'])
    P.op('act', lambda e: e.activation(out=oml[:], in_=dl[:], func=AF.Sigmoid, scale=-1.0), reads=['dl'], writes=['oml'])
    P.op('dve', lambda e: e.tensor_scalar(out=homl[:], in0=oml[:], scalar1=0.5, scalar2=None, op0=ALU.mult), reads=['oml'], writes=['oml'])
    P.op('dve', lambda e: e.tensor_scalar(out=nhoml[:], in0=oml[:], scalar1=-0.5, scalar2=None, op0=ALU.mult), reads=['oml'], writes=['oml'])
    for dst, src, nm in [(m2nw_b, m2nw_d, 'm2nw_b'), (fnw_b, fnw_d, 'fnw_b'), (dtb_b, dtb_d, 'dtb_b'), (aneg_b, alog_d, 'aneg_b'),
                         (dsk_b, dsk_d, 'dsk_b')]:
        P.dma('sp', lambda e, dst=dst, src=src: e.dma_start(out=dst[:], in_=src.partition_broadcast(128)), writes=[nm], chan=('setup', nm))
    P.dma('sp', lambda e: e.dma_start(out=flagc[:], in_=flag_d[:, :]), writes=['flagc'], chan=('setup', 'flag'))
    P.op('act', lambda e: e.activation(out=aneg_b[:], in_=aneg_b[:], func=AF.Exp), reads=['aneg_b'], writes=['aneg_b'])
    P.op('dve', lambda e: e.tensor_scalar(out=aneg_b[:], in0=aneg_b[:], scalar1=-1.0, scalar2=None, op0=ALU.mult),
         reads=['aneg_b'], writes=['aneg_b'])

    pieces = []

    def add_piece(wd, r0, nk, c0, ncols):
        pieces.append((wd, r0, nk, c0, ncols))
        return len(pieces) - 1

    wstate = {'issued': 0}
    live = []

    def issue_to(idx):
        while wstate['issued'] <= min(idx, len(pieces) - 1):
            i = wstate['issued']
            wd, r0, nk, c0, ncols = pieces[i]
            sl = i % NSLOT
            src = wd[r0:r0 + nk * 128, c0:c0 + ncols].rearrange("(k p) n -> p k n", p=128)
            P.dma('pool', lambda e, sl=sl, src=src, nk=nk, ncols=ncols: e.dma_start(out=ws[sl][:, 0:nk, 0:ncols], in_=src),
                  writes=[('ws', sl)], chan=('w', sl))
            wstate['issued'] += 1

    def use_piece(idx, hold=False):
        if not hold:
            for j in live:
                issue_to(j + NSLOT)
            del live[:]
        issue_to(idx)
        live.append(idx)
        sl = idx % NSLOT
        return ws[sl], ('ws', sl)

    sched = []
    sb_modes = ['prefix'] * n_pre + ['main'] * n_sb
    for sb in range(n_pre + n_sb):
        d_ = {}
        pref = sb_modes[sb] == 'prefix'
        for hg in range(2):
            d_[('f', hg)] = add_piece(w_in, 0, 8, 1024 + hg * 512, 512)
            if not pref:
                d_[('q', hg)] = add_piece(w_in, 0, 8, hg * 512, 512)
            d_[('i', hg)] = add_piece(w_in, 0, 8, 2048 + hg * 512, 512)
            if not pref:
                d_[('g', hg)] = add_piece(w_in, 0, 8, 3072 + hg * 512, 512)
        for j in range(3):
            d_[('xbc', j)] = add_piece(w_in, 0, 8, 5120 + j * 512, 512 if not (pref and j == 2) else 256)
        if not pref:
            for j in range(2):
                d_[('z', j)] = add_piece(w_in, 0, 8, 4096 + j * 512, 512)
        d_['dt'] = add_piece(w_in, 0, 8, 6656, 16)
        if pref:
            sched.append(d_)
            continue
        for c2 in range(2):
            for kh in range(2):
                d_[('wo', c2, kh)] = add_piece(w_out, kh * 1024, 8, c2 * 512, 512)
        for j in range(6):
            ncol = 512 if j < 5 else 256
            d_[('ug', j)] = add_piece(w_up, 0, 8, j * 512, ncol)
            d_[('uv', j)] = add_piece(w_up, 0, 8, DFF + j * 512, ncol)
        for c2 in range(2):
            for kp in range(3):
                nk = 8 if kp < 2 else 6
                d_[('wd', c2, kp)] = add_piece(w_down, kp * 1024, nk, c2 * 512, 512)
        sched.append(d_)

    def tap(name, ap, shape, reads):
        if name in taps:
            t = nc.dram_tensor("tap_" + name, list(shape), ap.dtype if hasattr(ap, 'dtype') else F32, kind="ExternalOutput").ap()
            tap_out[name] = t
            P.dma('sp', lambda e: e.dma_start(out=t, in_=ap), reads=reads, chan=('tap', name))

    def stop_pt(name):
        if stop_after == name:
            raise StopBuild()

    issue_to(NSLOT - 1)
    tok0 = 0
    try:
      def do_sb(sb, tok0, out0):
          pref = sb_modes[sb] == 'prefix'
          first = (sb == n_pre)
          T, tiles, chunks, segs = sb_layout(first)
          NCH = len(chunks)
          rmask = rmask0 if first else rmask1
          rmask_n = 'rmask0' if first else 'rmask1'
          pc = sched[sb]
          dbg = (sb == n_pre)

          def mm_fm(wt, wname, cb, rhs_t, rhs_names, evac):
              for (soff, sn) in segs:
                  if sn <= 16:
                      ps, pn = next_mini()
                      pn = [pn]
                  else:
                      ps, pn = next_acc()
                      ps = ps[:, 0:sn]
                      pn = [pn]
                  for k in range(8):
                      P.op('pe', lambda e, ps=ps, k=k, cb=cb, soff=soff, sn=sn: e.matmul(
                          ps, lhsT=wt[:, k, cb * 128:(cb + 1) * 128], rhs=rhs_t[:, k, soff:soff + sn], start=(k == 0), stop=(k == 7)),
                          reads=[wname] + rhs_names, writes=pn, inc=(k == 7))
                  evac(ps, pn, soff, sn)

          def norm_to_fm(r_w, with_load):
              def part_a(ti, off, n):
                  q = ti % 2
                  if with_load:
                      P.dma('sp', lambda e: e.dma_start(out=h[0:n, ti, :], in_=xin[tok0 + off:tok0 + off + n, :]),
                            writes=[('h', ti)], chan=('x', ti))
                  P.op('act', lambda e: e.activation(out=junk[0:n, :], in_=h[0:n, ti, :], func=AF.Square, accum_out=ssn[0:n, ti:ti + 1]),
                       reads=[('h', ti)], writes=['junk', ('ssn', ti)])
                  P.op('act', lambda e: e.activation(out=rsn[0:n, ti:ti + 1], in_=ssn[0:n, ti:ti + 1], func=AF.Sqrt, scale=1.0 / D, bias=epsc[0:n, :]),
                       reads=[('ssn', ti), 'epsc'], writes=[('rsn', ti)])
                  P.op('dve', lambda e: e.reciprocal(out=rsn[0:n, ti:ti + 1], in_=rsn[0:n, ti:ti + 1]), reads=[('rsn', ti)], writes=[('rsn', ti)])
                  P.op('dve', lambda e: e.tensor_scalar(out=xn2[q][0:n, :], in0=h[0:n, ti, :], scalar1=rsn[0:n, ti:ti + 1], scalar2=None,
                                                        op0=ALU.mult), reads=[('h', ti), ('rsn', ti)], writes=[('xn', q)])

              def part_b(ti, off, n):
                  q = ti % 2
                  tb, tbn = pTs[q]
                  for k in range(8):
                      P.op('pe', lambda e, k=k: e.transpose(out=tb[:, k, 0:n], in_=xn2[q][0:n, k * 128:(k + 1) * 128], identity=ident[0:n, 0:n]),
                           reads=[('xn', q), 'ident'], writes=[tbn], inc=(k == 7))
                  P.op('dve', lambda e: e.tensor_tensor(out=uT[:, :, off:off + n], in0=tb[:, :, 0:n],
                                                        in1=pcol[:, r_w:r_w + 8].unsqueeze(2).broadcast_to([128, 8, n]), op=ALU.mult),
                       reads=[tbn, 'pcol'], writes=['uT'])

              nt = len(tiles)
              for i in range(nt + 1):
                  if i < nt:
                      part_a(i, *tiles[i])
                  if i >= 1:
                      part_b(i - 1, *tiles[i - 1])

          A.reset(PH)
          norm_to_fm(R_N1, True)
          if dbg:
              tap('uT', uT[:, :, 0:T], [128, 8, T], ['uT'])
          stop_pt('s0')

          A.reset(PH)
          kinv = A.alloc("kinv", [128, 4, TM], BF16)
          kend = A.alloc("kend", [128, 4, TM], BF16)
          qdec = A.alloc("qdec", [128, 4, TM], BF16)
          sgf = A.alloc("sgf", [128, 4, TM], BF16)
          epos = A.alloc("epos", [128, 4, TM], F32)
          fa = A.alloc("fa", [128, 4, TM], F32)
          fb = A.alloc("fb", [128, 4, TM], F32)
          fc = A.alloc("fc", [128, 4, TM], F32)
          t1 = A.alloc("t1", [128, TM], F32)
          t3 = A.alloc("t3", [128, TM], F32)
          dch = A.alloc("dch", [128, 4, 16], F32)
          v_tm = A.alloc("v_tm", [64, 9, 512], BF16)
          ke_tm = A.alloc("ke_tm", [64, 9, 512], BF16)
          scT = [A.alloc(f"scT{i}", [64, 4, 64], BF16) for i in range(2)]
          o_sb = A.alloc("o_sb", [128, 4, TM], F32)
          osq4 = A.alloc("osq4", [128, 4, TM], BF16)
          rst4 = A.alloc("rst4", [128, 4, TM], F32)
          P.barrier()

          for hg in range(2):
              wt, wn = use_piece(pc[('f', hg)])
              for hl in range(4):
                  mm_fm(wt, wn, hl, uT, ['uT'], lambda ps, pn, soff, sn, hl=hl: P.op(
                      'act', lambda e: e.activation(out=fa[:, hl, soff:soff + sn], in_=ps, func=AF.Tanh, scale=0.5), reads=pn, writes=[('fa', hl)]))
              for hl in range(4):
                  hd = hg * 4 + hl
                  P.op('dve', lambda e, hl=hl, hd=hd: e.tensor_scalar(out=fa[:, hl, 0:T], in0=fa[:, hl, 0:T], scalar1=nhoml[:, hd:hd + 1],
                                                                     scalar2=homl[:, hd:hd + 1], op0=ALU.mult, op1=ALU.add),
                       reads=[('fa', hl), 'oml'], writes=[('fa', hl)])
              for hl in range(4):
                  P.op('act', lambda e, hl=hl: e.activation(out=fb[:, hl, 0:T], in_=fa[:, hl, 0:T], func=AF.Ln, scale=-1.0, bias=onec[:, :]),
                       reads=[('fa', hl), 'onec'], writes=[('fb', hl)])
              for hl in range(4):
                  P.op('dve', lambda e, hl=hl: e.tensor_tensor_scan(out=fc[:, hl, 0:T], data0=rmask[:, 0:T], data1=fb[:, hl, 0:T], initial=0.0,
                                                                    op0=ALU.mult, op1=ALU.add), reads=[('fb', hl), rmask_n], writes=[('fc', hl)])
              for hl in range(4):
                  P.op('act', lambda e, hl=hl: e.activation(out=fb[:, hl, 0:T], in_=fc[:, hl, 0:T], func=AF.Exp, scale=-1.0),
                       reads=[('fc', hl)], writes=[('fb', hl)])
                  P.op('act', lambda e, hl=hl: e.activation(out=epos[:, hl, 0:T], in_=fc[:, hl, 0:T], func=AF.Exp), reads=[('fc', hl)], writes=[('epos', hl)])
              ce0 = chunks[0][0] + chunks[0][1] - 1
              for hl in range(4):
                  P.op('dve', lambda e, hl=hl: e.tensor_tensor(out=kinv[:, hl, 0:T], in0=fa[:, hl, 0:T], in1=fb[:, hl, 0:T], op=ALU.mult),
                       reads=[('fa', hl), ('fb', hl)], writes=[('kinv', hl)])
                  P.op('dve', lambda e, hl=hl: e.tensor_copy(out=dch[:, hl, 0:NCH], in_=epos[:, hl, ce0:T:64]),
                       reads=[('epos', hl)], writes=[('dch', hl)])
                  cs = 0
                  if first:
                      P.op('dve', lambda e, hl=hl: e.tensor_scalar(out=kend[:, hl, 0:16], in0=kinv[:, hl, 0:16], scalar1=dch[:, hl, 0:1], scalar2=None,
                                                                   op0=ALU.mult), reads=[('kinv', hl), ('dch', hl)], writes=[('kend', hl)])
                      cs = 1
                  t0_ = chunks[cs][0]
                  P.op('dve', lambda e, hl=hl, cs=cs, t0_=t0_: e.tensor_tensor(
                      out=kend[:, hl, t0_:T].rearrange("p (c j) -> p c j", j=64), in0=kinv[:, hl, t0_:T].rearrange("p (c j) -> p c j", j=64),
                      in1=dch[:, hl, cs:cs + 8].unsqueeze(2).broadcast_to([128, 8, 64]), op=ALU.mult),
                      reads=[('kinv', hl), ('dch', hl)], writes=[('kend', hl)])
              stop_pt('h_f')
              if not pref:
                wt, wn = use_piece(pc[('q', hg)])
              for hl in range(4 if not pref else 0):
                  mm_fm(wt, wn, hl, uT, ['uT'], lambda ps, pn, soff, sn, hl=hl: P.op(
                      'act', lambda e: e.activation(out=(t1, t3)[hl % 2][:, soff:soff + sn], in_=ps, func=AF.Silu), reads=pn, writes=[('tq', hl % 2)]))
                  P.op('dve', lambda e, hl=hl: e.tensor_tensor(out=qdec[:, hl, 0:T], in0=(t1, t3)[hl % 2][:, 0:T], in1=epos[:, hl, 0:T], op=ALU.mult),
                       reads=[('tq', hl % 2), ('epos', hl)], writes=[('qdec', hl)])
              stop_pt('h_q')
              wt, wn = use_piece(pc[('i', hg)])
              for c, (c0, cn) in enumerate(chunks):
                  ps, pn = next_acc()
                  for k in range(8):
                      P.op('pe', lambda e, ps=ps, k=k, c0=c0, cn=cn, wt=wt: e.matmul(ps[0:cn, :], lhsT=uT[:, k, c0:c0 + cn], rhs=wt[:, k, :],
                                                                                      start=(k == 0), stop=(k == 7)),
                           reads=[wn, 'uT'], writes=[pn], inc=(k == 7))
                  P.op('act', lambda e, ps=ps, c=c, cn=cn: e.activation(out=v_tm[0:cn, c, :], in_=ps[0:cn, :], func=AF.Copy),
                       reads=[pn], writes=[('v_tm', c)])
              stop_pt('h_i')
              if not pref:
                wt, wn = use_piece(pc[('g', hg)])
              for hl in range(4 if not pref else 0):
                  mm_fm(wt, wn, hl, uT, ['uT'], lambda ps, pn, soff, sn, hl=hl: P.op(
                      'act', lambda e: e.activation(out=sgf[:, hl, soff:soff + sn], in_=ps, func=AF.Silu), reads=pn, writes=[('sgf', hl)]))
              stop_pt('h_g')
              for c, (c0, cn) in enumerate(chunks):
                  tb, tbn = pTs[c % 2]
                  for hl in range(4):
                      P.op('pe', lambda e, hl=hl, c0=c0, cn=cn, tb=tb: e.transpose(out=tb[0:cn, hl, :], in_=kend[:, hl, c0:c0 + cn], identity=ident[:, :]),
                           reads=[('kend', hl), 'ident'], writes=[tbn])
                  P.op('dve', lambda e, c=c, cn=cn, tb=tb: e.tensor_copy(out=ke_tm[0:cn, c, :].rearrange("p (h k) -> p h k", h=4), in_=tb[0:cn, 0:4, :]),
                       reads=[tbn], writes=[('ke_tm', c)])
              stop_pt('h_t')
              for c, (c0, cn) in enumerate(chunks):
                  par = c % 2
                  sbk, sbn = [(pb45[:, 0:512], 'pb4'), (acc[0][:, :], 'acc0')][par]
                  obk, obn = [(pb45[:, 512:1024], 'pb5'), (acc[1][:, :], 'acc1')][par]
                  psS = sbk[0:cn, 0:256].rearrange("p (h r) -> p h r", h=4)[:, :, 0:cn]
                  psO = obk[:, 0:256].rearrange("p (h r) -> p h r", h=4)[:, :, 0:cn]
                  for hl in range(4 if not pref else 0):
                      P.op('pe', lambda e, psS=psS, hl=hl, c0=c0, cn=cn: e.matmul(psS[:, hl, :], lhsT=kinv[:, hl, c0:c0 + cn], rhs=qdec[:, hl, c0:c0 + cn],
                                                                                  start=True, stop=True),
                           reads=[('kinv', hl), ('qdec', hl)], writes=[sbn])
                  if not pref:
                   P.op('dve', lambda e, psS=psS, par=par, cn=cn: e.tensor_tensor(
                      out=scT[par][0:cn, :, 0:cn], in0=psS, in1=triU[0:cn, 0:cn].unsqueeze(1).broadcast_to([cn, 4, cn]), op=ALU.mult),
                      reads=[sbn, 'triU'], writes=[('scT', par)])
                  ubk, ubn = [(pb6, 'pb6'), (pb7, 'pb7')][c % 2]
                  for hl in range(4):
                      P.op('pe', lambda e, hl=hl, c=c, cn=cn, ubk=ubk: e.matmul(ubk[:, hl * 128:(hl + 1) * 128], lhsT=ke_tm[0:cn, c, hl * 128:(hl + 1) * 128],
                                                                                rhs=v_tm[0:cn, c, hl * 128:(hl + 1) * 128], start=True, stop=True),
                           reads=[('ke_tm', c), ('v_tm', c)], writes=[ubn])
                  for hl in range(4):
                      hd = hg * 4 + hl
                      P.op('dve', lambda e, hl=hl, hd=hd, c=c, ubk=ubk: e.scalar_tensor_tensor(
                          out=S_f[:, hd, :], in0=S_f[:, hd, :], scalar=dch[:, hl, c:c + 1], in1=ubk[:, hl * 128:(hl + 1) * 128],
                          op0=ALU.mult, op1=ALU.add), reads=[('S_f', hd), ('dch', hl), ubn], writes=[('S_f', hd)])
                  for hl in range(4 if not pref else 0):
                      hd = hg * 4 + hl
                      P.op('pe', lambda e, psO=psO, hl=hl, c=c, cn=cn, par=par: e.matmul(
                          psO[:, hl, :], lhsT=v_tm[0:cn, c, hl * 128:(hl + 1) * 128], rhs=scT[par][0:cn, hl, 0:cn], start=True, stop=False),
                          reads=[('v_tm', c), ('scT', par)], writes=[obn])
                      P.op('pe', lambda e, psO=psO, hl=hl, hd=hd, c0=c0, cn=cn: e.matmul(
                          psO[:, hl, :], lhsT=S_b[:, hd, :], rhs=qdec[:, hl, c0:c0 + cn], start=False, stop=True),
                          reads=[('S_b', hd), ('qdec', hl)], writes=[obn])
                  if not pref:
                   P.op('act', lambda e, psO=psO, c0=c0, cn=cn: e.activation(out=o_sb[:, :, c0:c0 + cn], in_=psO, func=AF.Copy),
                       reads=[obn], writes=['o_sb'])
                  if (not pref) or c == NCH - 1:
                      P.op('act', lambda e, hg=hg: e.activation(out=S_b[:, hg * 4:(hg + 1) * 4, :], in_=S_f[:, hg * 4:(hg + 1) * 4, :], func=AF.Copy),
                           reads=[('S_f', hg * 4 + i) for i in range(4)], writes=[('S_b', hg * 4 + i) for i in range(4)])
              stop_pt('h_scan')
              NH = 4 if not pref else 0
              nbanks = [(pb45[:, 0:512], ['pb4']), (pb45[:, 512:1024], ['pb5']), (pb6[:, :], ['pb6']), (pb7[:, :], ['pb7'])]
              for hl in range(NH):
                  P.op('act', lambda e, hl=hl: e.activation(out=osq4[:, hl, 0:T], in_=o_sb[:, hl, 0:T], func=AF.Square), reads=['o_sb'], writes=[('osq', hl)])
              npss = {}
              for hl in range(NH):
                  for (soff, sn) in segs:
                      if sn <= 16:
                          ps, pn = next_mini()
                          pn = [pn]
                      else:
                          ps, pn = nbanks[hl][0][:, 0:sn], nbanks[hl][1]
                      npss[(hl, soff)] = (ps, pn)
                      P.op('pe', lambda e, ps=ps, soff=soff, sn=sn, hl=hl: e.matmul(ps, lhsT=onesb[:, :], rhs=osq4[:, hl, soff:soff + sn], start=True, stop=True),
                           reads=[('osq', hl), 'onesb'], writes=pn)
                      if sn <= 16:
                          P.op('act', lambda e, ps=ps, soff=soff, sn=sn, hl=hl: e.activation(out=rst4[:, hl, soff:soff + sn], in_=ps, func=AF.Sqrt, scale=1.0 / 128,
                                                                                             bias=epsc[:, :]), reads=pn + ['epsc'], writes=[('rst', hl)])
              for hl in range(NH):
                  for (soff, sn) in segs:
                      if sn > 16:
                          ps, pn = npss[(hl, soff)]
                          P.op('act', lambda e, ps=ps, soff=soff, sn=sn, hl=hl: e.activation(out=rst4[:, hl, soff:soff + sn], in_=ps, func=AF.Sqrt, scale=1.0 / 128,
                                                                                             bias=epsc[:, :]), reads=pn + ['epsc'], writes=[('rst', hl)])
              for hl in range(NH):
                  hd = hg * 4 + hl
                  P.op('dve', lambda e, hl=hl: e.reciprocal(out=rst4[:, hl, 0:T], in_=rst4[:, hl, 0:T]), reads=[('rst', hl)], writes=[('rst', hl)])
                  P.op('dve', lambda e, hl=hl: e.tensor_tensor(out=rst4[:, hl, 0:T], in0=o_sb[:, hl, 0:T], in1=rst4[:, hl, 0:T], op=ALU.mult),
                       reads=['o_sb', ('rst', hl)], writes=[('rst', hl)])
                  P.op('dve', lambda e, hl=hl, hd=hd: e.scalar_tensor_tensor(
                      out=mixT[:, hd, 0:T], in0=rst4[:, hl, 0:T], scalar=pcol[:, R_HGN + hd:R_HGN + hd + 1], in1=sgf[:, hl, 0:T], op0=ALU.mult, op1=ALU.mult),
                      reads=[('rst', hl), 'pcol', ('sgf', hl)], writes=[('mixT', hd)])
          if dbg:
              tap('mixA', mixT[:, 0:8, 0:T], [128, 8, T], [('mixT', i) for i in range(8)])
          stop_pt('hgrn')

          A.reset(PH)
          pre = [A.alloc(f"pre{i}", [128, 3 + TM], BF16) for i in range(4)]
          dg = [A.alloc(f"dg{i}", [128, 4, 128], BF16) for i in range(4)]
          xfs = [A.alloc(f"xf{i}", [128, 4, TM], BF16) for i in range(2)]
          B_fm = A.alloc("B_fm", [128, 2, TM], BF16)
          C_fm = A.alloc("C_fm", [128, 2, TM], BF16)
          x_tm = A.alloc("x_tm", [128, 5, 1024], BF16)
          B_tm = A.alloc("B_tm", [128, 5, 256], BF16)
          zs_tm = A.alloc("zs_tm", [128, 5, 1024], BF16)
          dt_tm = A.alloc("dt_tm", [128, 5, 16], F32)
          da_tm = A.alloc("da_tm", [128, 5, 16], F32)
          lndt = A.alloc("lndt", [128, 5, 16], F32)
          acum, nb, Eexp, cd, dte, w2, dtp, lnv, pol, msk = [A.alloc(nm, [128, 16], F32) for nm in
                                                           ['acum', 'nb', 'Eexp', 'cd', 'dte', 'w2', 'dtp', 'lnv', 'pol', 'msk']]
          rhsb = A.alloc("rhsb", [128, 16, 128], F32)
          LTs = [A.alloc(f"LT{i}", [128, 8, 128], BF16) for i in range(2)]
          MTs = [A.alloc(f"MT{i}", [128, 8, 128], BF16) for i in range(2)]
          xd = A.alloc("xd", [128, 1024], BF16)
          xw = A.alloc("xw", [128, 512], BF16)
          tmpf = A.alloc("tmpf", [128, 512], F32)
          yy = A.alloc("yy", [128, 1024], F32)
          y2 = A.alloc("y2", [128, 1024], F32)
          mo = A.alloc("mo", [128, 1024], BF16)
          P.barrier()

          def conv_fm(wt, wn, cbl, cb, ntap, prebuf, prename, dgbuf, dgname, halbuf, halname, rw, evac, rhs_t=uT, rhs_n='uT', phase=None):
              hw = ntap - 1
              hn = (halname, cb)
              if phase in (None, 'A'):
                  P.op('dve', lambda e: e.tensor_copy(out=prebuf[:, 0:hw], in_=halbuf[:, cb, 0:hw]), reads=[hn], writes=[prename])
                  for k in range(ntap):
                      P.op('dve', lambda e, k=k: e.tensor_scalar(out=dgbuf[:, k, :], in0=ident[:, :], scalar1=pcol[:, rw(k):rw(k) + 1], scalar2=None,
                                                                 op0=ALU.mult), reads=['ident', 'pcol'], writes=[dgname])
                  mm_fm(wt, wn, cbl, rhs_t, [rhs_n], lambda ps, pn, soff, sn: P.op(
                      'act', lambda e: e.activation(out=prebuf[:, hw + soff:hw + soff + sn], in_=ps, func=AF.Copy), reads=pn, writes=[prename]))
                  P.op('dve', lambda e: e.tensor_copy(out=halbuf[:, cb, 0:hw], in_=prebuf[:, T:T + hw]), reads=[prename], writes=[hn])
              if phase in (None, 'B'):
                  for (soff, sn) in segs:
                      if sn <= 16:
                          cps, cpn = next_mini()
                          cpn = [cpn]
                      else:
                          cps, cpn = next_cv()
                          cps = cps[:, 0:sn]
                      for k in range(ntap):
                          P.op('pe', lambda e, cps=cps, k=k, soff=soff, sn=sn: e.matmul(cps, lhsT=dgbuf[:, k, :], rhs=prebuf[:, soff + k:soff + k + sn],
                                                                                       start=(k == 0), stop=(k == ntap - 1)),
                               reads=[dgname, prename], writes=cpn, inc=(k == ntap - 1))
                      evac(cps, cpn, soff, sn)

          def run_skewed(tasks, skew):
              nt = len(tasks)
              for i in range(nt + skew):
                  if i < nt:
                      tasks[i]('A')
                  if i - skew >= 0:
                      tasks[i - skew]('B')

          for j in range(3):
              wt, wn = use_piece(pc[('xbc', j)])
              tasks = []
              for cbl in range(4 if not (pref and j == 2) else 2):
                  cb = j * 4 + cbl
                  pi = cb % 4
                  if cb < 8:
                      dst, dname = xfs[j % 2][:, cbl, :], ('xf', j % 2)
                  elif cb < 10:
                      dst, dname = B_fm[:, cb - 8, :], 'B_fm'
                  else:
                      dst, dname = C_fm[:, cb - 10, :], 'C_fm'
                  tasks.append(lambda ph, wt=wt, wn=wn, cbl=cbl, cb=cb, pi=pi, dst=dst, dname=dname: conv_fm(
                      wt, wn, cbl, cb, 4, pre[pi], ('pre', pi), dg[pi], ('dg', pi), hal, 'hal', lambda k, cb=cb: R_MCW + k * 12 + cb,
                      lambda cps, cpn, soff, sn, dst=dst, dname=dname, cb=cb: P.op(
                          'act', lambda e: e.activation(out=dst[:, soff:soff + sn], in_=cps, func=AF.Silu, bias=pcol[:, R_MCB + cb:R_MCB + cb + 1]),
                          reads=cpn + ['pcol'], writes=[dname]), phase=ph))
              run_skewed(tasks, 1)
              if j < 2:
                  for ti, (off, n) in enumerate(tiles):
                      tb, tbn = pTm[ti % 2]
                      for cbl in range(4):
                          P.op('pe', lambda e, cbl=cbl, off=off, n=n, tb=tb, j=j: e.transpose(out=tb[0:n, cbl, :], in_=xfs[j % 2][:, cbl, off:off + n], identity=ident[:, :]),
                               reads=[('xf', j % 2), 'ident'], writes=[tbn])
                      P.op('dve', lambda e, ti=ti, n=n, j=j, tb=tb: e.tensor_copy(
                          out=x_tm[0:n, ti, j * 512:(j + 1) * 512].rearrange("p (c k) -> p c k", c=4), in_=tb[0:n, 0:4, :]),
                          reads=[tbn], writes=[('x_tm', ti)])
              else:
                  for ti, (off, n) in enumerate(tiles):
                      tb, tbn = pTm[ti % 2]
                      for g in range(2):
                          P.op('pe', lambda e, g=g, off=off, n=n, tb=tb: e.transpose(out=tb[0:n, g, :], in_=B_fm[:, g, off:off + n], identity=ident[:, :]),
                               reads=['B_fm', 'ident'], writes=[tbn])
                      P.op('dve', lambda e, ti=ti, n=n, tb=tb: e.tensor_copy(out=B_tm[0:n, ti, :].rearrange("p (c k) -> p c k", c=2), in_=tb[0:n, 0:2, :]),
                           reads=[tbn], writes=[('B_tm', ti)])
          for j in range(2 if not pref else 0):
              wt, wn = use_piece(pc[('z', j)])
              for ti, (off, n) in enumerate(tiles):
                  ps, pn = next_acc()
                  for k in range(8):
                      P.op('pe', lambda e, ps=ps, k=k, off=off, n=n, wt=wt: e.matmul(ps[0:n, :], lhsT=uT[:, k, off:off + n], rhs=wt[:, k, :],
                                                                                      start=(k == 0), stop=(k == 7)), reads=[wn, 'uT'], writes=[pn], inc=(k == 7))
                  P.op('act', lambda e, ps=ps, ti=ti, n=n, j=j: e.activation(out=zs_tm[0:n, ti, j * 512:(j + 1) * 512], in_=ps[0:n, :], func=AF.Silu),
                       reads=[pn], writes=[('zs', ti)])
          wt, wn = use_piece(pc['dt'])
          for ti, (off, n) in enumerate(tiles):
              ps, pn = next_mini()
              for k in range(8):
                  P.op('pe', lambda e, ps=ps, k=k, off=off, n=n, wt=wt: e.matmul(ps[0:n, :], lhsT=uT[:, k, off:off + n], rhs=wt[:, k, 0:16],
                                                                                  start=(k == 0), stop=(k == 7)), reads=[wn, 'uT'], writes=[pn])
              P.op('dve', lambda e, ps=ps, n=n: e.tensor_tensor(out=dtp[0:n, :], in0=ps[0:n, :], in1=dtb_b[0:n, :], op=ALU.add),
                   reads=[pn, 'dtb_b'], writes=['dtp'])
              P.op('act', lambda e, n=n: e.activation(out=dtp[0:n, :], in_=dtp[0:n, :], func=AF.Exp), reads=['dtp'], writes=['dtp'])
              P.op('act', lambda e, n=n: e.activation(out=lnv[0:n, :], in_=dtp[0:n, :], func=AF.Ln, bias=onec[0:n, :]), reads=['dtp', 'onec'], writes=['lnv'])
              P.op('dve', lambda e, n=n: e.tensor_scalar(out=pol[0:n, :], in0=dtp[0:n, :], scalar1=1.0 / 7.0, scalar2=None, op0=ALU.mult), reads=['dtp'], writes=['pol'])
              for cc in (-1.0 / 6.0, 0.2, -0.25, 1.0 / 3.0, -0.5, 1.0):
                  P.op('dve', lambda e, n=n, cc=cc: e.scalar_tensor_tensor(out=pol[0:n, :], in0=pol[0:n, :], scalar=cc, in1=dtp[0:n, :], op0=ALU.add, op1=ALU.mult),
                       reads=['pol', 'dtp'], writes=['pol'])
              P.op('dve', lambda e, n=n: e.tensor_single_scalar(out=msk[0:n, :], in_=dtp[0:n, :], scalar=0.125, op=ALU.is_lt), reads=['dtp'], writes=['msk'])
              P.op('dve', lambda e, n=n: e.tensor_tensor(out=pol[0:n, :], in0=pol[0:n, :], in1=lnv[0:n, :], op=ALU.subtract), reads=['pol', 'lnv'], writes=['pol'])
              P.op('dve', lambda e, n=n: e.tensor_tensor(out=pol[0:n, :], in0=pol[0:n, :], in1=msk[0:n, :], op=ALU.mult), reads=['pol', 'msk'], writes=['pol'])
              P.op('dve', lambda e, n=n, ti=ti: e.tensor_tensor(out=dt_tm[0:n, ti, :], in0=pol[0:n, :], in1=lnv[0:n, :], op=ALU.add),
                   reads=['pol', 'lnv'], writes=[('dt', ti)])
              P.op('dve', lambda e, n=n, ti=ti: e.tensor_tensor(out=da_tm[0:n, ti, :], in0=dt_tm[0:n, ti, :], in1=aneg_b[0:n, :], op=ALU.mult),
                   reads=[('dt', ti), 'aneg_b'], writes=[('da', ti)])
              P.op('act', lambda e, n=n, ti=ti: e.activation(out=lndt[0:n, ti, :], in_=dt_tm[0:n, ti, :], func=AF.Ln), reads=[('dt', ti)], writes=[('lndt', ti)])
          stop_pt('m_proj')

          h8 = lambda ap: ap.rearrange("p (h q) -> p h q", q=64)
          for ti, (off, n) in enumerate(tiles):
              if not pref:
                P.op('dve', lambda e, ti=ti, n=n: e.tensor_tensor(out=h8(xd[0:n, :]), in0=h8(x_tm[0:n, ti, :]),
                                                                  in1=dsk_b[0:n, :].unsqueeze(2).broadcast_to([n, 16, 64]), op=ALU.mult),
                     reads=[('x_tm', ti), 'dsk_b'], writes=['xd'])
              psA, pnA = next_mini()
              P.op('pe', lambda e, psA=psA, ti=ti, n=n: e.matmul(psA[0:n, :], lhsT=triU[0:n, 0:n], rhs=da_tm[0:n, ti, :], start=True, stop=True),
                   reads=['triU', ('da', ti)], writes=[pnA])
              psL, pnL = next_mini()
              P.op('pe', lambda e, psL=psL, ti=ti, n=n: e.matmul(psL[:, :], lhsT=onesf[0:n, :], rhs=da_tm[0:n, ti, :], start=True, stop=True),
                   reads=['onesf', ('da', ti)], writes=[pnL])
              P.op('dve', lambda e, psA=psA, n=n: e.tensor_copy(out=acum[0:n, :], in_=psA[0:n, :]), reads=[pnA], writes=['acum'])
              if not pref:
                P.op('dve', lambda e, psA=psA, n=n, ti=ti: e.tensor_tensor(out=nb[0:n, :], in0=lndt[0:n, ti, :], in1=psA[0:n, :], op=ALU.subtract),
                   reads=[pnA, ('lndt', ti)], writes=['nb'])
                P.op('act', lambda e, psA=psA, n=n: e.activation(out=Eexp[0:n, :], in_=psA[0:n, :], func=AF.Exp), reads=[pnA], writes=['Eexp'])
              P.op('act', lambda e, psL=psL: e.activation(out=cd[:, :], in_=psL[:, :], func=AF.Exp), reads=[pnL], writes=['cd'])
              P.op('dve', lambda e, psL=psL, n=n: e.tensor_tensor(out=dte[0:n, :], in0=psL[0:n, :], in1=acum[0:n, :], op=ALU.subtract),
                   reads=[pnL, 'acum'], writes=['dte'])
              P.op('act', lambda e, n=n: e.activation(out=dte[0:n, :], in_=dte[0:n, :], func=AF.Exp), reads=['dte'], writes=['dte'])
              P.op('dve', lambda e, n=n, ti=ti: e.tensor_tensor(out=w2[0:n, :], in0=dte[0:n, :], in1=dt_tm[0:n, ti, :], op=ALU.mult),
                   reads=['dte', ('dt', ti)], writes=['w2'])
              if not pref:
               P.op('dve', lambda e, n=n, ti=ti: e.tensor_tensor(out=rhsb[0:n, :, 0:n], in0=triU[0:n, 0:n].unsqueeze(1).broadcast_to([n, 16, n]),
                                                                in1=da_tm[0:n, ti, :].unsqueeze(2).broadcast_to([n, 16, n]), op=ALU.mult),
                   reads=['triU', ('da', ti)], writes=['rhsb'])
              def gp1(g):
                  pcb = pb2[0:n, 256 + g * 128:256 + g * 128 + n]
                  if not pref:
                      gbk = [((pb45[:, 0:512], 'pb4'), (pb45[:, 512:1024], 'pb5')), ((acc[0][:, :], 'acc0'), (acc[1][:, :], 'acc1'))][g]
                      if n == 128:
                          for hh in range(2):
                              outp, bn = gbk[hh]
                              h0 = g * 8 + hh * 4
                              P.op('pe', lambda e, outp=outp, h0=h0: e.matmul(outp, lhsT=onesf[:, :], rhs=rhsb[:, h0:h0 + 4, :].rearrange("p h i -> p (h i)"),
                                                                              start=True, stop=False), reads=['onesf', 'rhsb'], writes=[bn])
                              P.op('pe', lambda e, outp=outp: e.matmul(outp, lhsT=ident[:, :], rhs=mneg[:, :, :].rearrange("p h i -> p (h i)"), start=False, stop=True),
                                   reads=['ident', 'mneg'], writes=[bn])
                      else:
                          for hl in range(8):
                              outp = gbk[hl // 4][0][0:n, (hl % 4) * 128:(hl % 4) * 128 + n]
                              bn = gbk[hl // 4][1]
                              P.op('pe', lambda e, outp=outp, n=n, hl=hl, g=g: e.matmul(outp, lhsT=onesf[0:n, 0:n], rhs=rhsb[0:n, g * 8 + hl, 0:n], start=True, stop=False),
                                   reads=['onesf', 'rhsb'], writes=[bn])
                              P.op('pe', lambda e, outp=outp, n=n: e.matmul(outp, lhsT=ident[0:n, 0:n], rhs=mneg[0:n, 0, 0:n], start=False, stop=True),
                                   reads=['ident', 'mneg'], writes=[bn])
                      P.op('pe', lambda e, pcb=pcb, g=g, off=off, n=n: e.matmul(pcb, lhsT=B_fm[:, g, off:off + n], rhs=C_fm[:, g, off:off + n], start=True, stop=True),
                           reads=['B_fm', 'C_fm'], writes=['pb2'])
                      P.op('pe', lambda e, g=g, off=off, n=n: e.matmul(yob[g][0][0:n, :], lhsT=C_fm[:, g, off:off + n], rhs=ST_b[:, g, :], start=True, stop=True),
                           reads=['C_fm', ('ST_b', g)], writes=[yob[g][1]])

              def gp2(g):
                  pcb = pb2[0:n, 256 + g * 128:256 + g * 128 + n]
                  P.op('dve', lambda e, g=g, n=n, ti=ti: e.tensor_tensor(out=h8(xw[0:n, :]), in0=h8(x_tm[0:n, ti, g * 512:(g + 1) * 512]),
                                                                         in1=w2[0:n, g * 8:(g + 1) * 8].unsqueeze(2).broadcast_to([n, 8, 64]), op=ALU.mult),
                       reads=[('x_tm', ti), 'w2'], writes=['xw'])
                  if not pref:
                      P.op('dve', lambda e, g=g, n=n: e.tensor_tensor(out=h8(tmpf[0:n, :]), in0=h8(yob[g][0][0:n, :]),
                                                                      in1=Eexp[0:n, g * 8:(g + 1) * 8].unsqueeze(2).broadcast_to([n, 8, 64]), op=ALU.mult),
                           reads=[yob[g][1], 'Eexp'], writes=['tmpf'])
                  psU, pnU = yob[g]
                  P.op('pe', lambda e, psU=psU, g=g, n=n, ti=ti: e.matmul(psU[:, :], lhsT=B_tm[0:n, ti, g * 128:(g + 1) * 128], rhs=xw[0:n, :], start=True, stop=True),
                       reads=[('B_tm', ti), 'xw'], writes=[pnU])
                  P.op('dve', lambda e, g=g: e.tensor_tensor(out=h8(ST_f[:, g, :]), in0=h8(ST_f[:, g, :]),
                                                             in1=cd[:, g * 8:(g + 1) * 8].unsqueeze(2).broadcast_to([128, 8, 64]), op=ALU.mult),
                       reads=[('ST_f', g), 'cd'], writes=[('ST_f', g)])
                  P.op('dve', lambda e, g=g, psU=psU: e.tensor_tensor(out=ST_f[:, g, :], in0=ST_f[:, g, :], in1=psU[:, :], op=ALU.add),
                       reads=[('ST_f', g), pnU], writes=[('ST_f', g)])
                  if not pref:
                      gbk = [((pb45[:, 0:512], 'pb4'), (pb45[:, 512:1024], 'pb5')), ((acc[0][:, :], 'acc0'), (acc[1][:, :], 'acc1'))][g]
                      for hl in range(8):
                          hh = g * 8 + hl
                          bk, bn = gbk[hl // 4]
                          P.op('act', lambda e, hl=hl, hh=hh, n=n, bk=bk: e.activation(out=LTs[g][0:n, hl, 0:n], in_=bk[0:n, (hl % 4) * 128:(hl % 4) * 128 + n], func=AF.Exp,
                                                                                      bias=nb[0:n, hh:hh + 1]), reads=[bn, 'nb'], writes=[('LT', g)])
                  P.op('act', lambda e, g=g: e.activation(out=ST_b[:, g, :], in_=ST_f[:, g, :], func=AF.Copy), reads=[('ST_f', g)], writes=[('ST_b', g)])

              def gp3(g):
                  pcb = pb2[0:n, 256 + g * 128:256 + g * 128 + n]
                  if not pref:
                      P.op('dve', lambda e, pcb=pcb, n=n: e.tensor_tensor(out=MTs[g][0:n, :, 0:n], in0=LTs[g][0:n, :, 0:n], in1=pcb.unsqueeze(1).broadcast_to([n, 8, n]),
                                                                          op=ALU.mult), reads=[('LT', g), 'pb2'], writes=[('MT', g)])
                      P.op('pe', lambda e, g=g, n=n: e.matmul(pb6[0:n, :], lhsT=ident[0:n, 0:n], rhs=xd[0:n, g * 512:(g + 1) * 512], start=True, stop=False),
                           reads=['ident', 'xd'], writes=['pb6'])
                      for hl in range(8):
                          hh = g * 8 + hl
                          P.op('pe', lambda e, hl=hl, hh=hh, n=n, ti=ti: e.matmul(pb6[0:n, hl * 64:(hl + 1) * 64], lhsT=MTs[g][0:n, hl, 0:n],
                                                                                  rhs=x_tm[0:n, ti, hh * 64:(hh + 1) * 64], start=False, stop=(hl == 7)),
                               reads=[('MT', g), ('x_tm', ti)], writes=['pb6'])
                      P.op('dve', lambda e, g=g, n=n: e.tensor_tensor(out=yy[0:n, g * 512:(g + 1) * 512], in0=pb6[0:n, :], in1=tmpf[0:n, :], op=ALU.add),
                           reads=['pb6', 'tmpf'], writes=['yy'])

              if pref:
                  gp2(0)
                  gp2(1)
              else:
                  gp1(0)
                  gp2(0)
                  gp1(1)
                  gp3(0)
                  gp2(1)
                  gp3(1)
              if pref:
                  continue
              if dbg and ti == 1:
                  tap('y_raw1', yy[:, :], [128, 1024], ['yy'])
              P.op('dve', lambda e, n=n, ti=ti: e.tensor_tensor(out=y2[0:n, :], in0=yy[0:n, :], in1=zs_tm[0:n, ti, :], op=ALU.mult),
                   reads=['yy', ('zs', ti)], writes=['y2'])
              for g in range(2):
                  P.op('act', lambda e, g=g, n=n: e.activation(out=junk[0:n, 0:512], in_=y2[0:n, g * 512:(g + 1) * 512], func=AF.Square,
                                                               accum_out=ss[0:n, g:g + 1]), reads=['y2'], writes=['junk', 'ss'])
              P.op('dve', lambda e, n=n: e.tensor_scalar(out=rs[0:n, 0:2], in0=ss[0:n, 0:2], scalar1=1.0 / 512, scalar2=EPS, op0=ALU.mult, op1=ALU.add),
                   reads=['ss'], writes=['rs'])
              P.op('pool', lambda e, n=n: e.tensor_tensor(out=rs[0:n, 0:2], in0=rs[0:n, 0:2], in1=nhalf[0:n, 0:2], op=ALU.pow), reads=['rs', 'nhalf'], writes=['rs'])
              for g in range(2):
                  P.op('dve', lambda e, g=g, n=n: e.scalar_tensor_tensor(out=mo[0:n, g * 512:(g + 1) * 512], in0=y2[0:n, g * 512:(g + 1) * 512],
                                                                         scalar=rs[0:n, g:g + 1], in1=m2nw_b[0:n, g * 512:(g + 1) * 512],
                                                                         op0=ALU.mult, op1=ALU.mult), reads=['y2', 'rs', 'm2nw_b'], writes=['mo'])
              for j in range(8):
                  P.op('pe', lambda e, j=j, n=n: e.transpose(out=pT[:, j, 0:n], in_=mo[0:n, j * 128:(j + 1) * 128], identity=ident[0:n, 0:n]),
                       reads=['mo', 'ident'], writes=['pT'])
              P.op('act', lambda e, off=off, n=n: e.activation(out=mixT[:, 8:16, off:off + n], in_=pT[:, :, 0:n], func=AF.Copy),
                   reads=['pT'], writes=[('mixT', 8 + j) for j in range(8)])
          if dbg:
              tap('mixB', mixT[:, 8:16, 0:T], [128, 8, T], [('mixT', 8 + j) for j in range(8)])
          stop_pt('mamba')
          if pref:
              return T

          A.reset(PH)
          for c2 in range(2):
              w0, n0 = use_piece(pc[('wo', c2, 0)])
              w1, n1 = use_piece(pc[('wo', c2, 1)], hold=True)
              for ti, (off, n) in enumerate(tiles):
                  ps, pn = next_acc()
                  for kk in range(16):
                      wt, wn = (w0, n0) if kk < 8 else (w1, n1)
                      P.op('pe', lambda e, ps=ps, kk=kk, off=off, n=n, wt=wt: e.matmul(ps[0:n, :], lhsT=mixT[:, kk, off:off + n], rhs=wt[:, kk % 8, :],
                                                                                        start=(kk == 0), stop=(kk == 15)),
                           reads=[('mixT', kk), wn], writes=[pn], inc=(kk % 8 == 7))
                  P.op('dve', lambda e, ps=ps, ti=ti, n=n, c2=c2: e.tensor_tensor(out=h[0:n, ti, c2 * 512:(c2 + 1) * 512], in0=h[0:n, ti, c2 * 512:(c2 + 1) * 512],
                                                                                  in1=ps[0:n, :], op=ALU.add), reads=[('h', ti), pn], writes=[('h', ti)])
          if dbg:
              tap('hmid1', h[:, 1, :], [128, 1024], [('h', 1)])
          norm_to_fm(R_N2, False)
          stop_pt('oproj')

          pre2 = [A.alloc(f"pre2{i}", [128, 2 + TM], BF16) for i in range(4)]
          dg2 = [A.alloc(f"dg2{i}", [128, 3, 128], BF16) for i in range(4)]
          gs = [A.alloc(f"gs{i}", [128, TM], BF16) for i in range(2)]
          aT = A.alloc("aT", [128, 22, TM], BF16)
          ob = [A.alloc(f"ob{i}", [128, D], F32) for i in range(2)]
          P.barrier()
          tasks = []
          for j in range(6):
              ncb = 4 if j < 5 else 2
              for cbl in range(ncb):
                  cbg = j * 4 + cbl
                  cbv = 22 + cbg
                  gi = cbg % 2
                  pg, pv = (2 * cbg) % 4, (2 * cbg + 1) % 4
                  tasks.append(lambda ph, j=j, cbl=cbl, cbg=cbg, gi=gi, pg=pg: conv_fm(
                      wsl[('ug', j)][0], wsl[('ug', j)][1], cbl, cbg, 3, pre2[pg], ('pre2', pg), dg2[pg], ('dg2', pg), hal2, 'hal2',
                      lambda k, cbg=cbg: R_FCW + k * 44 + cbg,
                      lambda cps, cpn, soff, sn, gi=gi, cbg=cbg: P.op(
                          'act', lambda e: e.activation(out=gs[gi][:, soff:soff + sn], in_=cps, func=AF.Silu, bias=pcol[:, R_FCB + cbg:R_FCB + cbg + 1]),
                          reads=cpn + ['pcol'], writes=[('gs', gi)]), phase=ph))
                  tasks.append(lambda ph, j=j, cbl=cbl, cbg=cbg, cbv=cbv, gi=gi, pv=pv: conv_fm(
                      wsl[('uv', j)][0], wsl[('uv', j)][1], cbl, cbv, 3, pre2[pv], ('pre2', pv), dg2[pv], ('dg2', pv), hal2, 'hal2',
                      lambda k, cbv=cbv: R_FCW + k * 44 + cbv,
                      lambda cps, cpn, soff, sn, gi=gi, cbg=cbg, cbv=cbv: P.op(
                          'dve', lambda e: e.scalar_tensor_tensor(out=aT[:, cbg, soff:soff + sn], in0=cps, scalar=pcol[:, R_FCB + cbv:R_FCB + cbv + 1],
                                                                  in1=gs[gi][:, soff:soff + sn], op0=ALU.add, op1=ALU.mult),
                          reads=cpn + ['pcol', ('gs', gi)], writes=[('aT', cbg)]), phase=ph))
          wsl = {}
          nt = len(tasks)
          for i in range(nt + 2):
              if i < nt and i % 8 == 0:
                  j = i // 8
                  wsl[('ug', j)] = use_piece(pc[('ug', j)])
                  wsl[('uv', j)] = use_piece(pc[('uv', j)], hold=True)
              if i < nt:
                  tasks[i]('A')
              if i - 2 >= 0:
                  tasks[i - 2]('B')
          stop_pt('ffn_up')

          dbanks = [(acc[0][:, :], 'acc0'), (acc[1][:, :], 'acc1'), (pb45[:, 0:512], 'pb4'), (pb45[:, 512:1024], 'pb5'), (pb6[:, :], 'pb6')]
          for c2 in range(2):
              for kp in range(3):
                  wt, wn = use_piece(pc[('wd', c2, kp)])
                  nk = 8 if kp < 2 else 6
                  for ti, (off, n) in enumerate(tiles):
                      ps, pn = dbanks[ti]
                      for k in range(nk):
                          kk = kp * 8 + k
                          P.op('pe', lambda e, ps=ps, kk=kk, k=k, off=off, n=n, wt=wt: e.matmul(ps[0:n, :], lhsT=aT[:, kk, off:off + n], rhs=wt[:, k, :],
                                                                                                 start=(kk == 0), stop=(kk == 21)),
                               reads=[('aT', kk), wn], writes=[pn], inc=(k == nk - 1))
              for ti, (off, n) in enumerate(tiles):
                  ps, pn = dbanks[ti]
                  P.op('dve', lambda e, ps=ps, ti=ti, n=n, c2=c2: e.tensor_tensor(out=h[0:n, ti, c2 * 512:(c2 + 1) * 512], in0=h[0:n, ti, c2 * 512:(c2 + 1) * 512],
                                                                                  in1=ps[0:n, :], op=ALU.add), reads=[('h', ti), pn], writes=[('h', ti)])
          for ti, (off, n) in enumerate(tiles):
              oi = ti % 2
              P.op('act', lambda e, ti=ti, n=n: e.activation(out=junk[0:n, :], in_=h[0:n, ti, :], func=AF.Square, accum_out=ssn[0:n, ti:ti + 1]),
                   reads=[('h', ti)], writes=['junk', ('ssn', ti)])
              P.op('act', lambda e, ti=ti, n=n: e.activation(out=rsn[0:n, ti:ti + 1], in_=ssn[0:n, ti:ti + 1], func=AF.Sqrt, scale=1.0 / D, bias=epsc[0:n, :]),
                   reads=[('ssn', ti), 'epsc'], writes=[('rsn', ti)])
              P.op('dve', lambda e, ti=ti, n=n: e.reciprocal(out=rsn[0:n, ti:ti + 1], in_=rsn[0:n, ti:ti + 1]), reads=[('rsn', ti)], writes=[('rsn', ti)])
              P.op('dve', lambda e, ti=ti, n=n, oi=oi: e.scalar_tensor_tensor(out=ob[oi][0:n, :], in0=h[0:n, ti, :], scalar=rsn[0:n, ti:ti + 1], in1=fnw_b[0:n, :],
                                                                              op0=ALU.mult, op1=ALU.mult), reads=[('h', ti), ('rsn', ti), 'fnw_b'], writes=[('ob', oi)])
              P.dma('sp', lambda e, oi=oi, off=off, n=n, out0=out0: e.dma_start(out=out[out0 + off:out0 + off + n, :], in_=ob[oi][0:n, :]),
                    reads=[('ob', oi)], chan=('o', oi))

          return T

      out0 = 0
      for sb_i in range((n_pre + n_sb) if stop_after != 'setup' else 0):
          T_ = do_sb(sb_i, tok0, out0)
          tok0 += T_
          if sb_modes[sb_i] == 'main':
              out0 += T_
          if sb_i == n_pre - 1:
              for tns, nm in [(S_f, 'S_f'), (S_b, 'S_b')]:
                  P.op('dve', lambda e, tns=tns: e.tensor_scalar(out=tns[:], in0=tns[:], scalar1=flagc[:, 0:1], scalar2=None, op0=ALU.mult),
                       reads=[(nm, i) for i in range(8)] + ['flagc'], writes=[(nm, i) for i in range(8)])
              for tns, nm in [(ST_f, 'ST_f'), (ST_b, 'ST_b')]:
                  P.op('dve', lambda e, tns=tns: e.tensor_scalar(out=tns[:], in0=tns[:], scalar1=flagc[:, 0:1], scalar2=None, op0=ALU.mult),
                       reads=[(nm, i) for i in range(2)] + ['flagc'], writes=[(nm, i) for i in range(2)])
              P.op('dve', lambda e: e.tensor_scalar(out=hal[:], in0=hal[:], scalar1=flagc[:, 0:1], scalar2=None, op0=ALU.mult),
                   reads=[('hal', i) for i in range(12)] + ['flagc'], writes=[('hal', i) for i in range(12)])
    except StopBuild:
        pass

    tap('pcol', pcol[:, :], [128, PROWS], ['pcol'])
    tap('oml', oml[:, :], [128, 8], ['oml'])
    P.wait_all('sp', [k for k in P.count if isinstance(k, tuple) and k[0] == 'd'])
    P.emit()
    return nc, tap_out


def pack_shared(inp):
    f = lambda a: np.ascontiguousarray(np.asarray(a, dtype=np.float32))
    rows = [f(inp['norm1_w'][0]).reshape(8, 128), f(inp['hg_lb_logits'][0]).reshape(8, 128), f(inp['hg_lb_logits'][1]).reshape(8, 128),
            f(inp['hg_norm_w'][0]).reshape(8, 128), f(inp['norm2_w'][0]).reshape(8, 128), f(inp['m2_conv_w'][0]).reshape(48, 128),
            f(inp['m2_conv_b'][0]).reshape(12, 128), f(inp['ffn_conv_w'][0]).reshape(132, 128), f(inp['ffn_conv_b'][0]).reshape(44, 128)]
    pvec = np.ascontiguousarray(np.concatenate(rows, 0))
    assert pvec.shape == (PROWS, 128)
    return {
        'w_in': f(inp['w_in'][0]), 'w_out': f(inp['w_out'][0]), 'w_up': f(inp['ffn_w_up'][0]), 'w_down': f(inp['ffn_w_down'][0]),
        'pvec': pvec, 'm2nw': f(inp['m2_norm_w'][0]), 'fnw': f(inp['final_norm_w']), 'dtb': f(inp['m2_dt_bias'][0]),
        'alog': f(inp['m2_a_log'][0]), 'dsk': f(inp['m2_d'][0]),
    }


def core_tokens(inp, b, TOK):
    seq = np.concatenate([np.asarray(inp['meta_tokens'], np.float32), np.asarray(inp['x'][b], np.float32)], 0)
    return np.ascontiguousarray(seq[:TOK])


_CACHE = {}
N_PRE, N_SB = 4, 4


def kernel(**inputs):
    TOKP, TOKM = 512 * N_PRE, 16 + 512 * N_SB
    if 'nc' not in _CACHE:
        _CACHE['nc'] = build_program(N_SB, n_pre=N_PRE)[0]
    nc = _CACHE['nc']
    shared = pack_shared(inputs)
    meta = np.asarray(inputs['meta_tokens'], np.float32)
    in_maps = []
    for c in range(8):
        b, half = c // 2, c % 2
        seq = np.concatenate([meta, np.asarray(inputs['x'][b], np.float32)], 0)
        m = dict(shared)
        if half == 0:
            m['xin'] = np.ascontiguousarray(np.concatenate([np.zeros((TOKP, D), np.float32), seq[0:TOKM]], 0))
            m['flag'] = np.zeros((128, 1), np.float32)
        else:
            m['xin'] = np.ascontiguousarray(seq[0:TOKP + TOKM])
            m['flag'] = np.ones((128, 1), np.float32)
        in_maps.append(m)
    res = run_bass_kernel_spmd(nc, in_maps, core_ids=list(range(8)))
    outs = []
    for b in range(4):
        outs.append(np.concatenate([res.results[2 * b]['out'][NMETA:], res.results[2 * b + 1]['out'][NMETA:]], 0))
    return np.stack(outs, 0).astype(np.float32)
```

```python
import numpy as np
import concourse.bass as bass
import concourse.mybir as mybir
from concourse.bass_utils import run_bass_kernel_spmd

F32 = mybir.dt.float32
BF16 = mybir.dt.bfloat16
ALU = mybir.AluOpType
AF = mybir.ActivationFunctionType

ENGS = ['pe', 'dve', 'act', 'pool', 'sp']
SAME_ENG_RAW = ('dve', 'act', 'pool')
SAME_ENG_ALL = True
EPS = 1e-6
NMETA = 16
D = 1024
DPROJ = 6672
DFF = 2816


class Prog:
    def __init__(self, nc):
        self.nc = nc
        self.streams = {e: [] for e in ENGS}
        self.sems = {}
        self.count = {}
        self.seen = {e: {} for e in ENGS}
        self.buf = {}

    def _sem(self, key):
        if key not in self.sems:
            name = "s_" + "_".join(str(k) for k in (key if isinstance(key, tuple) else (key,)))
            name = name.replace("(", "").replace(")", "").replace(",", "_").replace(" ", "").replace("'", "")
            self.sems[key] = self.nc.alloc_semaphore(name)
            self.count[key] = 0
        return self.sems[key]

    def _need(self, eng, reads, writes):
        need = {}

        def add(kv, raw):
            if kv is None:
                return
            k, v = kv
            if k == eng and not ((raw or SAME_ENG_ALL) and eng in SAME_ENG_RAW):
                return
            if v > need.get(k, 0):
                need[k] = v

        for b in reads:
            st = self.buf.get(b)
            if st:
                add(st[0], True)
        for b in writes:
            st = self.buf.get(b)
            if st:
                add(st[0], False)
                for k, v in st[1].items():
                    add((k, v), False)
        seen = self.seen[eng]
        for k, v in need.items():
            if v > seen.get(k, 0):
                self.streams[eng].append(('wait', k, v))
                seen[k] = v

    def _mark(self, key, val, reads, writes):
        for b in reads:
            st = self.buf.setdefault(b, [None, {}])
            st[1][key] = val
        for b in writes:
            self.buf[b] = [(key, val), {}]

    def op(self, eng, fn, reads=(), writes=()):
        self._need(eng, reads, writes)
        self._sem(eng)
        self.count[eng] += 1
        self.streams[eng].append(('op', fn, eng, 1))
        self._mark(eng, self.count[eng], reads, writes)

    def dma(self, q, fn, reads=(), writes=(), chan=None):
        self._need(q, reads, writes)
        key = ('d', chan)
        self._sem(key)
        self.count[key] += 16
        self.streams[q].append(('op', fn, key, 16))
        self._mark(key, self.count[key], reads, writes)

    def barrier(self):
        for e in ENGS:
            for k, v in self.count.items():
                if k == e:
                    continue
                if isinstance(k, tuple) and k[0] == 'd' and isinstance(k[1], tuple) and k[1][0] in ('w', 'x', 'setup'):
                    continue
                if v > self.seen[e].get(k, 0):
                    self.streams[e].append(('wait', k, v))
                    self.seen[e][k] = v

    def wait_all(self, eng, keys):
        for k in keys:
            v = self.count.get(k, 0)
            if v > self.seen[eng].get(k, 0):
                self.streams[eng].append(('wait', k, v))
                self.seen[eng][k] = v

    def emit(self):
        nc = self.nc
        engmap = {'pe': 'tensor', 'dve': 'vector', 'act': 'scalar', 'pool': 'gpsimd', 'sp': 'sync'}
        with nc.Block() as block:
            for e in ENGS:
                stream = self.streams[e]
                if not stream:
                    continue

                def body(engine, stream=stream):
                    for item in stream:
                        if item[0] == 'wait':
                            engine.wait_ge(self.sems[item[1]], item[2])
                        else:
                            ins = item[1](engine)
                            ins.then_inc(self.sems[item[2]], item[3])

                getattr(block, engmap[e])(body)


class StopBuild(Exception):
    pass


class Arena:
    def __init__(self, nc):
        self.nc = nc
        self.base = (int(nc.sbuf_base) + 63) // 64 * 64
        self.top = int(nc.sbuf_top)
        self.cur = self.base
        self.n = 0
        self.hi = self.base

    def alloc(self, name, shape, dtype):
        esz = 4 if dtype == F32 else 2
        nbytes = esz
        for s in shape[1:]:
            nbytes *= s
        nbytes = (nbytes + 63) // 64 * 64
        off = self.cur
        self.cur += nbytes
        self.hi = max(self.hi, self.cur)
        assert self.cur <= self.top, f"SBUF overflow at {name}: {self.cur} > {self.top}"
        self.n += 1
        return self.nc.alloc_sbuf_tensor_at(f"{name}_{self.n}", list(shape), dtype, offset=off)

    def mark(self):
        return self.cur

    def reset(self, m):
        self.cur = m


def sb_layout(first):
    if first:
        tiles = [(0, 16)] + [(16 + 128 * j, 128) for j in range(4)]
        chunks = [(0, 16)] + [(16 + 64 * j, 64) for j in range(8)]
        segs = [(0, 16), (16, 512)]
        T = 528
    else:
        tiles = [(128 * j, 128) for j in range(4)]
        chunks = [(64 * j, 64) for j in range(8)]
        segs = [(0, 512)]
        T = 512
    return T, tiles, chunks, segs


R_N1, R_L0, R_L1, R_HGN, R_N2, R_MCW, R_MCB, R_FCW, R_FCB, PROWS = 0, 8, 16, 24, 32, 40, 88, 100, 232, 276


def build_program(n_sb, taps=None, stop_after=None, n_pre=0):
    taps = taps or []
    TOKP = 512 * n_pre
    TOKM = 16 + 512 * n_sb
    TOK = TOKP + TOKM
    nc = bass.Bass("TRN2", target_bir_lowering=False)
    dr = lambda name, shape, kind="ExternalInput": nc.dram_tensor(name, shape, F32, kind=kind).ap()
    xin = dr("xin", [TOK, D])
    w_in = dr("w_in", [D, DPROJ])
    w_out = dr("w_out", [2 * D, D])
    w_up = dr("w_up", [D, 2 * DFF])
    w_down = dr("w_down", [DFF, D])
    pvec = dr("pvec", [PROWS, 128])
    m2nw_d = dr("m2nw", [D])
    fnw_d = dr("fnw", [D])
    dtb_d = dr("dtb", [16])
    alog_d = dr("alog", [16])
    dsk_d = dr("dsk", [16])
    flag_d = dr("flag", [128, 1])
    out = dr("out", [TOKM, D], kind="ExternalOutput")
    tap_out = {}

    P = Prog(nc)
    A = Arena(nc)
    TM = 528

    acc = [nc.alloc_psum_tensor(f"acc{i}", [128, 512], F32) for i in range(2)]
    pb2 = nc.alloc_psum_tensor("pb2", [128, 512], F32)
    pT = nc.alloc_psum_tensor("pT", [128, 8, 128], BF16)
    pb45 = nc.alloc_psum_tensor("pb45", [128, 1024], F32)
    pb6 = nc.alloc_psum_tensor("pb6", [128, 512], F32)
    pb7 = nc.alloc_psum_tensor("pb7", [128, 512], F32)
    PB4 = ['pb4']
    PB5 = ['pb5']
    PB6 = ['pb6']
    PB7 = ['pb7']
    pT7 = pb7[:, :].bitcast(BF16).rearrange("p (k c) -> p k c", k=8)
    pTs = [(pT, 'pT'), (pT7, 'pb7')]
    pT2v = pb2[:, :].bitcast(BF16).rearrange("p (k c) -> p k c", k=8)
    pTm = [(pT, 'pT'), (pT2v, 'pb2')]
    pTf = pT[:, :, :].rearrange("p k c -> p (k c)").bitcast(F32)
    yob = [(pb7[:, :], 'pb7'), (pTf, 'pT')]
    st = {'acc': 0, 'mini': 0, 'cv': 0}

    def next_acc():
        i = st['acc'] % 2
        st['acc'] += 1
        return acc[i], f'acc{i}'

    def next_mini():
        i = st['mini'] % 16
        st['mini'] += 1
        return pb2[:, i * 16:(i + 1) * 16], 'pb2'

    cvbanks = [(pb45[:, 0:512], PB4), (pb45[:, 512:1024], PB5), (pb6[:, :], PB6), (pb7[:, :], PB7)]

    def next_cv():
        i = st['cv'] % 4
        st['cv'] += 1
        return cvbanks[i]

    h = A.alloc("h", [128, 5, D], F32)
    S_f = A.alloc("S_f", [128, 8, 128], F32)
    S_b = A.alloc("S_b", [128, 8, 128], BF16)
    ST_f = A.alloc("ST_f", [128, 2, 512], F32)
    ST_b = A.alloc("ST_b", [128, 2, 512], BF16)
    hal = A.alloc("hal", [128, 12, 4], BF16)
    hal2 = A.alloc("hal2", [128, 44, 2], BF16)
    identf = A.alloc("identf", [128, 128], F32)
    ident = A.alloc("ident", [128, 128], BF16)
    onesf = A.alloc("onesf", [128, 128], F32)
    onesb = A.alloc("onesb", [128, 128], BF16)
    triU = A.alloc("triU", [128, 128], F32)
    mneg = A.alloc("mneg", [128, 4, 128], BF16)
    rmask0 = A.alloc("rmask0", [128, 528], F32)
    rmask1 = A.alloc("rmask1", [128, 512], F32)
    pcol = A.alloc("pcol", [128, PROWS], F32)
    oml = A.alloc("oml", [128, 8], F32)
    homl = A.alloc("homl", [128, 8], F32)
    nhoml = A.alloc("nhoml", [128, 8], F32)
    dl = A.alloc("dl", [128, 8], F32)
    m2nw_b = A.alloc("m2nw_b", [128, D], F32)
    fnw_b = A.alloc("fnw_b", [128, D], F32)
    dtb_b = A.alloc("dtb_b", [128, 16], F32)
    aneg_b = A.alloc("aneg_b", [128, 16], F32)
    dsk_b = A.alloc("dsk_b", [128, 16], F32)
    epsc = A.alloc("epsc", [128, 1], F32)
    onec = A.alloc("onec", [128, 1], F32)
    nhalf = A.alloc("nhalf", [128, 2], F32)
    flagc = A.alloc("flagc", [128, 1], F32)
    ptmp = A.alloc("ptmp", [128, 128], F32)
    NSLOT = 4
    ws = [A.alloc(f"ws{i}", [128, 8, 512], BF16) for i in range(NSLOT)]
    uT = A.alloc("uT", [128, 8, TM], BF16)
    mixT = A.alloc("mixT", [128, 16, TM], BF16)
    junk = A.alloc("junk", [128, D], BF16)
    xn = A.alloc("xn", [128, D], BF16)
    xn2 = [xn, A.alloc("xnb", [128, D], BF16)]
    ssn = A.alloc("ssn", [128, 8], F32)
    rsn = A.alloc("rsn", [128, 8], F32)
    ss = A.alloc("ss", [128, 4], F32)
    rs = A.alloc("rs", [128, 4], F32)
    PH = A.mark()

    P.op('pool', lambda e: e.memset(identf[:], 1.0), writes=['identf'])
    P.op('pool', lambda e: e.affine_select(out=identf[:], in_=identf[:], pattern=[[-1, 128]], compare_op=ALU.is_equal,
                                           fill=0.0, base=0, channel_multiplier=1), reads=['identf'], writes=['identf'])
    P.op('pool', lambda e: e.tensor_copy(out=ident[:], in_=identf[:]), reads=['identf'], writes=['ident'])
    P.op('pool', lambda e: e.memset(onesf[:], 1.0), writes=['onesf'])
    P.op('pool', lambda e: e.memset(onesb[:], 1.0), writes=['onesb'])
    P.op('pool', lambda e: e.memset(triU[:], 1.0), writes=['triU'])
    P.op('pool', lambda e: e.affine_select(out=triU[:], in_=triU[:], pattern=[[1, 128]], compare_op=ALU.is_ge,
                                           fill=0.0, base=0, channel_multiplier=-1), reads=['triU'], writes=['triU'])
    P.op('pool', lambda e: e.memset(mneg[:], 0.0), writes=['mneg'])
    P.op('pool', lambda e: e.affine_select(out=mneg[:], in_=mneg[:], pattern=[[0, 4], [1, 128]], compare_op=ALU.is_ge,
                                           fill=-30000.0, base=0, channel_multiplier=-1), reads=['mneg'], writes=['mneg'])
    P.op('pool', lambda e: e.memset(rmask0[:], 1.0), writes=['rmask0'])
    P.op('pool', lambda e: e.memset(rmask0[:, 0:1], 0.0), writes=['rmask0'])
    P.op('pool', lambda e: e.memset(rmask0[:, 16:528:64], 0.0), writes=['rmask0'])
    P.op('pool', lambda e: e.memset(rmask1[:], 1.0), writes=['rmask1'])
    P.op('pool', lambda e: e.memset(rmask1[:, 0:512:64], 0.0), writes=['rmask1'])
    P.op('pool', lambda e: e.memset(epsc[:], EPS), writes=['epsc'])
    P.op('pool', lambda e: e.memset(onec[:], 1.0), writes=['onec'])
    P.op('pool', lambda e: e.memset(nhalf[:], -0.5), writes=['nhalf'])
    P.op('pool', lambda e: e.memset(S_f[:], 0.0), writes=['S_f'])
    P.op('pool', lambda e: e.memset(S_b[:], 0.0), writes=['S_b'])
    P.op('pool', lambda e: e.memset(ST_f[:], 0.0), writes=['ST_f'])
    P.op('pool', lambda e: e.memset(ST_b[:], 0.0), writes=['ST_b'])
    P.op('pool', lambda e: e.memset(hal[:], 0.0), writes=[('hal', i) for i in range(12)])
    P.op('pool', lambda e: e.memset(hal2[:], 0.0), writes=[('hal2', i) for i in range(44)])
    for r0 in range(0, PROWS, 128):
        nr = min(128, PROWS - r0)
        P.dma('sp', lambda e, r0=r0, nr=nr: e.dma_start(out=ptmp[0:nr, :], in_=pvec[r0:r0 + nr, :]), writes=['ptmp'], chan=('setup', r0))
        P.op('pe', lambda e, nr=nr: e.transpose(out=acc[0][:, 0:nr], in_=ptmp[0:nr, :], identity=identf[0:nr, 0:nr]),
             reads=['ptmp', 'identf'], writes=['acc0'])
        P.op('dve', lambda e, r0=r0, nr=nr: e.tensor_copy(out=pcol[:, r0:r0 + nr], in_=acc[0][:, 0:nr]), reads=['acc0'], writes=['pcol'])
    P.op('dve', lambda e: e.tensor_tensor(out=dl[:], in0=pcol[:, R_L0:R_L0 + 8], in1=pcol[:, R_L1:R_L1 + 8], op=ALU.subtract),
         reads=['pcol'], writes=['dl'])
    P.op('act', lambda e: e.activation(out=oml[:], in_=dl[:], func=AF.Sigmoid, scale=-1.0), reads=['dl'], writes=['oml'])
    P.op('dve', lambda e: e.tensor_scalar(out=homl[:], in0=oml[:], scalar1=0.5, scalar2=None, op0=ALU.mult), reads=['oml'], writes=['oml'])
    P.op('dve', lambda e: e.tensor_scalar(out=nhoml[:], in0=oml[:], scalar1=-0.5, scalar2=None, op0=ALU.mult), reads=['oml'], writes=['oml'])
    for dst, src, nm in [(m2nw_b, m2nw_d, 'm2nw_b'), (fnw_b, fnw_d, 'fnw_b'), (dtb_b, dtb_d, 'dtb_b'), (aneg_b, alog_d, 'aneg_b'),
                         (dsk_b, dsk_d, 'dsk_b')]:
        P.dma('sp', lambda e, dst=dst, src=src: e.dma_start(out=dst[:], in_=src.partition_broadcast(128)), writes=[nm], chan=('setup', nm))
    P.dma('sp', lambda e: e.dma_start(out=flagc[:], in_=flag_d[:, :]), writes=['flagc'], chan=('setup', 'flag'))
    P.op('act', lambda e: e.activation(out=aneg_b[:], in_=aneg_b[:], func=AF.Exp), reads=['aneg_b'], writes=['aneg_b'])
    P.op('dve', lambda e: e.tensor_scalar(out=aneg_b[:], in0=aneg_b[:], scalar1=-1.0, scalar2=None, op0=ALU.mult),
         reads=['aneg_b'], writes=['aneg_b'])

    pieces = []

    def add_piece(wd, r0, nk, c0, ncols):
        pieces.append((wd, r0, nk, c0, ncols))
        return len(pieces) - 1

    wstate = {'issued': 0}
    live = []

    def issue_to(idx):
        while wstate['issued'] <= min(idx, len(pieces) - 1):
            i = wstate['issued']
            wd, r0, nk, c0, ncols = pieces[i]
            sl = i % NSLOT
            src = wd[r0:r0 + nk * 128, c0:c0 + ncols].rearrange("(k p) n -> p k n", p=128)
            P.dma('pool', lambda e, sl=sl, src=src, nk=nk, ncols=ncols: e.dma_start(out=ws[sl][:, 0:nk, 0:ncols], in_=src),
                  writes=[('ws', sl)], chan=('w', sl))
            wstate['issued'] += 1

    def use_piece(idx, hold=False):
        if not hold:
            for j in live:
                issue_to(j + NSLOT)
            del live[:]
        issue_to(idx)
        live.append(idx)
        sl = idx % NSLOT
        return ws[sl], ('ws', sl)

    sched = []
    sb_modes = ['prefix'] * n_pre + ['main'] * n_sb
    for sb in range(n_pre + n_sb):
        d_ = {}
        pref = sb_modes[sb] == 'prefix'
        for hg in range(2):
            d_[('f', hg)] = add_piece(w_in, 0, 8, 1024 + hg * 512, 512)
            if not pref:
                d_[('q', hg)] = add_piece(w_in, 0, 8, hg * 512, 512)
            d_[('i', hg)] = add_piece(w_in, 0, 8, 2048 + hg * 512, 512)
            if not pref:
                d_[('g', hg)] = add_piece(w_in, 0, 8, 3072 + hg * 512, 512)
        for j in range(3):
            d_[('xbc', j)] = add_piece(w_in, 0, 8, 5120 + j * 512, 512 if not (pref and j == 2) else 256)
        if not pref:
            for j in range(2):
                d_[('z', j)] = add_piece(w_in, 0, 8, 4096 + j * 512, 512)
        d_['dt'] = add_piece(w_in, 0, 8, 6656, 16)
        if pref:
            sched.append(d_)
            continue
        for c2 in range(2):
            for kh in range(2):
                d_[('wo', c2, kh)] = add_piece(w_out, kh * 1024, 8, c2 * 512, 512)
        for j in range(6):
            ncol = 512 if j < 5 else 256
            d_[('ug', j)] = add_piece(w_up, 0, 8, j * 512, ncol)
            d_[('uv', j)] = add_piece(w_up, 0, 8, DFF + j * 512, ncol)
        for c2 in range(2):
            for kp in range(3):
                nk = 8 if kp < 2 else 6
                d_[('wd', c2, kp)] = add_piece(w_down, kp * 1024, nk, c2 * 512, 512)
        sched.append(d_)

    def tap(name, ap, shape, reads):
        if name in taps:
            t = nc.dram_tensor("tap_" + name, list(shape), ap.dtype if hasattr(ap, 'dtype') else F32, kind="ExternalOutput").ap()
            tap_out[name] = t
            P.dma('sp', lambda e: e.dma_start(out=t, in_=ap), reads=reads, chan=('tap', name))

    def stop_pt(name):
        if stop_after == name:
            raise StopBuild()

    issue_to(NSLOT - 1)
    tok0 = 0
    try:
      def do_sb(sb, tok0, out0):
          pref = sb_modes[sb] == 'prefix'
          first = (sb == n_pre)
          T, tiles, chunks, segs = sb_layout(first)
          NCH = len(chunks)
          rmask = rmask0 if first else rmask1
          rmask_n = 'rmask0' if first else 'rmask1'
          pc = sched[sb]
          dbg = (sb == n_pre)

          def mm_fm(wt, wname, cb, rhs_t, rhs_names, evac):
              for (soff, sn) in segs:
                  if sn <= 16:
                      ps, pn = next_mini()
                      pn = [pn]
                  else:
                      ps, pn = next_acc()
                      ps = ps[:, 0:sn]
                      pn = [pn]
                  for k in range(8):
                      P.op('pe', lambda e, ps=ps, k=k, cb=cb, soff=soff, sn=sn: e.matmul(
                          ps, lhsT=wt[:, k, cb * 128:(cb + 1) * 128], rhs=rhs_t[:, k, soff:soff + sn], start=(k == 0), stop=(k == 7)),
                          reads=[wname] + rhs_names, writes=pn)
                  evac(ps, pn, soff, sn)

          def norm_to_fm(r_w, with_load):
              def part_a(ti, off, n):
                  q = ti % 2
                  if with_load:
                      P.dma('sp', lambda e: e.dma_start(out=h[0:n, ti, :], in_=xin[tok0 + off:tok0 + off + n, :]),
                            writes=[('h', ti)], chan=('x', ti))
                  P.op('act', lambda e: e.activation(out=junk[0:n, :], in_=h[0:n, ti, :], func=AF.Square, accum_out=ssn[0:n, ti:ti + 1]),
                       reads=[('h', ti)], writes=['junk', ('ssn', ti)])
                  P.op('act', lambda e: e.activation(out=rsn[0:n, ti:ti + 1], in_=ssn[0:n, ti:ti + 1], func=AF.Sqrt, scale=1.0 / D, bias=epsc[0:n, :]),
                       reads=[('ssn', ti), 'epsc'], writes=[('rsn', ti)])
                  P.op('dve', lambda e: e.reciprocal(out=rsn[0:n, ti:ti + 1], in_=rsn[0:n, ti:ti + 1]), reads=[('rsn', ti)], writes=[('rsn', ti)])
                  P.op('dve', lambda e: e.tensor_scalar(out=xn2[q][0:n, :], in0=h[0:n, ti, :], scalar1=rsn[0:n, ti:ti + 1], scalar2=None,
                                                        op0=ALU.mult), reads=[('h', ti), ('rsn', ti)], writes=[('xn', q)])

              def part_b(ti, off, n):
                  q = ti % 2
                  tb, tbn = pTs[q]
                  for k in range(8):
                      P.op('pe', lambda e, k=k: e.transpose(out=tb[:, k, 0:n], in_=xn2[q][0:n, k * 128:(k + 1) * 128], identity=ident[0:n, 0:n]),
                           reads=[('xn', q), 'ident'], writes=[tbn])
                  P.op('dve', lambda e: e.tensor_tensor(out=uT[:, :, off:off + n], in0=tb[:, :, 0:n],
                                                        in1=pcol[:, r_w:r_w + 8].unsqueeze(2).broadcast_to([128, 8, n]), op=ALU.mult),
                       reads=[tbn, 'pcol'], writes=['uT'])

              nt = len(tiles)
              for i in range(nt + 1):
                  if i < nt:
                      part_a(i, *tiles[i])
                  if i >= 1:
                      part_b(i - 1, *tiles[i - 1])

          A.reset(PH)
          norm_to_fm(R_N1, True)
          if dbg:
              tap('uT', uT[:, :, 0:T], [128, 8, T], ['uT'])
          stop_pt('s0')

          A.reset(PH)
          kinv = A.alloc("kinv", [128, 4, TM], BF16)
          kend = A.alloc("kend", [128, 4, TM], BF16)
          qdec = A.alloc("qdec", [128, 4, TM], BF16)
          sgf = A.alloc("sgf", [128, 4, TM], BF16)
          epos = A.alloc("epos", [128, 4, TM], F32)
          fa = A.alloc("fa", [128, 4, TM], F32)
          fb = A.alloc("fb", [128, 4, TM], F32)
          fc = A.alloc("fc", [128, 4, TM], F32)
          t1 = A.alloc("t1", [128, TM], F32)
          t3 = A.alloc("t3", [128, TM], F32)
          dch = A.alloc("dch", [128, 4, 16], F32)
          v_tm = A.alloc("v_tm", [64, 9, 512], BF16)
          ke_tm = A.alloc("ke_tm", [64, 9, 512], BF16)
          scT = [A.alloc(f"scT{i}", [64, 4, 64], BF16) for i in range(2)]
          o_sb = A.alloc("o_sb", [128, 4, TM], F32)
          osq4 = A.alloc("osq4", [128, 4, TM], BF16)
          rst4 = A.alloc("rst4", [128, 4, TM], F32)
          P.barrier()

          for hg in range(2):
              wt, wn = use_piece(pc[('f', hg)])
              for hl in range(4):
                  mm_fm(wt, wn, hl, uT, ['uT'], lambda ps, pn, soff, sn, hl=hl: P.op(
                      'act', lambda e: e.activation(out=fa[:, hl, soff:soff + sn], in_=ps, func=AF.Tanh, scale=0.5), reads=pn, writes=[('fa', hl)]))
              for hl in range(4):
                  hd = hg * 4 + hl
                  P.op('dve', lambda e, hl=hl, hd=hd: e.tensor_scalar(out=fa[:, hl, 0:T], in0=fa[:, hl, 0:T], scalar1=nhoml[:, hd:hd + 1],
                                                                     scalar2=homl[:, hd:hd + 1], op0=ALU.mult, op1=ALU.add),
                       reads=[('fa', hl), 'oml'], writes=[('fa', hl)])
              for hl in range(4):
                  P.op('act', lambda e, hl=hl: e.activation(out=fb[:, hl, 0:T], in_=fa[:, hl, 0:T], func=AF.Ln, scale=-1.0, bias=onec[:, :]),
                       reads=[('fa', hl), 'onec'], writes=[('fb', hl)])
              for hl in range(4):
                  P.op('dve', lambda e, hl=hl: e.tensor_tensor_scan(out=fc[:, hl, 0:T], data0=rmask[:, 0:T], data1=fb[:, hl, 0:T], initial=0.0,
                                                                    op0=ALU.mult, op1=ALU.add), reads=[('fb', hl), rmask_n], writes=[('fc', hl)])
              for hl in range(4):
                  P.op('act', lambda e, hl=hl: e.activation(out=fb[:, hl, 0:T], in_=fc[:, hl, 0:T], func=AF.Exp, scale=-1.0),
                       reads=[('fc', hl)], writes=[('fb', hl)])
                  P.op('act', lambda e, hl=hl: e.activation(out=epos[:, hl, 0:T], in_=fc[:, hl, 0:T], func=AF.Exp), reads=[('fc', hl)], writes=[('epos', hl)])
              ce0 = chunks[0][0] + chunks[0][1] - 1
              for hl in range(4):
                  P.op('dve', lambda e, hl=hl: e.tensor_tensor(out=kinv[:, hl, 0:T], in0=fa[:, hl, 0:T], in1=fb[:, hl, 0:T], op=ALU.mult),
                       reads=[('fa', hl), ('fb', hl)], writes=[('kinv', hl)])
                  P.op('dve', lambda e, hl=hl: e.tensor_copy(out=dch[:, hl, 0:NCH], in_=epos[:, hl, ce0:T:64]),
                       reads=[('epos', hl)], writes=[('dch', hl)])
                  cs = 0
                  if first:
                      P.op('dve', lambda e, hl=hl: e.tensor_scalar(out=kend[:, hl, 0:16], in0=kinv[:, hl, 0:16], scalar1=dch[:, hl, 0:1], scalar2=None,
                                                                   op0=ALU.mult), reads=[('kinv', hl), ('dch', hl)], writes=[('kend', hl)])
                      cs = 1
                  t0_ = chunks[cs][0]
                  P.op('dve', lambda e, hl=hl, cs=cs, t0_=t0_: e.tensor_tensor(
                      out=kend[:, hl, t0_:T].rearrange("p (c j) -> p c j", j=64), in0=kinv[:, hl, t0_:T].rearrange("p (c j) -> p c j", j=64),
                      in1=dch[:, hl, cs:cs + 8].unsqueeze(2).broadcast_to([128, 8, 64]), op=ALU.mult),
                      reads=[('kinv', hl), ('dch', hl)], writes=[('kend', hl)])
              stop_pt('h_f')
              if not pref:
                wt, wn = use_piece(pc[('q', hg)])
              for hl in range(4 if not pref else 0):
                  mm_fm(wt, wn, hl, uT, ['uT'], lambda ps, pn, soff, sn, hl=hl: P.op(
                      'act', lambda e: e.activation(out=(t1, t3)[hl % 2][:, soff:soff + sn], in_=ps, func=AF.Silu), reads=pn, writes=[('tq', hl % 2)]))
                  P.op('dve', lambda e, hl=hl: e.tensor_tensor(out=qdec[:, hl, 0:T], in0=(t1, t3)[hl % 2][:, 0:T], in1=epos[:, hl, 0:T], op=ALU.mult),
                       reads=[('tq', hl % 2), ('epos', hl)], writes=[('qdec', hl)])
              stop_pt('h_q')
              wt, wn = use_piece(pc[('i', hg)])
              for c, (c0, cn) in enumerate(chunks):
                  ps, pn = next_acc()
                  for k in range(8):
                      P.op('pe', lambda e, ps=ps, k=k, c0=c0, cn=cn, wt=wt: e.matmul(ps[0:cn, :], lhsT=uT[:, k, c0:c0 + cn], rhs=wt[:, k, :],
                                                                                      start=(k == 0), stop=(k == 7)),
                           reads=[wn, 'uT'], writes=[pn])
                  P.op('act', lambda e, ps=ps, c=c, cn=cn: e.activation(out=v_tm[0:cn, c, :], in_=ps[0:cn, :], func=AF.Copy),
                       reads=[pn], writes=[('v_tm', c)])
              stop_pt('h_i')
              if not pref:
                wt, wn = use_piece(pc[('g', hg)])
              for hl in range(4 if not pref else 0):
                  mm_fm(wt, wn, hl, uT, ['uT'], lambda ps, pn, soff, sn, hl=hl: P.op(
                      'act', lambda e: e.activation(out=sgf[:, hl, soff:soff + sn], in_=ps, func=AF.Silu), reads=pn, writes=[('sgf', hl)]))
              stop_pt('h_g')
              for c, (c0, cn) in enumerate(chunks):
                  tb, tbn = pTs[c % 2]
                  for hl in range(4):
                      P.op('pe', lambda e, hl=hl, c0=c0, cn=cn, tb=tb: e.transpose(out=tb[0:cn, hl, :], in_=kend[:, hl, c0:c0 + cn], identity=ident[:, :]),
                           reads=[('kend', hl), 'ident'], writes=[tbn])
                  P.op('dve', lambda e, c=c, cn=cn, tb=tb: e.tensor_copy(out=ke_tm[0:cn, c, :].rearrange("p (h k) -> p h k", h=4), in_=tb[0:cn, 0:4, :]),
                       reads=[tbn], writes=[('ke_tm', c)])
              stop_pt('h_t')
              for c, (c0, cn) in enumerate(chunks):
                  par = c % 2
                  sbk, sbn = [(pb45[:, 0:512], 'pb4'), (acc[0][:, :], 'acc0')][par]
                  obk, obn = [(pb45[:, 512:1024], 'pb5'), (acc[1][:, :], 'acc1')][par]
                  psS = sbk[0:cn, 0:256].rearrange("p (h r) -> p h r", h=4)[:, :, 0:cn]
                  psO = obk[:, 0:256].rearrange("p (h r) -> p h r", h=4)[:, :, 0:cn]
                  for hl in range(4 if not pref else 0):
                      P.op('pe', lambda e, psS=psS, hl=hl, c0=c0, cn=cn: e.matmul(psS[:, hl, :], lhsT=kinv[:, hl, c0:c0 + cn], rhs=qdec[:, hl, c0:c0 + cn],
                                                                                  start=True, stop=True),
                           reads=[('kinv', hl), ('qdec', hl)], writes=[sbn])
                  if not pref:
                   P.op('dve', lambda e, psS=psS, par=par, cn=cn: e.tensor_tensor(
                      out=scT[par][0:cn, :, 0:cn], in0=psS, in1=triU[0:cn, 0:cn].unsqueeze(1).broadcast_to([cn, 4, cn]), op=ALU.mult),
                      reads=[sbn, 'triU'], writes=[('scT', par)])
                  ubk, ubn = [(pb6, 'pb6'), (pb7, 'pb7')][c % 2]
                  for hl in range(4):
                      P.op('pe', lambda e, hl=hl, c=c, cn=cn, ubk=ubk: e.matmul(ubk[:, hl * 128:(hl + 1) * 128], lhsT=ke_tm[0:cn, c, hl * 128:(hl + 1) * 128],
                                                                                rhs=v_tm[0:cn, c, hl * 128:(hl + 1) * 128], start=True, stop=True),
                           reads=[('ke_tm', c), ('v_tm', c)], writes=[ubn])
                  for hl in range(4):
                      hd = hg * 4 + hl
                      P.op('dve', lambda e, hl=hl, hd=hd, c=c, ubk=ubk: e.scalar_tensor_tensor(
                          out=S_f[:, hd, :], in0=S_f[:, hd, :], scalar=dch[:, hl, c:c + 1], in1=ubk[:, hl * 128:(hl + 1) * 128],
                          op0=ALU.mult, op1=ALU.add), reads=[('S_f', hd), ('dch', hl), ubn], writes=[('S_f', hd)])
                  for hl in range(4 if not pref else 0):
                      hd = hg * 4 + hl
                      P.op('pe', lambda e, psO=psO, hl=hl, c=c, cn=cn, par=par: e.matmul(
                          psO[:, hl, :], lhsT=v_tm[0:cn, c, hl * 128:(hl + 1) * 128], rhs=scT[par][0:cn, hl, 0:cn], start=True, stop=False),
                          reads=[('v_tm', c), ('scT', par)], writes=[obn])
                      P.op('pe', lambda e, psO=psO, hl=hl, hd=hd, c0=c0, cn=cn: e.matmul(
                          psO[:, hl, :], lhsT=S_b[:, hd, :], rhs=qdec[:, hl, c0:c0 + cn], start=False, stop=True),
                          reads=[('S_b', hd), ('qdec', hl)], writes=[obn])
                  if not pref:
                   P.op('act', lambda e, psO=psO, c0=c0, cn=cn: e.activation(out=o_sb[:, :, c0:c0 + cn], in_=psO, func=AF.Copy),
                       reads=[obn], writes=['o_sb'])
                  if (not pref) or c == NCH - 1:
                      P.op('act', lambda e, hg=hg: e.activation(out=S_b[:, hg * 4:(hg + 1) * 4, :], in_=S_f[:, hg * 4:(hg + 1) * 4, :], func=AF.Copy),
                           reads=[('S_f', hg * 4 + i) for i in range(4)], writes=[('S_b', hg * 4 + i) for i in range(4)])
              stop_pt('h_scan')
              NH = 4 if not pref else 0
              nbanks = [(pb45[:, 0:512], ['pb4']), (pb45[:, 512:1024], ['pb5']), (pb6[:, :], ['pb6']), (pb7[:, :], ['pb7'])]
              for hl in range(NH):
                  P.op('act', lambda e, hl=hl: e.activation(out=osq4[:, hl, 0:T], in_=o_sb[:, hl, 0:T], func=AF.Square), reads=['o_sb'], writes=[('osq', hl)])
              npss = {}
              for hl in range(NH):
                  for (soff, sn) in segs:
                      if sn <= 16:
                          ps, pn = next_mini()
                          pn = [pn]
                      else:
                          ps, pn = nbanks[hl][0][:, 0:sn], nbanks[hl][1]
                      npss[(hl, soff)] = (ps, pn)
                      P.op('pe', lambda e, ps=ps, soff=soff, sn=sn, hl=hl: e.matmul(ps, lhsT=onesb[:, :], rhs=osq4[:, hl, soff:soff + sn], start=True, stop=True),
                           reads=[('osq', hl), 'onesb'], writes=pn)
                      if sn <= 16:
                          P.op('act', lambda e, ps=ps, soff=soff, sn=sn, hl=hl: e.activation(out=rst4[:, hl, soff:soff + sn], in_=ps, func=AF.Sqrt, scale=1.0 / 128,
                                                                                             bias=epsc[:, :]), reads=pn + ['epsc'], writes=[('rst', hl)])
              for hl in range(NH):
                  for (soff, sn) in segs:
                      if sn > 16:
                          ps, pn = npss[(hl, soff)]
                          P.op('act', lambda e, ps=ps, soff=soff, sn=sn, hl=hl: e.activation(out=rst4[:, hl, soff:soff + sn], in_=ps, func=AF.Sqrt, scale=1.0 / 128,
                                                                                             bias=epsc[:, :]), reads=pn + ['epsc'], writes=[('rst', hl)])
              for hl in range(NH):
                  hd = hg * 4 + hl
                  P.op('dve', lambda e, hl=hl: e.reciprocal(out=rst4[:, hl, 0:T], in_=rst4[:, hl, 0:T]), reads=[('rst', hl)], writes=[('rst', hl)])
                  P.op('dve', lambda e, hl=hl: e.tensor_tensor(out=rst4[:, hl, 0:T], in0=o_sb[:, hl, 0:T], in1=rst4[:, hl, 0:T], op=ALU.mult),
                       reads=['o_sb', ('rst', hl)], writes=[('rst', hl)])
                  P.op('dve', lambda e, hl=hl, hd=hd: e.scalar_tensor_tensor(
                      out=mixT[:, hd, 0:T], in0=rst4[:, hl, 0:T], scalar=pcol[:, R_HGN + hd:R_HGN + hd + 1], in1=sgf[:, hl, 0:T], op0=ALU.mult, op1=ALU.mult),
                      reads=[('rst', hl), 'pcol', ('sgf', hl)], writes=[('mixT', hd)])
          if dbg:
              tap('mixA', mixT[:, 0:8, 0:T], [128, 8, T], [('mixT', i) for i in range(8)])
          stop_pt('hgrn')

          A.reset(PH)
          pre = [A.alloc(f"pre{i}", [128, 3 + TM], BF16) for i in range(4)]
          dg = [A.alloc(f"dg{i}", [128, 4, 128], BF16) for i in range(4)]
          xfs = [A.alloc(f"xf{i}", [128, 4, TM], BF16) for i in range(2)]
          B_fm = A.alloc("B_fm", [128, 2, TM], BF16)
          C_fm = A.alloc("C_fm", [128, 2, TM], BF16)
          x_tm = A.alloc("x_tm", [128, 5, 1024], BF16)
          B_tm = A.alloc("B_tm", [128, 5, 256], BF16)
          zs_tm = A.alloc("zs_tm", [128, 5, 1024], BF16)
          dt_tm = A.alloc("dt_tm", [128, 5, 16], F32)
          da_tm = A.alloc("da_tm", [128, 5, 16], F32)
          lndt = A.alloc("lndt", [128, 5, 16], F32)
          acum, nb, Eexp, cd, dte, w2, dtp, lnv, pol, msk = [A.alloc(nm, [128, 16], F32) for nm in
                                                           ['acum', 'nb', 'Eexp', 'cd', 'dte', 'w2', 'dtp', 'lnv', 'pol', 'msk']]
          rhsb = A.alloc("rhsb", [128, 16, 128], F32)
          LTs = [A.alloc(f"LT{i}", [128, 8, 128], BF16) for i in range(2)]
          MTs = [A.alloc(f"MT{i}", [128, 8, 128], BF16) for i in range(2)]
          xd = A.alloc("xd", [128, 1024], BF16)
          xw = A.alloc("xw", [128, 512], BF16)
          tmpf = A.alloc("tmpf", [128, 512], F32)
          yy = A.alloc("yy", [128, 1024], F32)
          y2 = A.alloc("y2", [128, 1024], F32)
          mo = A.alloc("mo", [128, 1024], BF16)
          P.barrier()

          def conv_fm(wt, wn, cbl, cb, ntap, prebuf, prename, dgbuf, dgname, halbuf, halname, rw, evac, rhs_t=uT, rhs_n='uT', phase=None):
              hw = ntap - 1
              hn = (halname, cb)
              if phase in (None, 'A'):
                  P.op('dve', lambda e: e.tensor_copy(out=prebuf[:, 0:hw], in_=halbuf[:, cb, 0:hw]), reads=[hn], writes=[prename])
                  for k in range(ntap):
                      P.op('dve', lambda e, k=k: e.tensor_scalar(out=dgbuf[:, k, :], in0=ident[:, :], scalar1=pcol[:, rw(k):rw(k) + 1], scalar2=None,
                                                                 op0=ALU.mult), reads=['ident', 'pcol'], writes=[dgname])
                  mm_fm(wt, wn, cbl, rhs_t, [rhs_n], lambda ps, pn, soff, sn: P.op(
                      'act', lambda e: e.activation(out=prebuf[:, hw + soff:hw + soff + sn], in_=ps, func=AF.Copy), reads=pn, writes=[prename]))
                  P.op('dve', lambda e: e.tensor_copy(out=halbuf[:, cb, 0:hw], in_=prebuf[:, T:T + hw]), reads=[prename], writes=[hn])
              if phase in (None, 'B'):
                  for (soff, sn) in segs:
                      if sn <= 16:
                          cps, cpn = next_mini()
                          cpn = [cpn]
                      else:
                          cps, cpn = next_cv()
                          cps = cps[:, 0:sn]
                      for k in range(ntap):
                          P.op('pe', lambda e, cps=cps, k=k, soff=soff, sn=sn: e.matmul(cps, lhsT=dgbuf[:, k, :], rhs=prebuf[:, soff + k:soff + k + sn],
                                                                                       start=(k == 0), stop=(k == ntap - 1)),
                               reads=[dgname, prename], writes=cpn)
                      evac(cps, cpn, soff, sn)

          def run_skewed(tasks, skew):
              nt = len(tasks)
              for i in range(nt + skew):
                  if i < nt:
                      tasks[i]('A')
                  if i - skew >= 0:
                      tasks[i - skew]('B')

          for j in range(3):
              wt, wn = use_piece(pc[('xbc', j)])
              tasks = []
              for cbl in range(4 if not (pref and j == 2) else 2):
                  cb = j * 4 + cbl
                  pi = cb % 4
                  if cb < 8:
                      dst, dname = xfs[j % 2][:, cbl, :], ('xf', j % 2)
                  elif cb < 10:
                      dst, dname = B_fm[:, cb - 8, :], 'B_fm'
                  else:
                      dst, dname = C_fm[:, cb - 10, :], 'C_fm'
                  tasks.append(lambda ph, wt=wt, wn=wn, cbl=cbl, cb=cb, pi=pi, dst=dst, dname=dname: conv_fm(
                      wt, wn, cbl, cb, 4, pre[pi], ('pre', pi), dg[pi], ('dg', pi), hal, 'hal', lambda k, cb=cb: R_MCW + k * 12 + cb,
                      lambda cps, cpn, soff, sn, dst=dst, dname=dname, cb=cb: P.op(
                          'act', lambda e: e.activation(out=dst[:, soff:soff + sn], in_=cps, func=AF.Silu, bias=pcol[:, R_MCB + cb:R_MCB + cb + 1]),
                          reads=cpn + ['pcol'], writes=[dname]), phase=ph))
              run_skewed(tasks, 1)
              if j < 2:
                  for ti, (off, n) in enumerate(tiles):
                      tb, tbn = pTm[ti % 2]
                      for cbl in range(4):
                          P.op('pe', lambda e, cbl=cbl, off=off, n=n, tb=tb, j=j: e.transpose(out=tb[0:n, cbl, :], in_=xfs[j % 2][:, cbl, off:off + n], identity=ident[:, :]),
                               reads=[('xf', j % 2), 'ident'], writes=[tbn])
                      P.op('dve', lambda e, ti=ti, n=n, j=j, tb=tb: e.tensor_copy(
                          out=x_tm[0:n, ti, j * 512:(j + 1) * 512].rearrange("p (c k) -> p c k", c=4), in_=tb[0:n, 0:4, :]),
                          reads=[tbn], writes=[('x_tm', ti)])
              else:
                  for ti, (off, n) in enumerate(tiles):
                      tb, tbn = pTm[ti % 2]
                      for g in range(2):
                          P.op('pe', lambda e, g=g, off=off, n=n, tb=tb: e.transpose(out=tb[0:n, g, :], in_=B_fm[:, g, off:off + n], identity=ident[:, :]),
                               reads=['B_fm', 'ident'], writes=[tbn])
                      P.op('dve', lambda e, ti=ti, n=n, tb=tb: e.tensor_copy(out=B_tm[0:n, ti, :].rearrange("p (c k) -> p c k", c=2), in_=tb[0:n, 0:2, :]),
                           reads=[tbn], writes=[('B_tm', ti)])
          for j in range(2 if not pref else 0):
              wt, wn = use_piece(pc[('z', j)])
              for ti, (off, n) in enumerate(tiles):
                  ps, pn = next_acc()
                  for k in range(8):
                      P.op('pe', lambda e, ps=ps, k=k, off=off, n=n, wt=wt: e.matmul(ps[0:n, :], lhsT=uT[:, k, off:off + n], rhs=wt[:, k, :],
                                                                                      start=(k == 0), stop=(k == 7)), reads=[wn, 'uT'], writes=[pn])
                  P.op('act', lambda e, ps=ps, ti=ti, n=n, j=j: e.activation(out=zs_tm[0:n, ti, j * 512:(j + 1) * 512], in_=ps[0:n, :], func=AF.Silu),
                       reads=[pn], writes=[('zs', ti)])
          wt, wn = use_piece(pc['dt'])
          for ti, (off, n) in enumerate(tiles):
              ps, pn = next_mini()
              for k in range(8):
                  P.op('pe', lambda e, ps=ps, k=k, off=off, n=n, wt=wt: e.matmul(ps[0:n, :], lhsT=uT[:, k, off:off + n], rhs=wt[:, k, 0:16],
                                                                                  start=(k == 0), stop=(k == 7)), reads=[wn, 'uT'], writes=[pn])
              P.op('dve', lambda e, ps=ps, n=n: e.tensor_tensor(out=dtp[0:n, :], in0=ps[0:n, :], in1=dtb_b[0:n, :], op=ALU.add),
                   reads=[pn, 'dtb_b'], writes=['dtp'])
              P.op('act', lambda e, n=n: e.activation(out=dtp[0:n, :], in_=dtp[0:n, :], func=AF.Exp), reads=['dtp'], writes=['dtp'])
              P.op('act', lambda e, n=n: e.activation(out=lnv[0:n, :], in_=dtp[0:n, :], func=AF.Ln, bias=onec[0:n, :]), reads=['dtp', 'onec'], writes=['lnv'])
              P.op('dve', lambda e, n=n: e.tensor_scalar(out=pol[0:n, :], in0=dtp[0:n, :], scalar1=1.0 / 7.0, scalar2=None, op0=ALU.mult), reads=['dtp'], writes=['pol'])
              for cc in (-1.0 / 6.0, 0.2, -0.25, 1.0 / 3.0, -0.5, 1.0):
                  P.op('dve', lambda e, n=n, cc=cc: e.scalar_tensor_tensor(out=pol[0:n, :], in0=pol[0:n, :], scalar=cc, in1=dtp[0:n, :], op0=ALU.add, op1=ALU.mult),
                       reads=['pol', 'dtp'], writes=['pol'])
              P.op('dve', lambda e, n=n: e.tensor_single_scalar(out=msk[0:n, :], in_=dtp[0:n, :], scalar=0.125, op=ALU.is_lt), reads=['dtp'], writes=['msk'])
              P.op('dve', lambda e, n=n: e.tensor_tensor(out=pol[0:n, :], in0=pol[0:n, :], in1=lnv[0:n, :], op=ALU.subtract), reads=['pol', 'lnv'], writes=['pol'])
              P.op('dve', lambda e, n=n: e.tensor_tensor(out=pol[0:n, :], in0=pol[0:n, :], in1=msk[0:n, :], op=ALU.mult), reads=['pol', 'msk'], writes=['pol'])
              P.op('dve', lambda e, n=n, ti=ti: e.tensor_tensor(out=dt_tm[0:n, ti, :], in0=pol[0:n, :], in1=lnv[0:n, :], op=ALU.add),
                   reads=['pol', 'lnv'], writes=[('dt', ti)])
              P.op('dve', lambda e, n=n, ti=ti: e.tensor_tensor(out=da_tm[0:n, ti, :], in0=dt_tm[0:n, ti, :], in1=aneg_b[0:n, :], op=ALU.mult),
                   reads=[('dt', ti), 'aneg_b'], writes=[('da', ti)])
              P.op('act', lambda e, n=n, ti=ti: e.activation(out=lndt[0:n, ti, :], in_=dt_tm[0:n, ti, :], func=AF.Ln), reads=[('dt', ti)], writes=[('lndt', ti)])
          stop_pt('m_proj')

          h8 = lambda ap: ap.rearrange("p (h q) -> p h q", q=64)
          for ti, (off, n) in enumerate(tiles):
              if not pref:
                P.op('dve', lambda e, ti=ti, n=n: e.tensor_tensor(out=h8(xd[0:n, :]), in0=h8(x_tm[0:n, ti, :]),
                                                                  in1=dsk_b[0:n, :].unsqueeze(2).broadcast_to([n, 16, 64]), op=ALU.mult),
                     reads=[('x_tm', ti), 'dsk_b'], writes=['xd'])
              psA, pnA = next_mini()
              P.op('pe', lambda e, psA=psA, ti=ti, n=n: e.matmul(psA[0:n, :], lhsT=triU[0:n, 0:n], rhs=da_tm[0:n, ti, :], start=True, stop=True),
                   reads=['triU', ('da', ti)], writes=[pnA])
              psL, pnL = next_mini()
              P.op('pe', lambda e, psL=psL, ti=ti, n=n: e.matmul(psL[:, :], lhsT=onesf[0:n, :], rhs=da_tm[0:n, ti, :], start=True, stop=True),
                   reads=['onesf', ('da', ti)], writes=[pnL])
              P.op('dve', lambda e, psA=psA, n=n: e.tensor_copy(out=acum[0:n, :], in_=psA[0:n, :]), reads=[pnA], writes=['acum'])
              if not pref:
                P.op('dve', lambda e, psA=psA, n=n, ti=ti: e.tensor_tensor(out=nb[0:n, :], in0=lndt[0:n, ti, :], in1=psA[0:n, :], op=ALU.subtract),
                   reads=[pnA, ('lndt', ti)], writes=['nb'])
                P.op('act', lambda e, psA=psA, n=n: e.activation(out=Eexp[0:n, :], in_=psA[0:n, :], func=AF.Exp), reads=[pnA], writes=['Eexp'])
              P.op('act', lambda e, psL=psL: e.activation(out=cd[:, :], in_=psL[:, :], func=AF.Exp), reads=[pnL], writes=['cd'])
              P.op('dve', lambda e, psL=psL, n=n: e.tensor_tensor(out=dte[0:n, :], in0=psL[0:n, :], in1=acum[0:n, :], op=ALU.subtract),
                   reads=[pnL, 'acum'], writes=['dte'])
              P.op('act', lambda e, n=n: e.activation(out=dte[0:n, :], in_=dte[0:n, :], func=AF.Exp), reads=['dte'], writes=['dte'])
              P.op('dve', lambda e, n=n, ti=ti: e.tensor_tensor(out=w2[0:n, :], in0=dte[0:n, :], in1=dt_tm[0:n, ti, :], op=ALU.mult),
                   reads=['dte', ('dt', ti)], writes=['w2'])
              if not pref:
               P.op('dve', lambda e, n=n, ti=ti: e.tensor_tensor(out=rhsb[0:n, :, 0:n], in0=triU[0:n, 0:n].unsqueeze(1).broadcast_to([n, 16, n]),
                                                                in1=da_tm[0:n, ti, :].unsqueeze(2).broadcast_to([n, 16, n]), op=ALU.mult),
                   reads=['triU', ('da', ti)], writes=['rhsb'])
              def gp1(g):
                  pcb = pb2[0:n, 256 + g * 128:256 + g * 128 + n]
                  if not pref:
                      gbk = [((pb45[:, 0:512], 'pb4'), (pb45[:, 512:1024], 'pb5')), ((acc[0][:, :], 'acc0'), (acc[1][:, :], 'acc1'))][g]
                      if n == 128:
                          for hh in range(2):
                              outp, bn = gbk[hh]
                              h0 = g * 8 + hh * 4
                              P.op('pe', lambda e, outp=outp, h0=h0: e.matmul(outp, lhsT=onesf[:, :], rhs=rhsb[:, h0:h0 + 4, :].rearrange("p h i -> p (h i)"),
                                                                              start=True, stop=False), reads=['onesf', 'rhsb'], writes=[bn])
                              P.op('pe', lambda e, outp=outp: e.matmul(outp, lhsT=ident[:, :], rhs=mneg[:, :, :].rearrange("p h i -> p (h i)"), start=False, stop=True),
                                   reads=['ident', 'mneg'], writes=[bn])
                      else:
                          for hl in range(8):
                              outp = gbk[hl // 4][0][0:n, (hl % 4) * 128:(hl % 4) * 128 + n]
                              bn = gbk[hl // 4][1]
                              P.op('pe', lambda e, outp=outp, n=n, hl=hl, g=g: e.matmul(outp, lhsT=onesf[0:n, 0:n], rhs=rhsb[0:n, g * 8 + hl, 0:n], start=True, stop=False),
                                   reads=['onesf', 'rhsb'], writes=[bn])
                              P.op('pe', lambda e, outp=outp, n=n: e.matmul(outp, lhsT=ident[0:n, 0:n], rhs=mneg[0:n, 0, 0:n], start=False, stop=True),
                                   reads=['ident', 'mneg'], writes=[bn])
                      P.op('pe', lambda e, pcb=pcb, g=g, off=off, n=n: e.matmul(pcb, lhsT=B_fm[:, g, off:off + n], rhs=C_fm[:, g, off:off + n], start=True, stop=True),
                           reads=['B_fm', 'C_fm'], writes=['pb2'])
                      P.op('pe', lambda e, g=g, off=off, n=n: e.matmul(yob[g][0][0:n, :], lhsT=C_fm[:, g, off:off + n], rhs=ST_b[:, g, :], start=True, stop=True),
                           reads=['C_fm', ('ST_b', g)], writes=[yob[g][1]])

              def gp2(g):
                  pcb = pb2[0:n, 256 + g * 128:256 + g * 128 + n]
                  P.op('dve', lambda e, g=g, n=n, ti=ti: e.tensor_tensor(out=h8(xw[0:n, :]), in0=h8(x_tm[0:n, ti, g * 512:(g + 1) * 512]),
                                                                         in1=w2[0:n, g * 8:(g + 1) * 8].unsqueeze(2).broadcast_to([n, 8, 64]), op=ALU.mult),
                       reads=[('x_tm', ti), 'w2'], writes=['xw'])
                  if not pref:
                      P.op('dve', lambda e, g=g, n=n: e.tensor_tensor(out=h8(tmpf[0:n, :]), in0=h8(yob[g][0][0:n, :]),
                                                                      in1=Eexp[0:n, g * 8:(g + 1) * 8].unsqueeze(2).broadcast_to([n, 8, 64]), op=ALU.mult),
                           reads=[yob[g][1], 'Eexp'], writes=['tmpf'])
                  psU, pnU = yob[g]
                  P.op('pe', lambda e, psU=psU, g=g, n=n, ti=ti: e.matmul(psU[:, :], lhsT=B_tm[0:n, ti, g * 128:(g + 1) * 128], rhs=xw[0:n, :], start=True, stop=True),
                       reads=[('B_tm', ti), 'xw'], writes=[pnU])
                  P.op('dve', lambda e, g=g: e.tensor_tensor(out=h8(ST_f[:, g, :]), in0=h8(ST_f[:, g, :]),
                                                             in1=cd[:, g * 8:(g + 1) * 8].unsqueeze(2).broadcast_to([128, 8, 64]), op=ALU.mult),
                       reads=[('ST_f', g), 'cd'], writes=[('ST_f', g)])
                  P.op('dve', lambda e, g=g, psU=psU: e.tensor_tensor(out=ST_f[:, g, :], in0=ST_f[:, g, :], in1=psU[:, :], op=ALU.add),
                       reads=[('ST_f', g), pnU], writes=[('ST_f', g)])
                  if not pref:
                      gbk = [((pb45[:, 0:512], 'pb4'), (pb45[:, 512:1024], 'pb5')), ((acc[0][:, :], 'acc0'), (acc[1][:, :], 'acc1'))][g]
                      for hl in range(8):
                          hh = g * 8 + hl
                          bk, bn = gbk[hl // 4]
                          P.op('act', lambda e, hl=hl, hh=hh, n=n, bk=bk: e.activation(out=LTs[g][0:n, hl, 0:n], in_=bk[0:n, (hl % 4) * 128:(hl % 4) * 128 + n], func=AF.Exp,
                                                                                      bias=nb[0:n, hh:hh + 1]), reads=[bn, 'nb'], writes=[('LT', g)])
                  P.op('act', lambda e, g=g: e.activation(out=ST_b[:, g, :], in_=ST_f[:, g, :], func=AF.Copy), reads=[('ST_f', g)], writes=[('ST_b', g)])

              def gp3(g):
                  pcb = pb2[0:n, 256 + g * 128:256 + g * 128 + n]
                  if not pref:
                      P.op('dve', lambda e, pcb=pcb, n=n: e.tensor_tensor(out=MTs[g][0:n, :, 0:n], in0=LTs[g][0:n, :, 0:n], in1=pcb.unsqueeze(1).broadcast_to([n, 8, n]),
                                                                          op=ALU.mult), reads=[('LT', g), 'pb2'], writes=[('MT', g)])
                      P.op('pe', lambda e, g=g, n=n: e.matmul(pb6[0:n, :], lhsT=ident[0:n, 0:n], rhs=xd[0:n, g * 512:(g + 1) * 512], start=True, stop=False),
                           reads=['ident', 'xd'], writes=['pb6'])
                      for hl in range(8):
                          hh = g * 8 + hl
                          P.op('pe', lambda e, hl=hl, hh=hh, n=n, ti=ti: e.matmul(pb6[0:n, hl * 64:(hl + 1) * 64], lhsT=MTs[g][0:n, hl, 0:n],
                                                                                  rhs=x_tm[0:n, ti, hh * 64:(hh + 1) * 64], start=False, stop=(hl == 7)),
                               reads=[('MT', g), ('x_tm', ti)], writes=['pb6'])
                      P.op('dve', lambda e, g=g, n=n: e.tensor_tensor(out=yy[0:n, g * 512:(g + 1) * 512], in0=pb6[0:n, :], in1=tmpf[0:n, :], op=ALU.add),
                           reads=['pb6', 'tmpf'], writes=['yy'])

              if pref:
                  gp2(0)
                  gp2(1)
              else:
                  gp1(0)
                  gp2(0)
                  gp1(1)
                  gp3(0)
                  gp2(1)
                  gp3(1)
              if pref:
                  continue
              if dbg and ti == 1:
                  tap('y_raw1', yy[:, :], [128, 1024], ['yy'])
              P.op('dve', lambda e, n=n, ti=ti: e.tensor_tensor(out=y2[0:n, :], in0=yy[0:n, :], in1=zs_tm[0:n, ti, :], op=ALU.mult),
                   reads=['yy', ('zs', ti)], writes=['y2'])
              for g in range(2):
                  P.op('act', lambda e, g=g, n=n: e.activation(out=junk[0:n, 0:512], in_=y2[0:n, g * 512:(g + 1) * 512], func=AF.Square,
                                                               accum_out=ss[0:n, g:g + 1]), reads=['y2'], writes=['junk', 'ss'])
              P.op('dve', lambda e, n=n: e.tensor_scalar(out=rs[0:n, 0:2], in0=ss[0:n, 0:2], scalar1=1.0 / 512, scalar2=EPS, op0=ALU.mult, op1=ALU.add),
                   reads=['ss'], writes=['rs'])
              P.op('pool', lambda e, n=n: e.tensor_tensor(out=rs[0:n, 0:2], in0=rs[0:n, 0:2], in1=nhalf[0:n, 0:2], op=ALU.pow), reads=['rs', 'nhalf'], writes=['rs'])
              for g in range(2):
                  P.op('dve', lambda e, g=g, n=n: e.scalar_tensor_tensor(out=mo[0:n, g * 512:(g + 1) * 512], in0=y2[0:n, g * 512:(g + 1) * 512],
                                                                         scalar=rs[0:n, g:g + 1], in1=m2nw_b[0:n, g * 512:(g + 1) * 512],
                                                                         op0=ALU.mult, op1=ALU.mult), reads=['y2', 'rs', 'm2nw_b'], writes=['mo'])
              for j in range(8):
                  P.op('pe', lambda e, j=j, n=n: e.transpose(out=pT[:, j, 0:n], in_=mo[0:n, j * 128:(j + 1) * 128], identity=ident[0:n, 0:n]),
                       reads=['mo', 'ident'], writes=['pT'])
              P.op('act', lambda e, off=off, n=n: e.activation(out=mixT[:, 8:16, off:off + n], in_=pT[:, :, 0:n], func=AF.Copy),
                   reads=['pT'], writes=[('mixT', 8 + j) for j in range(8)])
          if dbg:
              tap('mixB', mixT[:, 8:16, 0:T], [128, 8, T], [('mixT', 8 + j) for j in range(8)])
          stop_pt('mamba')
          if pref:
              return T

          A.reset(PH)
          for c2 in range(2):
              w0, n0 = use_piece(pc[('wo', c2, 0)])
              w1, n1 = use_piece(pc[('wo', c2, 1)], hold=True)
              for ti, (off, n) in enumerate(tiles):
                  ps, pn = next_acc()
                  for kk in range(16):
                      wt, wn = (w0, n0) if kk < 8 else (w1, n1)
                      P.op('pe', lambda e, ps=ps, kk=kk, off=off, n=n, wt=wt: e.matmul(ps[0:n, :], lhsT=mixT[:, kk, off:off + n], rhs=wt[:, kk % 8, :],
                                                                                        start=(kk == 0), stop=(kk == 15)),
                           reads=[('mixT', kk), wn], writes=[pn])
                  P.op('dve', lambda e, ps=ps, ti=ti, n=n, c2=c2: e.tensor_tensor(out=h[0:n, ti, c2 * 512:(c2 + 1) * 512], in0=h[0:n, ti, c2 * 512:(c2 + 1) * 512],
                                                                                  in1=ps[0:n, :], op=ALU.add), reads=[('h', ti), pn], writes=[('h', ti)])
          if dbg:
              tap('hmid1', h[:, 1, :], [128, 1024], [('h', 1)])
          norm_to_fm(R_N2, False)
          stop_pt('oproj')

          pre2 = [A.alloc(f"pre2{i}", [128, 2 + TM], BF16) for i in range(4)]
          dg2 = [A.alloc(f"dg2{i}", [128, 3, 128], BF16) for i in range(4)]
          gs = [A.alloc(f"gs{i}", [128, TM], BF16) for i in range(4)]
          aT = A.alloc("aT", [128, 22, TM], BF16)
          ob = [A.alloc(f"ob{i}", [128, D], F32) for i in range(2)]
          P.barrier()
          tasks = []
          for j in range(6):
              ncb = 4 if j < 5 else 2
              for cbl in range(ncb):
                  cbg = j * 4 + cbl
                  cbv = 22 + cbg
                  gi = cbg % 4
                  pg, pv = (2 * cbg) % 4, (2 * cbg + 1) % 4
                  tasks.append(lambda ph, j=j, cbl=cbl, cbg=cbg, gi=gi, pg=pg: conv_fm(
                      wsl[('ug', j)][0], wsl[('ug', j)][1], cbl, cbg, 3, pre2[pg], ('pre2', pg), dg2[pg], ('dg2', pg), hal2, 'hal2',
                      lambda k, cbg=cbg: R_FCW + k * 44 + cbg,
                      lambda cps, cpn, soff, sn, gi=gi, cbg=cbg: P.op(
                          'act', lambda e: e.activation(out=gs[gi][:, soff:soff + sn], in_=cps, func=AF.Silu, bias=pcol[:, R_FCB + cbg:R_FCB + cbg + 1]),
                          reads=cpn + ['pcol'], writes=[('gs', gi)]), phase=ph))
                  tasks.append(lambda ph, j=j, cbl=cbl, cbg=cbg, cbv=cbv, gi=gi, pv=pv: conv_fm(
                      wsl[('uv', j)][0], wsl[('uv', j)][1], cbl, cbv, 3, pre2[pv], ('pre2', pv), dg2[pv], ('dg2', pv), hal2, 'hal2',
                      lambda k, cbv=cbv: R_FCW + k * 44 + cbv,
                      lambda cps, cpn, soff, sn, gi=gi, cbg=cbg, cbv=cbv: P.op(
                          'dve', lambda e: e.scalar_tensor_tensor(out=aT[:, cbg, soff:soff + sn], in0=cps, scalar=pcol[:, R_FCB + cbv:R_FCB + cbv + 1],
                                                                  in1=gs[gi][:, soff:soff + sn], op0=ALU.add, op1=ALU.mult),
                          reads=cpn + ['pcol', ('gs', gi)], writes=[('aT', cbg)]), phase=ph))
          wsl = {}
          nt = len(tasks)
          for i in range(nt + 2):
              if i < nt and i % 8 == 0:
                  j = i // 8
                  wsl[('ug', j)] = use_piece(pc[('ug', j)])
                  wsl[('uv', j)] = use_piece(pc[('uv', j)], hold=True)
              if i < nt:
                  tasks[i]('A')
              if i - 2 >= 0:
                  tasks[i - 2]('B')
          stop_pt('ffn_up')

          dbanks = [(acc[0][:, :], 'acc0'), (acc[1][:, :], 'acc1'), (pb45[:, 0:512], 'pb4'), (pb45[:, 512:1024], 'pb5'), (pb6[:, :], 'pb6')]
          for c2 in range(2):
              for kp in range(3):
                  wt, wn = use_piece(pc[('wd', c2, kp)])
                  nk = 8 if kp < 2 else 6
                  for ti, (off, n) in enumerate(tiles):
                      ps, pn = dbanks[ti]
                      for k in range(nk):
                          kk = kp * 8 + k
                          P.op('pe', lambda e, ps=ps, kk=kk, k=k, off=off, n=n, wt=wt: e.matmul(ps[0:n, :], lhsT=aT[:, kk, off:off + n], rhs=wt[:, k, :],
                                                                                                 start=(kk == 0), stop=(kk == 21)),
                               reads=[('aT', kk), wn], writes=[pn])
              for ti, (off, n) in enumerate(tiles):
                  ps, pn = dbanks[ti]
                  P.op('dve', lambda e, ps=ps, ti=ti, n=n, c2=c2: e.tensor_tensor(out=h[0:n, ti, c2 * 512:(c2 + 1) * 512], in0=h[0:n, ti, c2 * 512:(c2 + 1) * 512],
                                                                                  in1=ps[0:n, :], op=ALU.add), reads=[('h', ti), pn], writes=[('h', ti)])
          for ti, (off, n) in enumerate(tiles):
              oi = ti % 2
              P.op('act', lambda e, ti=ti, n=n: e.activation(out=junk[0:n, :], in_=h[0:n, ti, :], func=AF.Square, accum_out=ssn[0:n, ti:ti + 1]),
                   reads=[('h', ti)], writes=['junk', ('ssn', ti)])
              P.op('act', lambda e, ti=ti, n=n: e.activation(out=rsn[0:n, ti:ti + 1], in_=ssn[0:n, ti:ti + 1], func=AF.Sqrt, scale=1.0 / D, bias=epsc[0:n, :]),
                   reads=[('ssn', ti), 'epsc'], writes=[('rsn', ti)])
              P.op('dve', lambda e, ti=ti, n=n: e.reciprocal(out=rsn[0:n, ti:ti + 1], in_=rsn[0:n, ti:ti + 1]), reads=[('rsn', ti)], writes=[('rsn', ti)])
              P.op('dve', lambda e, ti=ti, n=n, oi=oi: e.scalar_tensor_tensor(out=ob[oi][0:n, :], in0=h[0:n, ti, :], scalar=rsn[0:n, ti:ti + 1], in1=fnw_b[0:n, :],
                                                                              op0=ALU.mult, op1=ALU.mult), reads=[('h', ti), ('rsn', ti), 'fnw_b'], writes=[('ob', oi)])
              P.dma('sp', lambda e, oi=oi, off=off, n=n, out0=out0: e.dma_start(out=out[out0 + off:out0 + off + n, :], in_=ob[oi][0:n, :]),
                    reads=[('ob', oi)], chan=('o', oi))

          return T

      out0 = 0
      for sb_i in range((n_pre + n_sb) if stop_after != 'setup' else 0):
          T_ = do_sb(sb_i, tok0, out0)
          tok0 += T_
          if sb_modes[sb_i] == 'main':
              out0 += T_
          if sb_i == n_pre - 1:
              for tns, nm in [(S_f, 'S_f'), (S_b, 'S_b')]:
                  P.op('dve', lambda e, tns=tns: e.tensor_scalar(out=tns[:], in0=tns[:], scalar1=flagc[:, 0:1], scalar2=None, op0=ALU.mult),
                       reads=[(nm, i) for i in range(8)] + ['flagc'], writes=[(nm, i) for i in range(8)])
              for tns, nm in [(ST_f, 'ST_f'), (ST_b, 'ST_b')]:
                  P.op('dve', lambda e, tns=tns: e.tensor_scalar(out=tns[:], in0=tns[:], scalar1=flagc[:, 0:1], scalar2=None, op0=ALU.mult),
                       reads=[(nm, i) for i in range(2)] + ['flagc'], writes=[(nm, i) for i in range(2)])
              P.op('dve', lambda e: e.tensor_scalar(out=hal[:], in0=hal[:], scalar1=flagc[:, 0:1], scalar2=None, op0=ALU.mult),
                   reads=[('hal', i) for i in range(12)] + ['flagc'], writes=[('hal', i) for i in range(12)])
    except StopBuild:
        pass

    tap('pcol', pcol[:, :], [128, PROWS], ['pcol'])
    tap('oml', oml[:, :], [128, 8], ['oml'])
    P.wait_all('sp', [k for k in P.count if isinstance(k, tuple) and k[0] == 'd'])
    P.emit()
    return nc, tap_out


def pack_shared(inp):
    f = lambda a: np.ascontiguousarray(np.asarray(a, dtype=np.float32))
    rows = [f(inp['norm1_w'][0]).reshape(8, 128), f(inp['hg_lb_logits'][0]).reshape(8, 128), f(inp['hg_lb_logits'][1]).reshape(8, 128),
            f(inp['hg_norm_w'][0]).reshape(8, 128), f(inp['norm2_w'][0]).reshape(8, 128), f(inp['m2_conv_w'][0]).reshape(48, 128),
            f(inp['m2_conv_b'][0]).reshape(12, 128), f(inp['ffn_conv_w'][0]).reshape(132, 128), f(inp['ffn_conv_b'][0]).reshape(44, 128)]
    pvec = np.ascontiguousarray(np.concatenate(rows, 0))
    assert pvec.shape == (PROWS, 128)
    return {
        'w_in': f(inp['w_in'][0]), 'w_out': f(inp['w_out'][0]), 'w_up': f(inp['ffn_w_up'][0]), 'w_down': f(inp['ffn_w_down'][0]),
        'pvec': pvec, 'm2nw': f(inp['m2_norm_w'][0]), 'fnw': f(inp['final_norm_w']), 'dtb': f(inp['m2_dt_bias'][0]),
        'alog': f(inp['m2_a_log'][0]), 'dsk': f(inp['m2_d'][0]),
    }


def core_tokens(inp, b, TOK):
    seq = np.concatenate([np.asarray(inp['meta_tokens'], np.float32), np.asarray(inp['x'][b], np.float32)], 0)
    return np.ascontiguousarray(seq[:TOK])


_CACHE = {}
N_PRE, N_SB = 4, 4


def kernel(**inputs):
    TOKP, TOKM = 512 * N_PRE, 16 + 512 * N_SB
    if 'nc' not in _CACHE:
        _CACHE['nc'] = build_program(N_SB, n_pre=N_PRE)[0]
    nc = _CACHE['nc']
    shared = pack_shared(inputs)
    meta = np.asarray(inputs['meta_tokens'], np.float32)
    in_maps = []
    for c in range(8):
        b, half = c // 2, c % 2
        seq = np.concatenate([meta, np.asarray(inputs['x'][b], np.float32)], 0)
        m = dict(shared)
        if half == 0:
            m['xin'] = np.ascontiguousarray(np.concatenate([np.zeros((TOKP, D), np.float32), seq[0:TOKM]], 0))
            m['flag'] = np.zeros((128, 1), np.float32)
        else:
            m['xin'] = np.ascontiguousarray(seq[0:TOKP + TOKM])
            m['flag'] = np.ones((128, 1), np.float32)
        in_maps.append(m)
    res = run_bass_kernel_spmd(nc, in_maps, core_ids=list(range(8)))
    outs = []
    for b in range(4):
        outs.append(np.concatenate([res.results[2 * b]['out'][NMETA:], res.results[2 * b + 1]['out'][NMETA:]], 0))
    return np.stack(outs, 0).astype(np.float32)
```

```python
import numpy as np
import concourse.bass as bass
import concourse.mybir as mybir
from concourse.bass_utils import run_bass_kernel_spmd

F32 = mybir.dt.float32
BF16 = mybir.dt.bfloat16
ALU = mybir.AluOpType
AF = mybir.ActivationFunctionType

ENGS = ['pe', 'dve', 'act', 'pool', 'sp']
SAME_ENG_RAW = ('dve', 'act', 'pool')
SAME_ENG_ALL = True
EPS = 1e-6
NMETA = 16
D = 1024
DPROJ = 6672
DFF = 2816


class Prog:
    def __init__(self, nc):
        self.nc = nc
        self.streams = {e: [] for e in ENGS}
        self.sems = {}
        self.count = {}
        self.seen = {e: {} for e in ENGS}
        self.buf = {}

    def _sem(self, key):
        if key not in self.sems:
            name = "s_" + "_".join(str(k) for k in (key if isinstance(key, tuple) else (key,)))
            name = name.replace("(", "").replace(")", "").replace(",", "_").replace(" ", "").replace("'", "")
            self.sems[key] = self.nc.alloc_semaphore(name)
            self.count[key] = 0
        return self.sems[key]

    def _need(self, eng, reads, writes):
        need = {}

        def add(kv, raw):
            if kv is None:
                return
            k, v = kv
            if k == eng and not ((raw or SAME_ENG_ALL) and eng in SAME_ENG_RAW):
                return
            if v > need.get(k, 0):
                need[k] = v

        for b in reads:
            st = self.buf.get(b)
            if st:
                add(st[0], True)
        for b in writes:
            st = self.buf.get(b)
            if st:
                add(st[0], False)
                for k, v in st[1].items():
                    add((k, v), False)
        seen = self.seen[eng]
        for k, v in need.items():
            if v > seen.get(k, 0):
                self.streams[eng].append(('wait', k, v))
                seen[k] = v

    def _mark(self, key, val, reads, writes):
        for b in reads:
            st = self.buf.setdefault(b, [None, {}])
            st[1][key] = val
        for b in writes:
            self.buf[b] = [(key, val), {}]

    def op(self, eng, fn, reads=(), writes=()):
        self._need(eng, reads, writes)
        self._sem(eng)
        self.count[eng] += 1
        self.streams[eng].append(('op', fn, eng, 1))
        self._mark(eng, self.count[eng], reads, writes)

    def dma(self, q, fn, reads=(), writes=(), chan=None):
        self._need(q, reads, writes)
        key = ('d', chan)
        self._sem(key)
        self.count[key] += 16
        self.streams[q].append(('op', fn, key, 16))
        self._mark(key, self.count[key], reads, writes)

    def barrier(self):
        for e in ENGS:
            for k, v in self.count.items():
                if k == e:
                    continue
                if isinstance(k, tuple) and k[0] == 'd' and isinstance(k[1], tuple) and k[1][0] in ('w', 'x', 'setup'):
                    continue
                if v > self.seen[e].get(k, 0):
                    self.streams[e].append(('wait', k, v))
                    self.seen[e][k] = v

    def wait_all(self, eng, keys):
        for k in keys:
            v = self.count.get(k, 0)
            if v > self.seen[eng].get(k, 0):
                self.streams[eng].append(('wait', k, v))
                self.seen[eng][k] = v

    def emit(self):
        nc = self.nc
        engmap = {'pe': 'tensor', 'dve': 'vector', 'act': 'scalar', 'pool': 'gpsimd', 'sp': 'sync'}
        with nc.Block() as block:
            for e in ENGS:
                stream = self.streams[e]
                if not stream:
                    continue

                def body(engine, stream=stream):
                    for item in stream:
                        if item[0] == 'wait':
                            engine.wait_ge(self.sems[item[1]], item[2])
                        else:
                            ins = item[1](engine)
                            ins.then_inc(self.sems[item[2]], item[3])

                getattr(block, engmap[e])(body)


class StopBuild(Exception):
    pass


class Arena:
    def __init__(self, nc):
        self.nc = nc
        self.base = (int(nc.sbuf_base) + 63) // 64 * 64
        self.top = int(nc.sbuf_top)
        self.cur = self.base
        self.n = 0
        self.hi = self.base

    def alloc(self, name, shape, dtype):
        esz = 4 if dtype == F32 else 2
        nbytes = esz
        for s in shape[1:]:
            nbytes *= s
        nbytes = (nbytes + 63) // 64 * 64
        off = self.cur
        self.cur += nbytes
        self.hi = max(self.hi, self.cur)
        assert self.cur <= self.top, f"SBUF overflow at {name}: {self.cur} > {self.top}"
        self.n += 1
        return self.nc.alloc_sbuf_tensor_at(f"{name}_{self.n}", list(shape), dtype, offset=off)

    def mark(self):
        return self.cur

    def reset(self, m):
        self.cur = m


def sb_layout(first):
    if first:
        tiles = [(0, 16)] + [(16 + 128 * j, 128) for j in range(4)]
        chunks = [(0, 16)] + [(16 + 64 * j, 64) for j in range(8)]
        segs = [(0, 16), (16, 512)]
        T = 528
    else:
        tiles = [(128 * j, 128) for j in range(4)]
        chunks = [(64 * j, 64) for j in range(8)]
        segs = [(0, 512)]
        T = 512
    return T, tiles, chunks, segs


R_N1, R_L0, R_L1, R_HGN, R_N2, R_MCW, R_MCB, R_FCW, R_FCB, PROWS = 0, 8, 16, 24, 32, 40, 88, 100, 232, 276


def build_program(n_sb, taps=None, stop_after=None, n_pre=0):
    taps = taps or []
    TOKP = 512 * n_pre
    TOKM = 16 + 512 * n_sb
    TOK = TOKP + TOKM
    nc = bass.Bass("TRN2", target_bir_lowering=False)
    dr = lambda name, shape, kind="ExternalInput": nc.dram_tensor(name, shape, F32, kind=kind).ap()
    xin = dr("xin", [TOK, D])
    w_in = dr("w_in", [D, DPROJ])
    w_out = dr("w_out", [2 * D, D])
    w_up = dr("w_up", [D, 2 * DFF])
    w_down = dr("w_down", [DFF, D])
    pvec = dr("pvec", [PROWS, 128])
    m2nw_d = dr("m2nw", [D])
    fnw_d = dr("fnw", [D])
    dtb_d = dr("dtb", [16])
    alog_d = dr("alog", [16])
    dsk_d = dr("dsk", [16])
    flag_d = dr("flag", [128, 1])
    out = dr("out", [TOKM, D], kind="ExternalOutput")
    tap_out = {}

    P = Prog(nc)
    A = Arena(nc)
    TM = 528

    acc = [nc.alloc_psum_tensor(f"acc{i}", [128, 512], F32) for i in range(2)]
    pb2 = nc.alloc_psum_tensor("pb2", [128, 512], F32)
    pT = nc.alloc_psum_tensor("pT", [128, 8, 128], BF16)
    pb45 = nc.alloc_psum_tensor("pb45", [128, 1024], F32)
    pb6 = nc.alloc_psum_tensor("pb6", [128, 512], F32)
    pb7 = nc.alloc_psum_tensor("pb7", [128, 512], F32)
    PB4 = ['pb4']
    PB5 = ['pb5']
    PB6 = ['pb6']
    PB7 = ['pb7']
    pT7 = pb7[:, :].bitcast(BF16).rearrange("p (k c) -> p k c", k=8)
    pTs = [(pT, 'pT'), (pT7, 'pb7')]
    pT2v = pb2[:, :].bitcast(BF16).rearrange("p (k c) -> p k c", k=8)
    pTm = [(pT, 'pT'), (pT2v, 'pb2')]
    pTf = pT[:, :, :].rearrange("p k c -> p (k c)").bitcast(F32)
    yob = [(pb7[:, :], 'pb7'), (pTf, 'pT')]
    st = {'acc': 0, 'mini': 0, 'cv': 0}

    def next_acc():
        i = st['acc'] % 2
        st['acc'] += 1
        return acc[i], f'acc{i}'

    def next_mini(alt=False):
        i = st['mini'] % 16
        st['mini'] += 1
        if alt and (st['mini'] % 2 == 0):
            return pTf[:, i * 16:(i + 1) * 16], 'pT'
        return pb2[:, i * 16:(i + 1) * 16], 'pb2'


    cvbanks = [(pb45[:, 0:512], PB4), (pb45[:, 512:1024], PB5), (pb6[:, :], PB6), (pb7[:, :], PB7)]

    def next_cv():
        i = st['cv'] % 4
        st['cv'] += 1
        return cvbanks[i]

    h = A.alloc("h", [128, 5, D], F32)
    S_f = A.alloc("S_f", [128, 8, 128], F32)
    S_b = A.alloc("S_b", [128, 8, 128], BF16)
    ST_f = A.alloc("ST_f", [128, 2, 512], F32)
    ST_b = A.alloc("ST_b", [128, 2, 512], BF16)
    hal = A.alloc("hal", [128, 12, 4], BF16)
    hal2 = A.alloc("hal2", [128, 44, 2], BF16)
    identf = A.alloc("identf", [128, 128], F32)
    ident = A.alloc("ident", [128, 128], BF16)
    onesf = A.alloc("onesf", [128, 128], F32)
    onesb = A.alloc("onesb", [128, 128], BF16)
    triU = A.alloc("triU", [128, 128], F32)
    mneg = A.alloc("mneg", [128, 4, 128], BF16)
    rmask0 = A.alloc("rmask0", [128, 528], F32)
    rmask1 = A.alloc("rmask1", [128, 512], F32)
    pcol = A.alloc("pcol", [128, PROWS], F32)
    oml = A.alloc("oml", [128, 8], F32)
    homl = A.alloc("homl", [128, 8], F32)
    nhoml = A.alloc("nhoml", [128, 8], F32)
    dl = A.alloc("dl", [128, 8], F32)
    m2nw_b = A.alloc("m2nw_b", [128, D], F32)
    fnw_b = A.alloc("fnw_b", [128, D], F32)
    dtb_b = A.alloc("dtb_b", [128, 16], F32)
    aneg_b = A.alloc("aneg_b", [128, 16], F32)
    dsk_b = A.alloc("dsk_b", [128, 16], F32)
    epsc = A.alloc("epsc", [128, 1], F32)
    onec = A.alloc("onec", [128, 1], F32)
    nhalf = A.alloc("nhalf", [128, 2], F32)
    flagc = A.alloc("flagc", [128, 1], F32)
    ptmp = A.alloc("ptmp", [128, 128], F32)
    NSLOT = 4
    ws = [A.alloc(f"ws{i}", [128, 8, 512], BF16) for i in range(NSLOT)]
    uT = A.alloc("uT", [128, 8, TM], BF16)
    mixT = A.alloc("mixT", [128, 16, TM], BF16)
    junk = A.alloc("junk", [128, D], BF16)
    xn = A.alloc("xn", [128, D], BF16)
    xn2 = [xn, A.alloc("xnb", [128, D], BF16)]
    ssn = A.alloc("ssn", [128, 8], F32)
    rsn = A.alloc("rsn", [128, 8], F32)
    ss = A.alloc("ss", [128, 4], F32)
    rs = A.alloc("rs", [128, 4], F32)
    PH = A.mark()

    P.op('pool', lambda e: e.memset(identf[:], 1.0), writes=['identf'])
    P.op('pool', lambda e: e.affine_select(out=identf[:], in_=identf[:], pattern=[[-1, 128]], compare_op=ALU.is_equal,
                                           fill=0.0, base=0, channel_multiplier=1), reads=['identf'], writes=['identf'])
    P.op('pool', lambda e: e.tensor_copy(out=ident[:], in_=identf[:]), reads=['identf'], writes=['ident'])
    P.op('pool', lambda e: e.memset(onesf[:], 1.0), writes=['onesf'])
    P.op('pool', lambda e: e.memset(onesb[:], 1.0), writes=['onesb'])
    P.op('pool', lambda e: e.memset(triU[:], 1.0), writes=['triU'])
    P.op('pool', lambda e: e.affine_select(out=triU[:], in_=triU[:], pattern=[[1, 128]], compare_op=ALU.is_ge,
                                           fill=0.0, base=0, channel_multiplier=-1), reads=['triU'], writes=['triU'])
    P.op('pool', lambda e: e.memset(mneg[:], 0.0), writes=['mneg'])
    P.op('pool', lambda e: e.affine_select(out=mneg[:], in_=mneg[:], pattern=[[0, 4], [1, 128]], compare_op=ALU.is_ge,
                                           fill=-30000.0, base=0, channel_multiplier=-1), reads=['mneg'], writes=['mneg'])
    P.op('pool', lambda e: e.memset(rmask0[:], 1.0), writes=['rmask0'])
    P.op('pool', lambda e: e.memset(rmask0[:, 0:1], 0.0), writes=['rmask0'])
    P.op('pool', lambda e: e.memset(rmask0[:, 16:528:64], 0.0), writes=['rmask0'])
    P.op('pool', lambda e: e.memset(rmask1[:], 1.0), writes=['rmask1'])
    P.op('pool', lambda e: e.memset(rmask1[:, 0:512:64], 0.0), writes=['rmask1'])
    P.op('pool', lambda e: e.memset(epsc[:], EPS), writes=['epsc'])
    P.op('pool', lambda e: e.memset(onec[:], 1.0), writes=['onec'])
    P.op('pool', lambda e: e.memset(nhalf[:], -0.5), writes=['nhalf'])
    P.op('pool', lambda e: e.memset(S_f[:], 0.0), writes=['S_f'])
    P.op('pool', lambda e: e.memset(S_b[:], 0.0), writes=['S_b'])
    P.op('pool', lambda e: e.memset(ST_f[:], 0.0), writes=['ST_f'])
    P.op('pool', lambda e: e.memset(ST_b[:], 0.0), writes=['ST_b'])
    P.op('pool', lambda e: e.memset(hal[:], 0.0), writes=[('hal', i) for i in range(12)])
    P.op('pool', lambda e: e.memset(hal2[:], 0.0), writes=[('hal2', i) for i in range(44)])
    for r0 in range(0, PROWS, 128):
        nr = min(128, PROWS - r0)
        P.dma('sp', lambda e, r0=r0, nr=nr: e.dma_start(out=ptmp[0:nr, :], in_=pvec[r0:r0 + nr, :]), writes=['ptmp'], chan=('setup', r0))
        P.op('pe', lambda e, nr=nr: e.transpose(out=acc[0][:, 0:nr], in_=ptmp[0:nr, :], identity=identf[0:nr, 0:nr]),
             reads=['ptmp', 'identf'], writes=['acc0'])
        P.op('dve', lambda e, r0=r0, nr=nr: e.tensor_copy(out=pcol[:, r0:r0 + nr], in_=acc[0][:, 0:nr]), reads=['acc0'], writes=['pcol'])
    P.op('dve', lambda e: e.tensor_tensor(out=dl[:], in0=pcol[:, R_L0:R_L0 + 8], in1=pcol[:, R_L1:R_L1 + 8], op=ALU.subtract),
         reads=['pcol'], writes=['dl'])
    P.op('act', lambda e: e.activation(out=oml[:], in_=dl[:], func=AF.Sigmoid, scale=-1.0), reads=['dl'], writes=['oml'])
    P.op('dve', lambda e: e.tensor_scalar(out=homl[:], in0=oml[:], scalar1=0.5, scalar2=None, op0=ALU.mult), reads=['oml'], writes=['oml'])
    P.op('dve', lambda e: e.tensor_scalar(out=nhoml[:], in0=oml[:], scalar1=-0.5, scalar2=None, op0=ALU.mult), reads=['oml'], writes=['oml'])
    for dst, src, nm in [(m2nw_b, m2nw_d, 'm2nw_b'), (fnw_b, fnw_d, 'fnw_b'), (dtb_b, dtb_d, 'dtb_b'), (aneg_b, alog_d, 'aneg_b'),
                         (dsk_b, dsk_d, 'dsk_b')]:
        P.dma('sp', lambda e, dst=dst, src=src: e.dma_start(out=dst[:], in_=src.partition_broadcast(128)), writes=[nm], chan=('setup', nm))
    P.dma('sp', lambda e: e.dma_start(out=flagc[:], in_=flag_d[:, :]), writes=['flagc'], chan=('setup', 'flag'))
    P.op('act', lambda e: e.activation(out=aneg_b[:], in_=aneg_b[:], func=AF.Exp), reads=['aneg_b'], writes=['aneg_b'])
    P.op('dve', lambda e: e.tensor_scalar(out=aneg_b[:], in0=aneg_b[:], scalar1=-1.0, scalar2=None, op0=ALU.mult),
         reads=['aneg_b'], writes=['aneg_b'])

    pieces = []

    def add_piece(wd, r0, nk, c0, ncols):
        pieces.append((wd, r0, nk, c0, ncols))
        return len(pieces) - 1

    wstate = {'issued': 0}
    live = []

    def issue_to(idx):
        while wstate['issued'] <= min(idx, len(pieces) - 1):
            i = wstate['issued']
            wd, r0, nk, c0, ncols = pieces[i]
            sl = i % NSLOT
            src = wd[r0:r0 + nk * 128, c0:c0 + ncols].rearrange("(k p) n -> p k n", p=128)
            P.dma('pool', lambda e, sl=sl, src=src, nk=nk, ncols=ncols: e.dma_start(out=ws[sl][:, 0:nk, 0:ncols], in_=src),
                  writes=[('ws', sl)], chan=('w', sl))
            wstate['issued'] += 1

    def use_piece(idx, hold=False):
        if not hold:
            for j in live:
                issue_to(j + NSLOT)
            del live[:]
        issue_to(idx)
        live.append(idx)
        sl = idx % NSLOT
        return ws[sl], ('ws', sl)

    sched = []
    sb_modes = ['prefix'] * n_pre + ['main'] * n_sb
    for sb in range(n_pre + n_sb):
        d_ = {}
        pref = sb_modes[sb] == 'prefix'
        for hg in range(2):
            d_[('f', hg)] = add_piece(w_in, 0, 8, 1024 + hg * 512, 512)
            if not pref:
                d_[('q', hg)] = add_piece(w_in, 0, 8, hg * 512, 512)
            d_[('i', hg)] = add_piece(w_in, 0, 8, 2048 + hg * 512, 512)
            if not pref:
                d_[('g', hg)] = add_piece(w_in, 0, 8, 3072 + hg * 512, 512)
        for j in range(3):
            d_[('xbc', j)] = add_piece(w_in, 0, 8, 5120 + j * 512, 512 if not (pref and j == 2) else 256)
        if not pref:
            for j in range(2):
                d_[('z', j)] = add_piece(w_in, 0, 8, 4096 + j * 512, 512)
        d_['dt'] = add_piece(w_in, 0, 8, 6656, 16)
        if pref:
            sched.append(d_)
            continue
        for c2 in range(2):
            for kh in range(2):
                d_[('wo', c2, kh)] = add_piece(w_out, kh * 1024, 8, c2 * 512, 512)
        for j in range(6):
            ncol = 512 if j < 5 else 256
            d_[('ug', j)] = add_piece(w_up, 0, 8, j * 512, ncol)
            d_[('uv', j)] = add_piece(w_up, 0, 8, DFF + j * 512, ncol)
        for c2 in range(2):
            for kp in range(3):
                nk = 8 if kp < 2 else 6
                d_[('wd', c2, kp)] = add_piece(w_down, kp * 1024, nk, c2 * 512, 512)
        sched.append(d_)

    def tap(name, ap, shape, reads):
        if name in taps:
            t = nc.dram_tensor("tap_" + name, list(shape), ap.dtype if hasattr(ap, 'dtype') else F32, kind="ExternalOutput").ap()
            tap_out[name] = t
            P.dma('sp', lambda e: e.dma_start(out=t, in_=ap), reads=reads, chan=('tap', name))

    def stop_pt(name):
        if stop_after == name:
            raise StopBuild()

    issue_to(NSLOT - 1)
    tok0 = 0
    try:
      def do_sb(sb, tok0, out0):
          pref = sb_modes[sb] == 'prefix'
          first = (sb == n_pre)
          T, tiles, chunks, segs = sb_layout(first)
          NCH = len(chunks)
          rmask = rmask0 if first else rmask1
          rmask_n = 'rmask0' if first else 'rmask1'
          pc = sched[sb]
          dbg = (sb == n_pre)

          def mm_fm(wt, wname, cb, rhs_t, rhs_names, evac):
              for (soff, sn) in segs:
                  if sn <= 16:
                      ps, pn = next_mini(alt=True)
                      pn = [pn]
                  else:
                      ps, pn = next_acc()
                      ps = ps[:, 0:sn]
                      pn = [pn]
                  for k in range(8):
                      P.op('pe', lambda e, ps=ps, k=k, cb=cb, soff=soff, sn=sn: e.matmul(
                          ps, lhsT=wt[:, k, cb * 128:(cb + 1) * 128], rhs=rhs_t[:, k, soff:soff + sn], start=(k == 0), stop=(k == 7)),
                          reads=[wname] + rhs_names, writes=pn)
                  evac(ps, pn, soff, sn)

          def norm_to_fm(r_w, with_load):
              def part_a(ti, off, n):
                  q = ti % 2
                  if with_load:
                      P.dma('sp', lambda e: e.dma_start(out=h[0:n, ti, :], in_=xin[tok0 + off:tok0 + off + n, :]),
                            writes=[('h', ti)], chan=('x', ti))
                  P.op('act', lambda e: e.activation(out=junk[0:n, :], in_=h[0:n, ti, :], func=AF.Square, accum_out=ssn[0:n, ti:ti + 1]),
                       reads=[('h', ti)], writes=['junk', ('ssn', ti)])
                  P.op('act', lambda e: e.activation(out=rsn[0:n, ti:ti + 1], in_=ssn[0:n, ti:ti + 1], func=AF.Sqrt, scale=1.0 / D, bias=epsc[0:n, :]),
                       reads=[('ssn', ti), 'epsc'], writes=[('rsn', ti)])
                  P.op('dve', lambda e: e.reciprocal(out=rsn[0:n, ti:ti + 1], in_=rsn[0:n, ti:ti + 1]), reads=[('rsn', ti)], writes=[('rsn', ti)])
                  P.op('dve', lambda e: e.tensor_scalar(out=xn2[q][0:n, :], in0=h[0:n, ti, :], scalar1=rsn[0:n, ti:ti + 1], scalar2=None,
                                                        op0=ALU.mult), reads=[('h', ti), ('rsn', ti)], writes=[('xn', q)])

              def part_b(ti, off, n):
                  q = ti % 2
                  tb, tbn = pTs[q]
                  for k in range(8):
                      P.op('pe', lambda e, k=k: e.transpose(out=tb[:, k, 0:n], in_=xn2[q][0:n, k * 128:(k + 1) * 128], identity=ident[0:n, 0:n]),
                           reads=[('xn', q), 'ident'], writes=[tbn])
                  P.op('dve', lambda e: e.tensor_tensor(out=uT[:, :, off:off + n], in0=tb[:, :, 0:n],
                                                        in1=pcol[:, r_w:r_w + 8].unsqueeze(2).broadcast_to([128, 8, n]), op=ALU.mult),
                       reads=[tbn, 'pcol'], writes=['uT'])

              nt = len(tiles)
              for i in range(nt + 1):
                  if i < nt:
                      part_a(i, *tiles[i])
                  if i >= 1:
                      part_b(i - 1, *tiles[i - 1])

          A.reset(PH)
          norm_to_fm(R_N1, True)
          if dbg:
              tap('uT', uT[:, :, 0:T], [128, 8, T], ['uT'])
          stop_pt('s0')

          A.reset(PH)
          kinv = A.alloc("kinv", [128, 4, TM], BF16)
          kend = A.alloc("kend", [128, 4, TM], BF16)
          qdec = A.alloc("qdec", [128, 4, TM], BF16)
          sgf = A.alloc("sgf", [128, 4, TM], BF16)
          epos = A.alloc("epos", [128, 4, TM], F32)
          fa = A.alloc("fa", [128, 4, TM], F32)
          fb = A.alloc("fb", [128, 4, TM], F32)
          fc = A.alloc("fc", [128, 4, TM], F32)
          t1 = A.alloc("t1", [128, TM], F32)
          t3 = A.alloc("t3", [128, TM], F32)
          dch = A.alloc("dch", [128, 4, 16], F32)
          v_tm = A.alloc("v_tm", [64, 9, 512], BF16)
          ke_tm = A.alloc("ke_tm", [64, 9, 512], BF16)
          scT = [A.alloc(f"scT{i}", [64, 4, 64], BF16) for i in range(2)]
          o_sb = A.alloc("o_sb", [128, 4, TM], F32)
          osq4 = A.alloc("osq4", [128, 4, TM], BF16)
          rst4 = A.alloc("rst4", [128, 4, TM], F32)
          P.barrier()

          for hg in range(2):
              wt, wn = use_piece(pc[('f', hg)])
              for hl in range(4):
                  mm_fm(wt, wn, hl, uT, ['uT'], lambda ps, pn, soff, sn, hl=hl: P.op(
                      'act', lambda e: e.activation(out=fa[:, hl, soff:soff + sn], in_=ps, func=AF.Tanh, scale=0.5), reads=pn, writes=[('fa', hl)]))
              for hl in range(4):
                  hd = hg * 4 + hl
                  P.op('dve', lambda e, hl=hl, hd=hd: e.tensor_scalar(out=fa[:, hl, 0:T], in0=fa[:, hl, 0:T], scalar1=nhoml[:, hd:hd + 1],
                                                                     scalar2=homl[:, hd:hd + 1], op0=ALU.mult, op1=ALU.add),
                       reads=[('fa', hl), 'oml'], writes=[('fa', hl)])
              for hl in range(4):
                  P.op('act', lambda e, hl=hl: e.activation(out=fb[:, hl, 0:T], in_=fa[:, hl, 0:T], func=AF.Ln, scale=-1.0, bias=onec[:, :]),
                       reads=[('fa', hl), 'onec'], writes=[('fb', hl)])
              for hl in range(4):
                  P.op('dve', lambda e, hl=hl: e.tensor_tensor_scan(out=fc[:, hl, 0:T], data0=rmask[:, 0:T], data1=fb[:, hl, 0:T], initial=0.0,
                                                                    op0=ALU.mult, op1=ALU.add), reads=[('fb', hl), rmask_n], writes=[('fc', hl)])
              for hl in range(4):
                  P.op('act', lambda e, hl=hl: e.activation(out=fb[:, hl, 0:T], in_=fc[:, hl, 0:T], func=AF.Exp, scale=-1.0),
                       reads=[('fc', hl)], writes=[('fb', hl)])
                  P.op('act', lambda e, hl=hl: e.activation(out=epos[:, hl, 0:T], in_=fc[:, hl, 0:T], func=AF.Exp), reads=[('fc', hl)], writes=[('epos', hl)])
              ce0 = chunks[0][0] + chunks[0][1] - 1
              for hl in range(4):
                  P.op('dve', lambda e, hl=hl: e.tensor_tensor(out=kinv[:, hl, 0:T], in0=fa[:, hl, 0:T], in1=fb[:, hl, 0:T], op=ALU.mult),
                       reads=[('fa', hl), ('fb', hl)], writes=[('kinv', hl)])
                  P.op('dve', lambda e, hl=hl: e.tensor_copy(out=dch[:, hl, 0:NCH], in_=epos[:, hl, ce0:T:64]),
                       reads=[('epos', hl)], writes=[('dch', hl)])
                  cs = 0
                  if first:
                      P.op('dve', lambda e, hl=hl: e.tensor_scalar(out=kend[:, hl, 0:16], in0=kinv[:, hl, 0:16], scalar1=dch[:, hl, 0:1], scalar2=None,
                                                                   op0=ALU.mult), reads=[('kinv', hl), ('dch', hl)], writes=[('kend', hl)])
                      cs = 1
                  t0_ = chunks[cs][0]
                  P.op('dve', lambda e, hl=hl, cs=cs, t0_=t0_: e.tensor_tensor(
                      out=kend[:, hl, t0_:T].rearrange("p (c j) -> p c j", j=64), in0=kinv[:, hl, t0_:T].rearrange("p (c j) -> p c j", j=64),
                      in1=dch[:, hl, cs:cs + 8].unsqueeze(2).broadcast_to([128, 8, 64]), op=ALU.mult),
                      reads=[('kinv', hl), ('dch', hl)], writes=[('kend', hl)])
              stop_pt('h_f')
              if not pref:
                wt, wn = use_piece(pc[('q', hg)])
              for hl in range(4 if not pref else 0):
                  mm_fm(wt, wn, hl, uT, ['uT'], lambda ps, pn, soff, sn, hl=hl: P.op(
                      'act', lambda e: e.activation(out=(t1, t3)[hl % 2][:, soff:soff + sn], in_=ps, func=AF.Silu), reads=pn, writes=[('tq', hl % 2)]))
                  P.op('dve', lambda e, hl=hl: e.tensor_tensor(out=qdec[:, hl, 0:T], in0=(t1, t3)[hl % 2][:, 0:T], in1=epos[:, hl, 0:T], op=ALU.mult),
                       reads=[('tq', hl % 2), ('epos', hl)], writes=[('qdec', hl)])
              stop_pt('h_q')
              wt, wn = use_piece(pc[('i', hg)])
              for c, (c0, cn) in enumerate(chunks):
                  ps, pn = next_acc()
                  for k in range(8):
                      P.op('pe', lambda e, ps=ps, k=k, c0=c0, cn=cn, wt=wt: e.matmul(ps[0:cn, :], lhsT=uT[:, k, c0:c0 + cn], rhs=wt[:, k, :],
                                                                                      start=(k == 0), stop=(k == 7)),
                           reads=[wn, 'uT'], writes=[pn])
                  P.op('act', lambda e, ps=ps, c=c, cn=cn: e.activation(out=v_tm[0:cn, c, :], in_=ps[0:cn, :], func=AF.Copy),
                       reads=[pn], writes=[('v_tm', c)])
              stop_pt('h_i')
              if not pref:
                wt, wn = use_piece(pc[('g', hg)])
              for hl in range(4 if not pref else 0):
                  mm_fm(wt, wn, hl, uT, ['uT'], lambda ps, pn, soff, sn, hl=hl: P.op(
                      'act', lambda e: e.activation(out=sgf[:, hl, soff:soff + sn], in_=ps, func=AF.Silu), reads=pn, writes=[('sgf', hl)]))
              stop_pt('h_g')
              for c, (c0, cn) in enumerate(chunks):
                  tb, tbn = pTs[c % 2]
                  for hl in range(4):
                      P.op('pe', lambda e, hl=hl, c0=c0, cn=cn, tb=tb: e.transpose(out=tb[0:cn, hl, :], in_=kend[:, hl, c0:c0 + cn], identity=ident[:, :]),
                           reads=[('kend', hl), 'ident'], writes=[tbn])
                  P.op('dve', lambda e, c=c, cn=cn, tb=tb: e.tensor_copy(out=ke_tm[0:cn, c, :].rearrange("p (h k) -> p h k", h=4), in_=tb[0:cn, 0:4, :]),
                       reads=[tbn], writes=[('ke_tm', c)])
              stop_pt('h_t')
              for c, (c0, cn) in enumerate(chunks):
                  par = c % 2
                  sbk, sbn = [(pb45[:, 0:512], 'pb4'), (acc[0][:, :], 'acc0')][par]
                  obk, obn = [(pb45[:, 512:1024], 'pb5'), (acc[1][:, :], 'acc1')][par]
                  psS = sbk[0:cn, 0:256].rearrange("p (h r) -> p h r", h=4)[:, :, 0:cn]
                  psO = obk[:, 0:256].rearrange("p (h r) -> p h r", h=4)[:, :, 0:cn]
                  for hl in range(4 if not pref else 0):
                      P.op('pe', lambda e, psS=psS, hl=hl, c0=c0, cn=cn: e.matmul(psS[:, hl, :], lhsT=kinv[:, hl, c0:c0 + cn], rhs=qdec[:, hl, c0:c0 + cn],
                                                                                  start=True, stop=True),
                           reads=[('kinv', hl), ('qdec', hl)], writes=[sbn])
                  if not pref:
                   P.op('dve', lambda e, psS=psS, par=par, cn=cn: e.tensor_tensor(
                      out=scT[par][0:cn, :, 0:cn], in0=psS, in1=triU[0:cn, 0:cn].unsqueeze(1).broadcast_to([cn, 4, cn]), op=ALU.mult),
                      reads=[sbn, 'triU'], writes=[('scT', par)])
                  ubk, ubn = [(pb6, 'pb6'), (pb7, 'pb7')][c % 2]
                  for hl in range(4):
                      P.op('pe', lambda e, hl=hl, c=c, cn=cn, ubk=ubk: e.matmul(ubk[:, hl * 128:(hl + 1) * 128], lhsT=ke_tm[0:cn, c, hl * 128:(hl + 1) * 128],
                                                                                rhs=v_tm[0:cn, c, hl * 128:(hl + 1) * 128], start=True, stop=True),
                           reads=[('ke_tm', c), ('v_tm', c)], writes=[ubn])
                  for hl in range(4):
                      hd = hg * 4 + hl
                      P.op('dve', lambda e, hl=hl, hd=hd, c=c, ubk=ubk: e.scalar_tensor_tensor(
                          out=S_f[:, hd, :], in0=S_f[:, hd, :], scalar=dch[:, hl, c:c + 1], in1=ubk[:, hl * 128:(hl + 1) * 128],
                          op0=ALU.mult, op1=ALU.add), reads=[('S_f', hd), ('dch', hl), ubn], writes=[('S_f', hd)])
                  for hl in range(4 if not pref else 0):
                      hd = hg * 4 + hl
                      P.op('pe', lambda e, psO=psO, hl=hl, c=c, cn=cn, par=par: e.matmul(
                          psO[:, hl, :], lhsT=v_tm[0:cn, c, hl * 128:(hl + 1) * 128], rhs=scT[par][0:cn, hl, 0:cn], start=True, stop=False),
                          reads=[('v_tm', c), ('scT', par)], writes=[obn])
                      P.op('pe', lambda e, psO=psO, hl=hl, hd=hd, c0=c0, cn=cn: e.matmul(
                          psO[:, hl, :], lhsT=S_b[:, hd, :], rhs=qdec[:, hl, c0:c0 + cn], start=False, stop=True),
                          reads=[('S_b', hd), ('qdec', hl)], writes=[obn])
                  if not pref:
                   P.op('act', lambda e, psO=psO, c0=c0, cn=cn: e.activation(out=o_sb[:, :, c0:c0 + cn], in_=psO, func=AF.Copy),
                       reads=[obn], writes=['o_sb'])
                  if (not pref) or c == NCH - 1:
                      P.op('act', lambda e, hg=hg: e.activation(out=S_b[:, hg * 4:(hg + 1) * 4, :], in_=S_f[:, hg * 4:(hg + 1) * 4, :], func=AF.Copy),
                           reads=[('S_f', hg * 4 + i) for i in range(4)], writes=[('S_b', hg * 4 + i) for i in range(4)])
              stop_pt('h_scan')
              NH = 4 if not pref else 0
              nbanks = [(pb45[:, 0:512], ['pb4']), (pb45[:, 512:1024], ['pb5']), (pb6[:, :], ['pb6']), (pb7[:, :], ['pb7'])]
              for hl in range(NH):
                  P.op('act', lambda e, hl=hl: e.activation(out=osq4[:, hl, 0:T], in_=o_sb[:, hl, 0:T], func=AF.Square), reads=['o_sb'], writes=[('osq', hl)])
              npss = {}
              for hl in range(NH):
                  for (soff, sn) in segs:
                      if sn <= 16:
                          ps, pn = next_mini()
                          pn = [pn]
                      else:
                          ps, pn = nbanks[hl][0][:, 0:sn], nbanks[hl][1]
                      npss[(hl, soff)] = (ps, pn)
                      P.op('pe', lambda e, ps=ps, soff=soff, sn=sn, hl=hl: e.matmul(ps, lhsT=onesb[:, :], rhs=osq4[:, hl, soff:soff + sn], start=True, stop=True),
                           reads=[('osq', hl), 'onesb'], writes=pn)
                      if sn <= 16:
                          P.op('act', lambda e, ps=ps, soff=soff, sn=sn, hl=hl: e.activation(out=rst4[:, hl, soff:soff + sn], in_=ps, func=AF.Sqrt, scale=1.0 / 128,
                                                                                             bias=epsc[:, :]), reads=pn + ['epsc'], writes=[('rst', hl)])
              for hl in range(NH):
                  for (soff, sn) in segs:
                      if sn > 16:
                          ps, pn = npss[(hl, soff)]
                          P.op('act', lambda e, ps=ps, soff=soff, sn=sn, hl=hl: e.activation(out=rst4[:, hl, soff:soff + sn], in_=ps, func=AF.Sqrt, scale=1.0 / 128,
                                                                                             bias=epsc[:, :]), reads=pn + ['epsc'], writes=[('rst', hl)])
              for hl in range(NH):
                  hd = hg * 4 + hl
                  P.op('dve', lambda e, hl=hl: e.reciprocal(out=rst4[:, hl, 0:T], in_=rst4[:, hl, 0:T]), reads=[('rst', hl)], writes=[('rst', hl)])
                  P.op('dve', lambda e, hl=hl: e.tensor_tensor(out=rst4[:, hl, 0:T], in0=o_sb[:, hl, 0:T], in1=rst4[:, hl, 0:T], op=ALU.mult),
                       reads=['o_sb', ('rst', hl)], writes=[('rst', hl)])
                  P.op('dve', lambda e, hl=hl, hd=hd: e.scalar_tensor_tensor(
                      out=mixT[:, hd, 0:T], in0=rst4[:, hl, 0:T], scalar=pcol[:, R_HGN + hd:R_HGN + hd + 1], in1=sgf[:, hl, 0:T], op0=ALU.mult, op1=ALU.mult),
                      reads=[('rst', hl), 'pcol', ('sgf', hl)], writes=[('mixT', hd)])
          if dbg:
              tap('mixA', mixT[:, 0:8, 0:T], [128, 8, T], [('mixT', i) for i in range(8)])
          stop_pt('hgrn')

          A.reset(PH)
          pre = [A.alloc(f"pre{i}", [128, 3 + TM], BF16) for i in range(4)]
          dg = [A.alloc(f"dg{i}", [128, 4, 128], BF16) for i in range(4)]
          xfs = [A.alloc(f"xf{i}", [128, 4, TM], BF16) for i in range(2)]
          B_fm = A.alloc("B_fm", [128, 2, TM], BF16)
          C_fm = A.alloc("C_fm", [128, 2, TM], BF16)
          x_tm = A.alloc("x_tm", [128, 5, 1024], BF16)
          B_tm = A.alloc("B_tm", [128, 5, 256], BF16)
          zs_tm = A.alloc("zs_tm", [128, 5, 1024], BF16)
          dt_tm = A.alloc("dt_tm", [128, 5, 16], F32)
          da_tm = A.alloc("da_tm", [128, 5, 16], F32)
          lndt = A.alloc("lndt", [128, 5, 16], F32)
          acum, nb, Eexp, cd, dte, w2, dtp, lnv, pol, msk = [A.alloc(nm, [128, 16], F32) for nm in
                                                           ['acum', 'nb', 'Eexp', 'cd', 'dte', 'w2', 'dtp', 'lnv', 'pol', 'msk']]
          rhsb = A.alloc("rhsb", [128, 16, 128], F32)
          LTs = [A.alloc(f"LT{i}", [128, 8, 128], BF16) for i in range(2)]
          MTs = [A.alloc(f"MT{i}", [128, 8, 128], BF16) for i in range(2)]
          xd = A.alloc("xd", [128, 1024], BF16)
          xw = A.alloc("xw", [128, 512], BF16)
          tmpf = A.alloc("tmpf", [128, 512], F32)
          yy = A.alloc("yy", [128, 1024], F32)
          y2 = A.alloc("y2", [128, 1024], F32)
          mo = A.alloc("mo", [128, 1024], BF16)
          P.barrier()

          def conv_fm(wt, wn, cbl, cb, ntap, prebuf, prename, dgbuf, dgname, halbuf, halname, rw, evac, rhs_t=uT, rhs_n='uT', phase=None):
              hw = ntap - 1
              hn = (halname, cb)
              if phase in (None, 'A'):
                  P.op('dve', lambda e: e.tensor_copy(out=prebuf[:, 0:hw], in_=halbuf[:, cb, 0:hw]), reads=[hn], writes=[prename])
                  for k in range(ntap):
                      P.op('dve', lambda e, k=k: e.tensor_scalar(out=dgbuf[:, k, :], in0=ident[:, :], scalar1=pcol[:, rw(k):rw(k) + 1], scalar2=None,
                                                                 op0=ALU.mult), reads=['ident', 'pcol'], writes=[dgname])
                  mm_fm(wt, wn, cbl, rhs_t, [rhs_n], lambda ps, pn, soff, sn: P.op(
                      'act', lambda e: e.activation(out=prebuf[:, hw + soff:hw + soff + sn], in_=ps, func=AF.Copy), reads=pn, writes=[prename]))
                  P.op('dve', lambda e: e.tensor_copy(out=halbuf[:, cb, 0:hw], in_=prebuf[:, T:T + hw]), reads=[prename], writes=[hn])
              if phase in (None, 'B'):
                  for (soff, sn) in segs:
                      if sn <= 16:
                          cps, cpn = next_mini(alt=True)
                          cpn = [cpn]
                      else:
                          cps, cpn = next_cv()
                          cps = cps[:, 0:sn]
                      for k in range(ntap):
                          P.op('pe', lambda e, cps=cps, k=k, soff=soff, sn=sn: e.matmul(cps, lhsT=dgbuf[:, k, :], rhs=prebuf[:, soff + k:soff + k + sn],
                                                                                       start=(k == 0), stop=(k == ntap - 1)),
                               reads=[dgname, prename], writes=cpn)
                      evac(cps, cpn, soff, sn)

          def run_skewed(tasks, skew):
              nt = len(tasks)
              for i in range(nt + skew):
                  if i < nt:
                      tasks[i]('A')
                  if i - skew >= 0:
                      tasks[i - skew]('B')

          for j in range(3):
              wt, wn = use_piece(pc[('xbc', j)])
              tasks = []
              for cbl in range(4 if not (pref and j == 2) else 2):
                  cb = j * 4 + cbl
                  pi = cb % 4
                  if cb < 8:
                      dst, dname = xfs[j % 2][:, cbl, :], ('xf', j % 2)
                  elif cb < 10:
                      dst, dname = B_fm[:, cb - 8, :], 'B_fm'
                  else:
                      dst, dname = C_fm[:, cb - 10, :], 'C_fm'
                  tasks.append(lambda ph, wt=wt, wn=wn, cbl=cbl, cb=cb, pi=pi, dst=dst, dname=dname: conv_fm(
                      wt, wn, cbl, cb, 4, pre[pi], ('pre', pi), dg[pi], ('dg', pi), hal, 'hal', lambda k, cb=cb: R_MCW + k * 12 + cb,
                      lambda cps, cpn, soff, sn, dst=dst, dname=dname, cb=cb: P.op(
                          'act', lambda e: e.activation(out=dst[:, soff:soff + sn], in_=cps, func=AF.Silu, bias=pcol[:, R_MCB + cb:R_MCB + cb + 1]),
                          reads=cpn + ['pcol'], writes=[dname]), phase=ph))
              run_skewed(tasks, 1)
              if j < 2:
                  for ti, (off, n) in enumerate(tiles):
                      tb, tbn = pTm[ti % 2]
                      for cbl in range(4):
                          P.op('pe', lambda e, cbl=cbl, off=off, n=n, tb=tb, j=j: e.transpose(out=tb[0:n, cbl, :], in_=xfs[j % 2][:, cbl, off:off + n], identity=ident[:, :]),
                               reads=[('xf', j % 2), 'ident'], writes=[tbn])
                      P.op('dve', lambda e, ti=ti, n=n, j=j, tb=tb: e.tensor_copy(
                          out=x_tm[0:n, ti, j * 512:(j + 1) * 512].rearrange("p (c k) -> p c k", c=4), in_=tb[0:n, 0:4, :]),
                          reads=[tbn], writes=[('x_tm', ti)])
              else:
                  for ti, (off, n) in enumerate(tiles):
                      tb, tbn = pTm[ti % 2]
                      for g in range(2):
                          P.op('pe', lambda e, g=g, off=off, n=n, tb=tb: e.transpose(out=tb[0:n, g, :], in_=B_fm[:, g, off:off + n], identity=ident[:, :]),
                               reads=['B_fm', 'ident'], writes=[tbn])
                      P.op('dve', lambda e, ti=ti, n=n, tb=tb: e.tensor_copy(out=B_tm[0:n, ti, :].rearrange("p (c k) -> p c k", c=2), in_=tb[0:n, 0:2, :]),
                           reads=[tbn], writes=[('B_tm', ti)])
          for j in range(2 if not pref else 0):
              wt, wn = use_piece(pc[('z', j)])
              for ti, (off, n) in enumerate(tiles):
                  ps, pn = next_acc()
                  for k in range(8):
                      P.op('pe', lambda e, ps=ps, k=k, off=off, n=n, wt=wt: e.matmul(ps[0:n, :], lhsT=uT[:, k, off:off + n], rhs=wt[:, k, :],
                                                                                      start=(k == 0), stop=(k == 7)), reads=[wn, 'uT'], writes=[pn])
                  P.op('act', lambda e, ps=ps, ti=ti, n=n, j=j: e.activation(out=zs_tm[0:n, ti, j * 512:(j + 1) * 512], in_=ps[0:n, :], func=AF.Silu),
                       reads=[pn], writes=[('zs', ti)])
          wt, wn = use_piece(pc['dt'])
          for ti, (off, n) in enumerate(tiles):
              ps, pn = next_mini()
              for k in range(8):
                  P.op('pe', lambda e, ps=ps, k=k, off=off, n=n, wt=wt: e.matmul(ps[0:n, :], lhsT=uT[:, k, off:off + n], rhs=wt[:, k, 0:16],
                                                                                  start=(k == 0), stop=(k == 7)), reads=[wn, 'uT'], writes=[pn])
              P.op('dve', lambda e, ps=ps, n=n: e.tensor_tensor(out=dtp[0:n, :], in0=ps[0:n, :], in1=dtb_b[0:n, :], op=ALU.add),
                   reads=[pn, 'dtb_b'], writes=['dtp'])
              P.op('act', lambda e, n=n: e.activation(out=dtp[0:n, :], in_=dtp[0:n, :], func=AF.Exp), reads=['dtp'], writes=['dtp'])
              P.op('act', lambda e, n=n: e.activation(out=lnv[0:n, :], in_=dtp[0:n, :], func=AF.Ln, bias=onec[0:n, :]), reads=['dtp', 'onec'], writes=['lnv'])
              P.op('dve', lambda e, n=n: e.tensor_scalar(out=pol[0:n, :], in0=dtp[0:n, :], scalar1=1.0 / 7.0, scalar2=None, op0=ALU.mult), reads=['dtp'], writes=['pol'])
              for cc in (-1.0 / 6.0, 0.2, -0.25, 1.0 / 3.0, -0.5, 1.0):
                  P.op('dve', lambda e, n=n, cc=cc: e.scalar_tensor_tensor(out=pol[0:n, :], in0=pol[0:n, :], scalar=cc, in1=dtp[0:n, :], op0=ALU.add, op1=ALU.mult),
                       reads=['pol', 'dtp'], writes=['pol'])
              P.op('dve', lambda e, n=n: e.tensor_single_scalar(out=msk[0:n, :], in_=dtp[0:n, :], scalar=0.125, op=ALU.is_lt), reads=['dtp'], writes=['msk'])
              P.op('dve', lambda e, n=n: e.tensor_tensor(out=pol[0:n, :], in0=pol[0:n, :], in1=lnv[0:n, :], op=ALU.subtract), reads=['pol', 'lnv'], writes=['pol'])
              P.op('dve', lambda e, n=n: e.tensor_tensor(out=pol[0:n, :], in0=pol[0:n, :], in1=msk[0:n, :], op=ALU.mult), reads=['pol', 'msk'], writes=['pol'])
              P.op('dve', lambda e, n=n, ti=ti: e.tensor_tensor(out=dt_tm[0:n, ti, :], in0=pol[0:n, :], in1=lnv[0:n, :], op=ALU.add),
                   reads=['pol', 'lnv'], writes=[('dt', ti)])
              P.op('dve', lambda e, n=n, ti=ti: e.tensor_tensor(out=da_tm[0:n, ti, :], in0=dt_tm[0:n, ti, :], in1=aneg_b[0:n, :], op=ALU.mult),
                   reads=[('dt', ti), 'aneg_b'], writes=[('da', ti)])
              P.op('act', lambda e, n=n, ti=ti: e.activation(out=lndt[0:n, ti, :], in_=dt_tm[0:n, ti, :], func=AF.Ln), reads=[('dt', ti)], writes=[('lndt', ti)])
          stop_pt('m_proj')

          h8 = lambda ap: ap.rearrange("p (h q) -> p h q", q=64)
          for ti, (off, n) in enumerate(tiles):
              if not pref:
                P.op('dve', lambda e, ti=ti, n=n: e.tensor_tensor(out=h8(xd[0:n, :]), in0=h8(x_tm[0:n, ti, :]),
                                                                  in1=dsk_b[0:n, :].unsqueeze(2).broadcast_to([n, 16, 64]), op=ALU.mult),
                     reads=[('x_tm', ti), 'dsk_b'], writes=['xd'])
              psA, pnA = next_mini()
              P.op('pe', lambda e, psA=psA, ti=ti, n=n: e.matmul(psA[0:n, :], lhsT=triU[0:n, 0:n], rhs=da_tm[0:n, ti, :], start=True, stop=True),
                   reads=['triU', ('da', ti)], writes=[pnA])
              psL, pnL = next_mini()
              P.op('pe', lambda e, psL=psL, ti=ti, n=n: e.matmul(psL[:, :], lhsT=onesf[0:n, :], rhs=da_tm[0:n, ti, :], start=True, stop=True),
                   reads=['onesf', ('da', ti)], writes=[pnL])
              P.op('dve', lambda e, psA=psA, n=n: e.tensor_copy(out=acum[0:n, :], in_=psA[0:n, :]), reads=[pnA], writes=['acum'])
              if not pref:
                P.op('dve', lambda e, psA=psA, n=n, ti=ti: e.tensor_tensor(out=nb[0:n, :], in0=lndt[0:n, ti, :], in1=psA[0:n, :], op=ALU.subtract),
                   reads=[pnA, ('lndt', ti)], writes=['nb'])
                P.op('act', lambda e, psA=psA, n=n: e.activation(out=Eexp[0:n, :], in_=psA[0:n, :], func=AF.Exp), reads=[pnA], writes=['Eexp'])
              P.op('act', lambda e, psL=psL: e.activation(out=cd[:, :], in_=psL[:, :], func=AF.Exp), reads=[pnL], writes=['cd'])
              P.op('dve', lambda e, psL=psL, n=n: e.tensor_tensor(out=dte[0:n, :], in0=psL[0:n, :], in1=acum[0:n, :], op=ALU.subtract),
                   reads=[pnL, 'acum'], writes=['dte'])
              P.op('act', lambda e, n=n: e.activation(out=dte[0:n, :], in_=dte[0:n, :], func=AF.Exp), reads=['dte'], writes=['dte'])
              P.op('dve', lambda e, n=n, ti=ti: e.tensor_tensor(out=w2[0:n, :], in0=dte[0:n, :], in1=dt_tm[0:n, ti, :], op=ALU.mult),
                   reads=['dte', ('dt', ti)], writes=['w2'])
              if not pref:
               P.op('dve', lambda e, n=n, ti=ti: e.tensor_tensor(out=rhsb[0:n, :, 0:n], in0=triU[0:n, 0:n].unsqueeze(1).broadcast_to([n, 16, n]),
                                                                in1=da_tm[0:n, ti, :].unsqueeze(2).broadcast_to([n, 16, n]), op=ALU.mult),
                   reads=['triU', ('da', ti)], writes=['rhsb'])
              def gp1(g):
                  pcb = pb2[0:n, 256 + g * 128:256 + g * 128 + n]
                  if not pref:
                      gbk = [((pb45[:, 0:512], 'pb4'), (pb45[:, 512:1024], 'pb5')), ((acc[0][:, :], 'acc0'), (acc[1][:, :], 'acc1'))][g]
                      if n == 128:
                          for hh in range(2):
                              outp, bn = gbk[hh]
                              h0 = g * 8 + hh * 4
                              P.op('pe', lambda e, outp=outp, h0=h0: e.matmul(outp, lhsT=onesf[:, :], rhs=rhsb[:, h0:h0 + 4, :].rearrange("p h i -> p (h i)"),
                                                                              start=True, stop=False), reads=['onesf', 'rhsb'], writes=[bn])
                              P.op('pe', lambda e, outp=outp: e.matmul(outp, lhsT=ident[:, :], rhs=mneg[:, :, :].rearrange("p h i -> p (h i)"), start=False, stop=True),
                                   reads=['ident', 'mneg'], writes=[bn])
                      else:
                          for hl in range(8):
                              outp = gbk[hl // 4][0][0:n, (hl % 4) * 128:(hl % 4) * 128 + n]
                              bn = gbk[hl // 4][1]
                              P.op('pe', lambda e, outp=outp, n=n, hl=hl, g=g: e.matmul(outp, lhsT=onesf[0:n, 0:n], rhs=rhsb[0:n, g * 8 + hl, 0:n], start=True, stop=False),
                                   reads=['onesf', 'rhsb'], writes=[bn])
                              P.op('pe', lambda e, outp=outp, n=n: e.matmul(outp, lhsT=ident[0:n, 0:n], rhs=mneg[0:n, 0, 0:n], start=False, stop=True),
                                   reads=['ident', 'mneg'], writes=[bn])
                      P.op('pe', lambda e, pcb=pcb, g=g, off=off, n=n: e.matmul(pcb, lhsT=B_fm[:, g, off:off + n], rhs=C_fm[:, g, off:off + n], start=True, stop=True),
                           reads=['B_fm', 'C_fm'], writes=['pb2'])
                      P.op('pe', lambda e, g=g, off=off, n=n: e.matmul(yob[g][0][0:n, :], lhsT=C_fm[:, g, off:off + n], rhs=ST_b[:, g, :], start=True, stop=True),
                           reads=['C_fm', ('ST_b', g)], writes=[yob[g][1]])

              def gp2(g):
                  pcb = pb2[0:n, 256 + g * 128:256 + g * 128 + n]
                  P.op('dve', lambda e, g=g, n=n, ti=ti: e.tensor_tensor(out=h8(xw[0:n, :]), in0=h8(x_tm[0:n, ti, g * 512:(g + 1) * 512]),
                                                                         in1=w2[0:n, g * 8:(g + 1) * 8].unsqueeze(2).broadcast_to([n, 8, 64]), op=ALU.mult),
                       reads=[('x_tm', ti), 'w2'], writes=['xw'])
                  if not pref:
                      P.op('dve', lambda e, g=g, n=n: e.tensor_tensor(out=h8(tmpf[0:n, :]), in0=h8(yob[g][0][0:n, :]),
                                                                      in1=Eexp[0:n, g * 8:(g + 1) * 8].unsqueeze(2).broadcast_to([n, 8, 64]), op=ALU.mult),
                           reads=[yob[g][1], 'Eexp'], writes=['tmpf'])
                  psU, pnU = yob[g]
                  P.op('pe', lambda e, psU=psU, g=g, n=n, ti=ti: e.matmul(psU[:, :], lhsT=B_tm[0:n, ti, g * 128:(g + 1) * 128], rhs=xw[0:n, :], start=True, stop=True),
                       reads=[('B_tm', ti), 'xw'], writes=[pnU])
                  P.op('dve', lambda e, g=g: e.tensor_tensor(out=h8(ST_f[:, g, :]), in0=h8(ST_f[:, g, :]),
                                                             in1=cd[:, g * 8:(g + 1) * 8].unsqueeze(2).broadcast_to([128, 8, 64]), op=ALU.mult),
                       reads=[('ST_f', g), 'cd'], writes=[('ST_f', g)])
                  P.op('dve', lambda e, g=g, psU=psU: e.tensor_tensor(out=ST_f[:, g, :], in0=ST_f[:, g, :], in1=psU[:, :], op=ALU.add),
                       reads=[('ST_f', g), pnU], writes=[('ST_f', g)])
                  if not pref:
                      gbk = [((pb45[:, 0:512], 'pb4'), (pb45[:, 512:1024], 'pb5')), ((acc[0][:, :], 'acc0'), (acc[1][:, :], 'acc1'))][g]
                      for hl in range(8):
                          hh = g * 8 + hl
                          bk, bn = gbk[hl // 4]
                          P.op('act', lambda e, hl=hl, hh=hh, n=n, bk=bk: e.activation(out=LTs[g][0:n, hl, 0:n], in_=bk[0:n, (hl % 4) * 128:(hl % 4) * 128 + n], func=AF.Exp,
                                                                                      bias=nb[0:n, hh:hh + 1]), reads=[bn, 'nb'], writes=[('LT', g)])
                  P.op('act', lambda e, g=g: e.activation(out=ST_b[:, g, :], in_=ST_f[:, g, :], func=AF.Copy), reads=[('ST_f', g)], writes=[('ST_b', g)])

              def gp3(g):
                  pcb = pb2[0:n, 256 + g * 128:256 + g * 128 + n]
                  if not pref:
                      P.op('dve', lambda e, pcb=pcb, n=n: e.tensor_tensor(out=MTs[g][0:n, :, 0:n], in0=LTs[g][0:n, :, 0:n], in1=pcb.unsqueeze(1).broadcast_to([n, 8, n]),
                                                                          op=ALU.mult), reads=[('LT', g), 'pb2'], writes=[('MT', g)])
                      P.op('pe', lambda e, g=g, n=n: e.matmul(pb6[0:n, :], lhsT=ident[0:n, 0:n], rhs=xd[0:n, g * 512:(g + 1) * 512], start=True, stop=False),
                           reads=['ident', 'xd'], writes=['pb6'])
                      for hl in range(8):
                          hh = g * 8 + hl
                          P.op('pe', lambda e, hl=hl, hh=hh, n=n, ti=ti: e.matmul(pb6[0:n, hl * 64:(hl + 1) * 64], lhsT=MTs[g][0:n, hl, 0:n],
                                                                                  rhs=x_tm[0:n, ti, hh * 64:(hh + 1) * 64], start=False, stop=(hl == 7)),
                               reads=[('MT', g), ('x_tm', ti)], writes=['pb6'])
                      P.op('dve', lambda e, g=g, n=n: e.tensor_tensor(out=yy[0:n, g * 512:(g + 1) * 512], in0=pb6[0:n, :], in1=tmpf[0:n, :], op=ALU.add),
                           reads=['pb6', 'tmpf'], writes=['yy'])

              if pref:
                  gp2(0)
                  gp2(1)
              else:
                  gp1(0)
                  gp2(0)
                  gp1(1)
                  gp3(0)
                  gp2(1)
                  gp3(1)
              if pref:
                  continue
              if dbg and ti == 1:
                  tap('y_raw1', yy[:, :], [128, 1024], ['yy'])
              P.op('dve', lambda e, n=n, ti=ti: e.tensor_tensor(out=y2[0:n, :], in0=yy[0:n, :], in1=zs_tm[0:n, ti, :], op=ALU.mult),
                   reads=['yy', ('zs', ti)], writes=['y2'])
              for g in range(2):
                  P.op('act', lambda e, g=g, n=n: e.activation(out=junk[0:n, 0:512], in_=y2[0:n, g * 512:(g + 1) * 512], func=AF.Square,
                                                               accum_out=ss[0:n, g:g + 1]), reads=['y2'], writes=['junk', 'ss'])
              P.op('dve', lambda e, n=n: e.tensor_scalar(out=rs[0:n, 0:2], in0=ss[0:n, 0:2], scalar1=1.0 / 512, scalar2=EPS, op0=ALU.mult, op1=ALU.add),
                   reads=['ss'], writes=['rs'])
              P.op('pool', lambda e, n=n: e.tensor_tensor(out=rs[0:n, 0:2], in0=rs[0:n, 0:2], in1=nhalf[0:n, 0:2], op=ALU.pow), reads=['rs', 'nhalf'], writes=['rs'])
              for g in range(2):
                  P.op('dve', lambda e, g=g, n=n: e.scalar_tensor_tensor(out=mo[0:n, g * 512:(g + 1) * 512], in0=y2[0:n, g * 512:(g + 1) * 512],
                                                                         scalar=rs[0:n, g:g + 1], in1=m2nw_b[0:n, g * 512:(g + 1) * 512],
                                                                         op0=ALU.mult, op1=ALU.mult), reads=['y2', 'rs', 'm2nw_b'], writes=['mo'])
              for j in range(8):
                  P.op('pe', lambda e, j=j, n=n: e.transpose(out=pT[:, j, 0:n], in_=mo[0:n, j * 128:(j + 1) * 128], identity=ident[0:n, 0:n]),
                       reads=['mo', 'ident'], writes=['pT'])
              P.op('act', lambda e, off=off, n=n: e.activation(out=mixT[:, 8:16, off:off + n], in_=pT[:, :, 0:n], func=AF.Copy),
                   reads=['pT'], writes=[('mixT', 8 + j) for j in range(8)])
          if dbg:
              tap('mixB', mixT[:, 8:16, 0:T], [128, 8, T], [('mixT', 8 + j) for j in range(8)])
          stop_pt('mamba')
          if pref:
              return T

          A.reset(PH)
          for c2 in range(2):
              w0, n0 = use_piece(pc[('wo', c2, 0)])
              w1, n1 = use_piece(pc[('wo', c2, 1)], hold=True)
              for ti, (off, n) in enumerate(tiles):
                  ps, pn = next_acc()
                  for kk in range(16):
                      wt, wn = (w0, n0) if kk < 8 else (w1, n1)
                      P.op('pe', lambda e, ps=ps, kk=kk, off=off, n=n, wt=wt: e.matmul(ps[0:n, :], lhsT=mixT[:, kk, off:off + n], rhs=wt[:, kk % 8, :],
                                                                                        start=(kk == 0), stop=(kk == 15)),
                           reads=[('mixT', kk), wn], writes=[pn])
                  P.op('dve', lambda e, ps=ps, ti=ti, n=n, c2=c2: e.tensor_tensor(out=h[0:n, ti, c2 * 512:(c2 + 1) * 512], in0=h[0:n, ti, c2 * 512:(c2 + 1) * 512],
                                                                                  in1=ps[0:n, :], op=ALU.add), reads=[('h', ti), pn], writes=[('h', ti)])
          if dbg:
              tap('hmid1', h[:, 1, :], [128, 1024], [('h', 1)])
          norm_to_fm(R_N2, False)
          stop_pt('oproj')

          pre2 = [A.alloc(f"pre2{i}", [128, 2 + TM], BF16) for i in range(4)]
          dg2 = [A.alloc(f"dg2{i}", [128, 3, 128], BF16) for i in range(4)]
          gs = [A.alloc(f"gs{i}", [128, TM], BF16) for i in range(2)]
          aT = A.alloc("aT", [128, 22, TM], BF16)
          ob = [A.alloc(f"ob{i}", [128, D], F32) for i in range(2)]
          P.barrier()
          tasks = []
          for j in range(6):
              ncb = 4 if j < 5 else 2
              for cbl in range(ncb):
                  cbg = j * 4 + cbl
                  cbv = 22 + cbg
                  gi = cbg % 2
                  pg, pv = (2 * cbg) % 4, (2 * cbg + 1) % 4
                  tasks.append(lambda ph, j=j, cbl=cbl, cbg=cbg, gi=gi, pg=pg: conv_fm(
                      wsl[('ug', j)][0], wsl[('ug', j)][1], cbl, cbg, 3, pre2[pg], ('pre2', pg), dg2[pg], ('dg2', pg), hal2, 'hal2',
                      lambda k, cbg=cbg: R_FCW + k * 44 + cbg,
                      lambda cps, cpn, soff, sn, gi=gi, cbg=cbg: P.op(
                          'act', lambda e: e.activation(out=gs[gi][:, soff:soff + sn], in_=cps, func=AF.Silu, bias=pcol[:, R_FCB + cbg:R_FCB + cbg + 1]),
                          reads=cpn + ['pcol'], writes=[('gs', gi)]), phase=ph))
                  tasks.append(lambda ph, j=j, cbl=cbl, cbg=cbg, cbv=cbv, gi=gi, pv=pv: conv_fm(
                      wsl[('uv', j)][0], wsl[('uv', j)][1], cbl, cbv, 3, pre2[pv], ('pre2', pv), dg2[pv], ('dg2', pv), hal2, 'hal2',
                      lambda k, cbv=cbv: R_FCW + k * 44 + cbv,
                      lambda cps, cpn, soff, sn, gi=gi, cbg=cbg, cbv=cbv: P.op(
                          'dve', lambda e: e.scalar_tensor_tensor(out=aT[:, cbg, soff:soff + sn], in0=cps, scalar=pcol[:, R_FCB + cbv:R_FCB + cbv + 1],
                                                                  in1=gs[gi][:, soff:soff + sn], op0=ALU.add, op1=ALU.mult),
                          reads=cpn + ['pcol', ('gs', gi)], writes=[('aT', cbg)]), phase=ph))
          wsl = {}
          nt = len(tasks)
          for i in range(nt + 2):
              if i < nt and i % 8 == 0:
                  j = i // 8
                  wsl[('ug', j)] = use_piece(pc[('ug', j)])
                  wsl[('uv', j)] = use_piece(pc[('uv', j)], hold=True)
              if i < nt:
                  tasks[i]('A')
              if i - 2 >= 0:
                  tasks[i - 2]('B')
          stop_pt('ffn_up')

          dbanks = [(acc[0][:, :], 'acc0'), (acc[1][:, :], 'acc1'), (pb45[:, 0:512], 'pb4'), (pb45[:, 512:1024], 'pb5'), (pb6[:, :], 'pb6')]
          for c2 in range(2):
              for kp in range(3):
                  wt, wn = use_piece(pc[('wd', c2, kp)])
                  nk = 8 if kp < 2 else 6
                  for ti, (off, n) in enumerate(tiles):
                      ps, pn = dbanks[ti]
                      for k in range(nk):
                          kk = kp * 8 + k
                          P.op('pe', lambda e, ps=ps, kk=kk, k=k, off=off, n=n, wt=wt: e.matmul(ps[0:n, :], lhsT=aT[:, kk, off:off + n], rhs=wt[:, k, :],
                                                                                                 start=(kk == 0), stop=(kk == 21)),
                               reads=[('aT', kk), wn], writes=[pn])
              for ti, (off, n) in enumerate(tiles):
                  ps, pn = dbanks[ti]
                  P.op('dve', lambda e, ps=ps, ti=ti, n=n, c2=c2: e.tensor_tensor(out=h[0:n, ti, c2 * 512:(c2 + 1) * 512], in0=h[0:n, ti, c2 * 512:(c2 + 1) * 512],
                                                                                  in1=ps[0:n, :], op=ALU.add), reads=[('h', ti), pn], writes=[('h', ti)])
          for ti, (off, n) in enumerate(tiles):
              oi = ti % 2
              P.op('act', lambda e, ti=ti, n=n: e.activation(out=junk[0:n, :], in_=h[0:n, ti, :], func=AF.Square, accum_out=ssn[0:n, ti:ti + 1]),
                   reads=[('h', ti)], writes=['junk', ('ssn', ti)])
              P.op('act', lambda e, ti=ti, n=n: e.activation(out=rsn[0:n, ti:ti + 1], in_=ssn[0:n, ti:ti + 1], func=AF.Sqrt, scale=1.0 / D, bias=epsc[0:n, :]),
                   reads=[('ssn', ti), 'epsc'], writes=[('rsn', ti)])
              P.op('dve', lambda e, ti=ti, n=n: e.reciprocal(out=rsn[0:n, ti:ti + 1], in_=rsn[0:n, ti:ti + 1]), reads=[('rsn', ti)], writes=[('rsn', ti)])
              P.op('dve', lambda e, ti=ti, n=n, oi=oi: e.scalar_tensor_tensor(out=ob[oi][0:n, :], in0=h[0:n, ti, :], scalar=rsn[0:n, ti:ti + 1], in1=fnw_b[0:n, :],
                                                                              op0=ALU.mult, op1=ALU.mult), reads=[('h', ti), ('rsn', ti), 'fnw_b'], writes=[('ob', oi)])
              P.dma('sp', lambda e, oi=oi, off=off, n=n, out0=out0: e.dma_start(out=out[out0 + off:out0 + off + n, :], in_=ob[oi][0:n, :]),
                    reads=[('ob', oi)], chan=('o', oi))

          return T

      out0 = 0
      for sb_i in range((n_pre + n_sb) if stop_after != 'setup' else 0):
          T_ = do_sb(sb_i, tok0, out0)
          tok0 += T_
          if sb_modes[sb_i] == 'main':
              out0 += T_
          if sb_i == n_pre - 1:
              for tns, nm in [(S_f, 'S_f'), (S_b, 'S_b')]:
                  P.op('dve', lambda e, tns=tns: e.tensor_scalar(out=tns[:], in0=tns[:], scalar1=flagc[:, 0:1], scalar2=None, op0=ALU.mult),
                       reads=[(nm, i) for i in range(8)] + ['flagc'], writes=[(nm, i) for i in range(8)])
              for tns, nm in [(ST_f, 'ST_f'), (ST_b, 'ST_b')]:
                  P.op('dve', lambda e, tns=tns: e.tensor_scalar(out=tns[:], in0=tns[:], scalar1=flagc[:, 0:1], scalar2=None, op0=ALU.mult),
                       reads=[(nm, i) for i in range(2)] + ['flagc'], writes=[(nm, i) for i in range(2)])
              P.op('dve', lambda e: e.tensor_scalar(out=hal[:], in0=hal[:], scalar1=flagc[:, 0:1], scalar2=None, op0=ALU.mult),
                   reads=[('hal', i) for i in range(12)] + ['flagc'], writes=[('hal', i) for i in range(12)])
    except StopBuild:
        pass

    tap('pcol', pcol[:, :], [128, PROWS], ['pcol'])
    tap('oml', oml[:, :], [128, 8], ['oml'])
    P.wait_all('sp', [k for k in P.count if isinstance(k, tuple) and k[0] == 'd'])
    P.emit()
    return nc, tap_out


def pack_shared(inp):
    f = lambda a: np.ascontiguousarray(np.asarray(a, dtype=np.float32))
    rows = [f(inp['norm1_w'][0]).reshape(8, 128), f(inp['hg_lb_logits'][0]).reshape(8, 128), f(inp['hg_lb_logits'][1]).reshape(8, 128),
            f(inp['hg_norm_w'][0]).reshape(8, 128), f(inp['norm2_w'][0]).reshape(8, 128), f(inp['m2_conv_w'][0]).reshape(48, 128),
            f(inp['m2_conv_b'][0]).reshape(12, 128), f(inp['ffn_conv_w'][0]).reshape(132, 128), f(inp['ffn_conv_b'][0]).reshape(44, 128)]
    pvec = np.ascontiguousarray(np.concatenate(rows, 0))
    assert pvec.shape == (PROWS, 128)
    return {
        'w_in': f(inp['w_in'][0]), 'w_out': f(inp['w_out'][0]), 'w_up': f(inp['ffn_w_up'][0]), 'w_down': f(inp['ffn_w_down'][0]),
        'pvec': pvec, 'm2nw': f(inp['m2_norm_w'][0]), 'fnw': f(inp['final_norm_w']), 'dtb': f(inp['m2_dt_bias'][0]),
        'alog': f(inp['m2_a_log'][0]), 'dsk': f(inp['m2_d'][0]),
    }


def core_tokens(inp, b, TOK):
    seq = np.concatenate([np.asarray(inp['meta_tokens'], np.float32), np.asarray(inp['x'][b], np.float32)], 0)
    return np.ascontiguousarray(seq[:TOK])


_CACHE = {}
N_PRE, N_SB = 4, 4


def kernel(**inputs):
    TOKP, TOKM = 512 * N_PRE, 16 + 512 * N_SB
    if 'nc' not in _CACHE:
        _CACHE['nc'] = build_program(N_SB, n_pre=N_PRE)[0]
    nc = _CACHE['nc']
    shared = pack_shared(inputs)
    meta = np.asarray(inputs['meta_tokens'], np.float32)
    in_maps = []
    for c in range(8):
        b, half = c // 2, c % 2
        seq = np.concatenate([meta, np.asarray(inputs['x'][b], np.float32)], 0)
        m = dict(shared)
        if half == 0:
            m['xin'] = np.ascontiguousarray(np.concatenate([np.zeros((TOKP, D), np.float32), seq[0:TOKM]], 0))
            m['flag'] = np.zeros((128, 1), np.float32)
        else:
            m['xin'] = np.ascontiguousarray(seq[0:TOKP + TOKM])
            m['flag'] = np.ones((128, 1), np.float32)
        in_maps.append(m)
    res = run_bass_kernel_spmd(nc, in_maps, core_ids=list(range(8)))
    outs = []
    for b in range(4):
        outs.append(np.concatenate([res.results[2 * b]['out'][NMETA:], res.results[2 * b + 1]['out'][NMETA:]], 0))
    return np.stack(outs, 0).astype(np.float32)
```
